# Optimizing a Trainium2 kernel written in Bass

```python
import math
import jax
import jax.numpy as jnp
from jax import lax
import numpy as np


D_MODEL = 1024
BATCH = 4
SEQ = 8192
DEPTH = 2

MEM_LEN = 256
N_MIXERS = 4
GROUP_WIDTH = D_MODEL // N_MIXERS
DIL_HEADS = 4
DIL_HEAD_DIM = GROUP_WIDTH // DIL_HEADS
DIL_BRANCHES = ((128, 1), (512, 4), (2048, 16))
ATTN_BLOCK = 128
RET_HEADS = 4
RET_DK = GROUP_WIDTH // RET_HEADS
RET_DV = GROUP_WIDTH // RET_HEADS
RET_CHUNK = 128
GLA_HEADS = 4
GLA_DV = GROUP_WIDTH // GLA_HEADS
GLA_DK = GLA_DV // 2
GLA_GATE_RANK = 16
GLA_TAU = 16.0
GLA_CHUNK = 64
S5_CH = 16
S5_GROUPS = GROUP_WIDTH // S5_CH
S5_STATE = 64
S5_DT_MIN = 1e-3
S5_DT_MAX = 1e-1
MEM_HEADS = 4
MEM_HEAD_DIM = D_MODEL // MEM_HEADS
D_FF = (8 * D_MODEL + 767) // 768 * 256
DEEPNORM_ALPHA = (2 * DEPTH) ** 0.25
DEEPNORM_BETA = (8 * DEPTH) ** -0.25
LN_EPS = 1e-5
NEG_INF = -1e30

IN_SIZES = (
    GROUP_WIDTH, GROUP_WIDTH, GROUP_WIDTH,
    RET_HEADS * RET_DK, RET_HEADS * RET_DK, RET_HEADS * RET_DV,
    RET_HEADS * RET_DV,
    GLA_HEADS * GLA_DK, GLA_HEADS * GLA_DK, GLA_HEADS * GLA_DV,
    GLA_GATE_RANK, GLA_HEADS * GLA_DV,
    GROUP_WIDTH,
)
D_IN = sum(IN_SIZES)

kernel_name = 'hybrid_dilated_ret_gla_s5_block'


def layer_norm(x, g, b):
    xf = x.astype(jnp.float32)
    mu = xf.mean(-1, keepdims=True)
    var = jnp.square(xf - mu).mean(-1, keepdims=True)
    return ((xf - mu) * lax.rsqrt(var + LN_EPS) * g + b).astype(x.dtype)


def head_norm(o, g):
    mu = o.mean(-1, keepdims=True)
    var = jnp.square(o - mu).mean(-1, keepdims=True)
    return (o - mu) * lax.rsqrt(var + LN_EPS) * g.reshape(o.shape[-2], o.shape[-1])


def chunk_state_scan(chunk_decay, u):
    def step(r, inp):
        a, uc = inp
        return a[..., None] * r + uc, r
    r0 = jnp.zeros(u.shape[:1] + u.shape[2:], u.dtype)
    _, prev = lax.scan(step, r0, (jnp.moveaxis(chunk_decay, 1, 0), jnp.moveaxis(u, 1, 0)))
    return jnp.moveaxis(prev, 0, 1)


def dilated_branch(q, k, v, window, dilation, slopes):
    bsz, seq, heads, hd = q.shape
    band = window // dilation
    n_prev = -(-band // ATTN_BLOCK)
    span = dilation * ATTN_BLOCK
    seq_pad = -(-seq // span) * span
    nb = seq_pad // span
    kw_len = (n_prev + 1) * ATTN_BLOCK

    def to_sub(t):
        t = jnp.pad(t, ((0, 0), (0, seq_pad - seq), (0, 0), (0, 0)))
        return t.reshape(bsz, nb, ATTN_BLOCK, dilation, heads, hd)

    def key_window(t):
        t = jnp.pad(t, ((0, 0), (n_prev, 0), (0, 0), (0, 0), (0, 0), (0, 0)))
        return jnp.concatenate([t[:, i:i + nb] for i in range(n_prev + 1)], axis=2)

    qs = to_sub(q)
    ks = key_window(to_sub(k))
    vs = key_window(to_sub(v))
    s = jnp.einsum('bnqrhd,bnkrhd->bnrhqk', qs, ks).astype(jnp.float32) * (hd ** -0.5)
    qi = jnp.arange(ATTN_BLOCK)[:, None]
    ki = jnp.arange(kw_len)[None, :]
    dist = qi + n_prev * ATTN_BLOCK - ki
    key_pos = jnp.arange(nb)[:, None] * ATTN_BLOCK - n_prev * ATTN_BLOCK + jnp.arange(kw_len)[None, :]
    valid = ((dist >= 0) & (dist <= band))[None] & (key_pos >= 0)[:, None, :]
    alibi = -slopes[:, None, None] * (dist * dilation).astype(jnp.float32)[None]
    s = jnp.where(valid[None, :, None, None], s + alibi, NEG_INF)
    m = s.max(-1, keepdims=True)
    p = jnp.exp(s - m)
    l = p.sum(-1)
    o = jnp.einsum('bnrhqk,bnkrhd->bnqrhd', p.astype(v.dtype), vs)
    inv = jnp.transpose(1.0 / l, (0, 1, 4, 2, 3))
    o = o * inv[..., None].astype(o.dtype)
    lse = jnp.transpose(m[..., 0] + jnp.log(l), (0, 1, 4, 2, 3))
    o = o.reshape(bsz, seq_pad, heads, hd)[:, :seq]
    lse = lse.reshape(bsz, seq_pad, heads)[:, :seq]
    return o, lse


def dilated_attention(q, k, v):
    bsz, seq, heads, hd = q.shape
    slopes = 2.0 ** (-8.0 * jnp.arange(1, heads + 1, dtype=jnp.float32) / heads)
    results = [dilated_branch(q, k, v, w, r, slopes) for (w, r) in DIL_BRANCHES]
    outs = jnp.stack([o for o, _ in results]).astype(jnp.float32)
    lses = jnp.stack([z for _, z in results])
    wts = jax.nn.softmax(lses, axis=0)
    o = jnp.sum(wts[..., None] * outs, axis=0)
    return o.reshape(bsz, seq, heads * hd).astype(q.dtype)


def retention(q, k, v, gate, gn_g):
    bsz, seq, heads, dk = q.shape
    dv = v.shape[-1]
    c = RET_CHUNK
    nc = seq // c
    log_g = jnp.log(1.0 - 2.0 ** (-5.0 - jnp.arange(heads, dtype=jnp.float32)))
    qc = q.reshape(bsz, nc, c, heads, dk)
    kc = k.reshape(bsz, nc, c, heads, dk) * (dk ** -0.5)
    vc = v.reshape(bsz, nc, c, heads, dv)
    pos = jnp.arange(c, dtype=jnp.float32)
    rel = pos[:, None] - pos[None, :]
    decay = jnp.where(rel[None] >= 0, jnp.exp(jnp.maximum(rel, 0.0)[None] * log_g[:, None, None]), 0.0)
    scores = jnp.einsum('bcnhd,bcmhd->bchnm', qc, kc) * decay
    inner = jnp.einsum('bchnm,bcmhe->bcnhe', scores, vc)
    k_dec = kc * jnp.exp((c - 1 - pos)[:, None] * log_g[None, :])[..., None]
    u = jnp.einsum('bcmhd,bcmhe->bchde', k_dec, vc)
    chunk_decay = jnp.broadcast_to(jnp.exp(c * log_g)[:, None], (bsz, nc, heads, dk))
    r_prev = chunk_state_scan(chunk_decay, u)
    q_dec = qc * jnp.exp((pos + 1.0)[:, None] * log_g[None, :])[..., None]
    cross = jnp.einsum('bcnhd,bchde->bcnhe', q_dec, r_prev)
    o = (inner + cross).astype(jnp.float32).reshape(bsz, seq, heads, dv)
    o = head_norm(o, gn_g) * jax.nn.silu(gate.astype(jnp.float32)).reshape(bsz, seq, heads, dv)
    return o.reshape(bsz, seq, heads * dv).astype(q.dtype)


def gla(q, k, v, g_low, w_gate, b_gate, out_gate, gn_g):
    bsz, seq, heads, dk = q.shape
    dv = v.shape[-1]
    c = GLA_CHUNK
    nc = seq // c
    z = (g_low @ w_gate + b_gate).astype(jnp.float32)
    log_a = jax.nn.log_sigmoid(z).reshape(bsz, nc, c, heads, dk) / GLA_TAU
    b = jnp.cumsum(log_a, axis=2)
    qc = q.astype(jnp.float32).reshape(bsz, nc, c, heads, dk) * (dk ** -0.5)
    kc = k.astype(jnp.float32).reshape(bsz, nc, c, heads, dk)
    vc = v.astype(jnp.float32).reshape(bsz, nc, c, heads, dv)
    q_in = qc * jnp.exp(b)
    att = jnp.einsum('bcnhd,bcmhd->bchnm', q_in, kc * jnp.exp(-b))
    causal = jnp.tril(jnp.ones((c, c), dtype=bool))
    intra = jnp.einsum('bchnm,bcmhe->bcnhe', jnp.where(causal, att, 0.0), vc)
    b_last = b[:, :, -1:]
    u = jnp.einsum('bcmhd,bcmhe->bchde', kc * jnp.exp(b_last - b), vc)
    r_prev = chunk_state_scan(jnp.exp(b_last[:, :, 0]), u)
    cross = jnp.einsum('bcnhd,bchde->bcnhe', q_in, r_prev)
    o = (intra + cross).reshape(bsz, seq, heads, dv)
    o = head_norm(o, gn_g) * jax.nn.silu(out_gate.astype(jnp.float32)).reshape(bsz, seq, heads, dv)
    return o.reshape(bsz, seq, heads * dv).astype(q.dtype)


def s5_layer(u, a_re, a_im, log_dt, b_re, b_im, c_re, c_im, d, w_glu, b_glu):
    bsz, seq, _ = u.shape
    ug = u.astype(jnp.float32).reshape(bsz, seq, S5_GROUPS, S5_CH)
    dt = jnp.exp(log_dt.astype(jnp.float32))[:, None]
    ea = jnp.exp(a_re * dt)
    ab_re = ea * jnp.cos(a_im * dt)
    ab_im = ea * jnp.sin(a_im * dt)
    den = a_re * a_re + a_im * a_im
    nr = ab_re - 1.0
    f_re = (nr * a_re + ab_im * a_im) / den
    f_im = (ab_im * a_re - nr * a_im) / den
    bb_re = f_re[..., None] * b_re - f_im[..., None] * b_im
    bb_im = f_re[..., None] * b_im + f_im[..., None] * b_re
    bu_re = jnp.einsum('bsgi,gpi->bsgp', ug, bb_re)
    bu_im = jnp.einsum('bsgi,gpi->bsgp', ug, bb_im)
    at_re = jnp.broadcast_to(ab_re, bu_re.shape)
    at_im = jnp.broadcast_to(ab_im, bu_im.shape)

    def combine(e1, e2):
        a1r, a1i, b1r, b1i = e1
        a2r, a2i, b2r, b2i = e2
        return (a2r * a1r - a2i * a1i,
                a2r * a1i + a2i * a1r,
                a2r * b1r - a2i * b1i + b2r,
                a2r * b1i + a2i * b1r + b2i)

    _, _, x_re, x_im = lax.associative_scan(combine, (at_re, at_im, bu_re, bu_im), axis=1)
    y = (jnp.einsum('bsgp,gip->bsgi', x_re, c_re) - jnp.einsum('bsgp,gip->bsgi', x_im, c_im)
         + d * ug)
    y = jax.nn.gelu(y).reshape(bsz, seq, S5_GROUPS * S5_CH)
    out = y * jax.nn.sigmoid(y @ w_glu + b_glu)
    return out.astype(u.dtype)


def hybrid_mixer(x, w_in, ret_gn_g, gla_w_gate, gla_b_gate, gla_gn_g, s5_a_re, s5_a_im,
                 s5_log_dt, s5_b_re, s5_b_im, s5_c_re, s5_c_im, s5_d, s5_w_glu, s5_b_glu,
                 w_mix_out):
    bsz, seq, _ = x.shape
    proj = x @ w_in
    (dq, dk_, dv_, rq, rk, rv, rg, gq, gk, gv, glow, gog, su) = jnp.split(
        proj, np.cumsum(IN_SIZES)[:-1].tolist(), axis=-1)

    def heads(t, h):
        return t.reshape(bsz, seq, h, -1)

    y_a = dilated_attention(heads(dq, DIL_HEADS), heads(dk_, DIL_HEADS), heads(dv_, DIL_HEADS))
    y_b = retention(heads(rq, RET_HEADS), heads(rk, RET_HEADS), heads(rv, RET_HEADS), rg, ret_gn_g)
    y_c = gla(heads(gq, GLA_HEADS), heads(gk, GLA_HEADS), heads(gv, GLA_HEADS), glow,
              gla_w_gate, gla_b_gate, gog, gla_gn_g)
    y_d = s5_layer(su, s5_a_re, s5_a_im, s5_log_dt, s5_b_re, s5_b_im, s5_c_re, s5_c_im,
                   s5_d, s5_w_glu, s5_b_glu)
    y = jnp.concatenate([y_a, y_b, y_c, y_d], axis=-1)
    return y @ w_mix_out


def memory_cross_attention(x, mem, w_q, w_kv, w_o):
    bsz, seq, _ = x.shape
    q = (x @ w_q).reshape(bsz, seq, MEM_HEADS, MEM_HEAD_DIM)
    k, v = jnp.split(mem @ w_kv, 2, axis=-1)
    k = k.reshape(bsz, -1, MEM_HEADS, MEM_HEAD_DIM)
    v = v.reshape(bsz, -1, MEM_HEADS, MEM_HEAD_DIM)
    s = jnp.einsum('bshd,bmhd->bhsm', q, k).astype(jnp.float32) * (MEM_HEAD_DIM ** -0.5)
    p = jax.nn.softmax(s, axis=-1)
    o = jnp.einsum('bhsm,bmhd->bshd', p.astype(v.dtype), v).reshape(bsz, seq, D_MODEL)
    return o @ w_o


def swiglu_ffn(x, w_gate, w_up, w_down):
    return (jax.nn.silu(x @ w_gate) * (x @ w_up)) @ w_down


def setup_inputs(seed: int = 0) -> dict:
    key = jax.random.key(seed)
    ks = jax.random.split(key, 32)
    L, D, W = DEPTH, D_MODEL, GROUP_WIDTH
    G, P, CH = S5_GROUPS, S5_STATE, S5_CH
    f32 = jnp.float32

    def nrm(k, shape, scale):
        return scale * jax.random.normal(k, shape, f32)

    return {
        'x': nrm(ks[0], (BATCH, SEQ, D), 1.0),
        'mem': nrm(ks[1], (BATCH, MEM_LEN, D), 1.0),
        'w_in': nrm(ks[2], (L, D, D_IN), D ** -0.5),
        'ret_gn_g': 1.0 + nrm(ks[3], (L, RET_HEADS * RET_DV), 0.02),
        'gla_w_gate': nrm(ks[4], (L, GLA_GATE_RANK, GLA_HEADS * GLA_DK), GLA_GATE_RANK ** -0.5),
        'gla_b_gate': nrm(ks[5], (L, GLA_HEADS * GLA_DK), 0.1),
        'gla_gn_g': 1.0 + nrm(ks[6], (L, GLA_HEADS * GLA_DV), 0.02),
        's5_a_re': -0.5 + nrm(ks[7], (L, G, P), 0.01),
        's5_a_im': math.pi * jnp.arange(P, dtype=f32) + nrm(ks[8], (L, G, P), 0.01),
        's5_log_dt': jax.random.uniform(ks[9], (L, G), f32, math.log(S5_DT_MIN), math.log(S5_DT_MAX)),
        's5_b_re': nrm(ks[10], (L, G, P, CH), (2 * CH) ** -0.5),
        's5_b_im': nrm(ks[11], (L, G, P, CH), (2 * CH) ** -0.5),
        's5_c_re': nrm(ks[12], (L, G, CH, P), P ** -0.5),
        's5_c_im': nrm(ks[13], (L, G, CH, P), P ** -0.5),
        's5_d': nrm(ks[14], (L, G, CH), 1.0),
        's5_w_glu': nrm(ks[15], (L, W, W), W ** -0.5),
        's5_b_glu': nrm(ks[16], (L, W), 0.02),
        'w_mix_out': nrm(ks[17], (L, D, D), D ** -0.5 * DEEPNORM_BETA),
        'ln_mix_g': 1.0 + nrm(ks[18], (L, D), 0.02),
        'ln_mix_b': nrm(ks[19], (L, D), 0.02),
        'w_mem_q': nrm(ks[20], (L, D, D), D ** -0.5),
        'w_mem_kv': nrm(ks[21], (L, D, 2 * D), D ** -0.5),
        'w_mem_o': nrm(ks[22], (L, D, D), D ** -0.5 * DEEPNORM_BETA),
        'ln_mem_g': 1.0 + nrm(ks[23], (L, D), 0.02),
        'ln_mem_b': nrm(ks[24], (L, D), 0.02),
        'w_ff_gate': nrm(ks[25], (L, D, D_FF), D ** -0.5),
        'w_ff_up': nrm(ks[26], (L, D, D_FF), D ** -0.5),
        'w_ff_down': nrm(ks[27], (L, D_FF, D), D_FF ** -0.5 * DEEPNORM_BETA),
        'ln_ff_g': 1.0 + nrm(ks[28], (L, D), 0.02),
        'ln_ff_b': nrm(ks[29], (L, D), 0.02),
    }


def reference(x, mem, w_in, ret_gn_g, gla_w_gate, gla_b_gate, gla_gn_g, s5_a_re, s5_a_im,
              s5_log_dt, s5_b_re, s5_b_im, s5_c_re, s5_c_im, s5_d, s5_w_glu, s5_b_glu,
              w_mix_out, ln_mix_g, ln_mix_b, w_mem_q, w_mem_kv, w_mem_o, ln_mem_g, ln_mem_b,
              w_ff_gate, w_ff_up, w_ff_down, ln_ff_g, ln_ff_b):
    for l in range(DEPTH):
        h = hybrid_mixer(x, w_in[l], ret_gn_g[l], gla_w_gate[l], gla_b_gate[l], gla_gn_g[l],
                         s5_a_re[l], s5_a_im[l], s5_log_dt[l], s5_b_re[l], s5_b_im[l],
                         s5_c_re[l], s5_c_im[l], s5_d[l], s5_w_glu[l], s5_b_glu[l], w_mix_out[l])
        x = layer_norm(DEEPNORM_ALPHA * x + h, ln_mix_g[l], ln_mix_b[l])
        h = memory_cross_attention(x, mem, w_mem_q[l], w_mem_kv[l], w_mem_o[l])
        x = layer_norm(DEEPNORM_ALPHA * x + h, ln_mem_g[l], ln_mem_b[l])
        h = swiglu_ffn(x, w_ff_gate[l], w_ff_up[l], w_ff_down[l])
        x = layer_norm(DEEPNORM_ALPHA * x + h, ln_ff_g[l], ln_ff_b[l])
    return x
```

```python
import math
import numpy as np
import ml_dtypes
from contextlib import ExitStack
import concourse.bass as bass
import concourse.mybir as mybir
from concourse.bass_utils import run_bass_kernel_spmd

F32 = mybir.dt.float32
BF16 = mybir.dt.bfloat16
I32 = mybir.dt.int32
AF = mybir.ActivationFunctionType
ALU = mybir.AluOpType
AX = mybir.AxisListType

ENGS = ('pe', 'act', 'dve', 'pool', 'sp')
WIN = 2000
NP_DMA = 6
WIN_D = 120


def is_psum_key(b):
    n = b[0] if isinstance(b, tuple) else b
    return isinstance(n, str) and n.startswith('ps') or n == 'pacc'


class Prog:
    def __init__(self, nc, es):
        self.nc = nc
        self.es = es
        self.ops = []
        self.last_w = {}
        self.readers = {}
        self.enabled = True
        self.start = 0
        self.sems = {}
        self.ccount = {e: 0 for e in ENGS}
        self.dcount = {e: 0 for e in ENGS}
        self.comp = {}
        self.throttle = {}
        self.waited = {e: {} for e in ENGS}

    def op(self, eng, fn, reads=(), writes=(), dma=False, coll=False):
        if not self.enabled:
            return None
        i = len(self.ops)
        reads = list(reads)
        writes = list(writes)
        for b in reads:
            if is_psum_key(b) and b not in writes:
                writes.append(b)
        deps = set()
        for b in reads:
            w = self.last_w.get(b)
            if w is not None:
                deps.add(w)
        for b in writes:
            w = self.last_w.get(b)
            if w is not None:
                deps.add(w)
            for r in self.readers.get(b, ()):
                deps.add(r)
        for b in reads:
            self.readers.setdefault(b, []).append(i)
        for b in writes:
            self.last_w[b] = i
            self.readers[b] = []
        deps.discard(i)
        self.ops.append(dict(eng=eng, fn=fn, deps=deps, dma=dma, coll=coll))
        return i

    def dma(self, eng, out, in_, reads=(), writes=(), **kw):
        return self.op(eng, lambda e: e.dma_start(out=out, in_=in_, **kw), reads, writes, dma=True)

    def _sem(self, key):
        if key not in self.sems:
            self.sems[key] = self.es.enter_context(self.nc.semaphore("s_" + "_".join(str(x) for x in key)))
        return self.sems[key]

    def emit_block(self):
        nc = self.nc
        ops = self.ops
        lo = self.start
        asyncs = [i for i in range(lo, len(ops)) if ops[i]['dma'] or ops[i]['coll']]
        ops.append(dict(eng='sp', fn=None, deps=set(asyncs), dma=False, coll=False))
        hi = len(ops)

        def skip(od, o):
            return od['eng'] == 'pe' and o['eng'] == 'pe' and not od['dma'] and not o['dma']

        need = set()
        for i in range(lo, hi):
            for d in ops[i]['deps']:
                if not skip(ops[d], ops[i]):
                    need.add(d)
        for i in range(lo, hi):
            o = ops[i]
            e = o['eng']
            if o['dma']:
                j = self.dcount[e]
                self.dcount[e] += 1
                s, r = j % NP_DMA, j // NP_DMA
                key = ('d', e, s, r // WIN_D)
                self.comp[i] = (key, 16 * (r % WIN_D + 1))
                if r >= 1 and (r % WIN_D) != 0:
                    self.throttle[i] = (key, 16 * (r % WIN_D))
                elif r >= 1:
                    self.throttle[i] = (('d', e, s, (r - 1) // WIN_D), 16 * ((r - 1) % WIN_D + 1))
            elif i in need:
                k = self.ccount[e]
                self.ccount[e] += 1
                self.comp[i] = (('c', e, k // WIN), k % WIN + 1)
        comp, throttle = self.comp, self.throttle
        per = {e: [i for i in range(lo, hi) if ops[i]['eng'] == e] for e in ENGS}
        with nc.Block() as block:
            def run(e, eng):
                waited = self.waited[e]
                for i in per[e]:
                    o = ops[i]
                    ws = [comp[d] for d in sorted(o['deps']) if not skip(ops[d], o)]
                    if i in throttle:
                        ws.append(throttle[i])
                    for key, val in ws:
                        if waited.get(key, 0) >= val:
                            continue
                        waited[key] = val
                        eng.wait_ge(self._sem(key), val)
                    if o['fn'] is None:
                        continue
                    ins = o['fn'](eng)
                    if i in comp:
                        key, val = comp[i]
                        ins.then_inc(self._sem(key), 16 if o['dma'] else 1)

            @block.tensor
            def _(eng):
                run('pe', eng)

            @block.scalar
            def _(eng):
                run('act', eng)

            @block.vector
            def _(eng):
                run('dve', eng)

            @block.gpsimd
            def _(eng):
                run('pool', eng)

            @block.sync
            def _(eng):
                run('sp', eng)
        self.start = hi
        self.last_w.clear()
        self.readers.clear()


ALPHA = float((2 * 2) ** 0.25)
LN_EPS = 1e-5
NT = 512


class Rot:
    def __init__(self, slots):
        self.slots = slots
        self.i = 0

    def get(self):
        s = self.slots[self.i % len(self.slots)]
        self.i += 1
        return s


def emit_D(nc, P, es, l, dr, TOK, last):
    ntile = TOK // NT
    memT, vecs = dr['memT'], dr['vecs'][l]
    w_out, w_q, w_o, w_kv, w_glu, w_gu, w_d = (dr['wb_' + k][l] for k in ('w_out', 'w_q', 'w_o', 'w_kv', 'w_glu', 'w_gu', 'w_d'))
    xsrc = dr['xT_own'] if l == 0 else dr['xown']
    xTv = xsrc.rearrange("(k p) n -> p k n", p=128)
    xdst = dr['xo'] if last else dr['xown']
    xov = xdst.rearrange("(k p) n -> p k n", p=128)
    sfx = "_D%d" % l
    if True:
        sb = lambda name, shape, dt=BF16: es.enter_context(nc.sbuf_tensor(name + sfx, shape, dt))
        wo_s = sb("wo_s", [128, 8, 1024])
        wq_s = sb("wq_s", [128, 8, 1024])
        woo_s = sb("woo_s", [128, 8, 1024])
        wglu_s = sb("wglu_s", [128, 2, 256])
        kT_s = sb("kT_s", [128, 8, 256])
        V_s = sb("V_s", [128, 2, 1024])
        memT_s = sb("memT_s", [128, 8, 256])
        vec_s = sb("vec_s", [128, 50], F32)
        ones_m = sb("ones_m", [128, 128])
        ones_1 = sb("ones_1", [128, 128])
        xs = sb("xs", [128, 8, NT], F32)
        xb = sb("xb", [128, 8, NT])
        ys = sb("ys", [128, 8, NT])
        ysA = sb("ysA", [128, 8, NT])
        ysB = sb("ysB", [128, 8, NT])
        sel_s = sb("sel_s", [128, 2], F32)
        dmy = sb("dmy", [128, 2], F32)
        yd = sb("yd", [128, 2, NT])
        qs = sb("qs", [128, 8, NT])
        os_ = sb("os", [128, 8, NT])
        hid = sb("hid", [128, 22, NT])
        wst = sb("wst", [128, 3, 4096])
        wdst = sb("wdst", [128, 3, 1024])
        zb = sb("zb", [128, 8, NT])
        rstd = sb("rstd", [128, 2, NT], F32)
        pt = sb("pt", [128, 4, NT])
        rl = sb("rl", [128, 2, NT], F32)
        sg = sb("sg", [128, 2, NT], F32)
        psb = [es.enter_context(nc.psum_tensor("ps%d" % i + sfx, [128, 512], F32)) for i in range(8)]

        psA = Rot([(psb[i], ('ps', i)) for i in range(4)])
        psD = [(psb[4 + i], ('ps', 4 + i)) for i in range(4)]
        psAtt = Rot([(psb[i], ('ps', i)) for i in range(8)])
        evac_i = [0]

        P.op('dve', lambda e: e.memset(ones_m[:], 1.0 / 1024.0), writes=['ones_m'])
        P.op('dve', lambda e: e.memset(ones_1[:], 1.0), writes=['ones_1'])
        P.dma('sp', vec_s[:], vecs, writes=['vec'])
        P.dma('sp', sel_s[:], dr['sel'], writes=['sel'])
        P.dma('pool', memT_s[:], memT, writes=['memT'])
        P.dma('sp', wglu_s[:], w_glu, writes=['wglu'])
        for (dst, src, key) in ((wo_s, w_out, 'wo'), (wq_s, w_q, 'wq'), (woo_s, w_o, 'woo')):
            for h in range(2):
                P.dma('sp', dst[:, 4 * h:4 * h + 4, :], src[:, 4 * h:4 * h + 4, :], writes=[(key, h)])
        WO = [('wo', 0), ('wo', 1)]
        WQ = [('wq', 0), ('wq', 1)]
        WOO = [('woo', 0), ('woo', 1)]

        for pc in range(4):
            slot = pc % 2
            wv = wst[:, slot, :].rearrange("p (k n) -> p k n", k=8)
            P.dma('sp', wv, w_kv[pc], writes=[('wst', slot)])
            if pc < 2:
                for j in range(4):
                    ps, pk = psA.get()
                    for kc in range(8):
                        P.op('pe', lambda e, ps=ps, wv=wv, kc=kc, j=j: e.matmul(
                            ps[:, 0:256], wv[:, kc, j * 128:(j + 1) * 128], memT_s[:, kc, :],
                            start=(kc == 0), stop=(kc == 7)), reads=[('wst', slot), 'memT'], writes=[pk])
                    P.op('act', lambda e, ps=ps, mc=4 * pc + j: e.activation(
                        out=kT_s[:, mc, :], in_=ps[:, 0:256], func=AF.Copy), reads=[pk], writes=['kT'])
            else:
                for mm in range(2):
                    ps, pk = psA.get()
                    for kc in range(8):
                        P.op('pe', lambda e, ps=ps, wv=wv, kc=kc, mm=mm: e.matmul(
                            ps[:], memT_s[:, kc, mm * 128:(mm + 1) * 128], wv[:, kc, :],
                            start=(kc == 0), stop=(kc == 7)), reads=[('wst', slot), 'memT'], writes=[pk])
                    P.op('act', lambda e, ps=ps, mm=mm, c0=(pc - 2) * 512: e.activation(
                        out=V_s[:, mm, c0:c0 + 512], in_=ps[:], func=AF.Copy), reads=[pk], writes=['V'])

        def resid(ps, pk, mc):
            P.op('dve', lambda e: e.scalar_tensor_tensor(
                out=xs[:, mc, :], in0=xs[:, mc, :], scalar=ALPHA, in1=ps[:], op0=ALU.mult, op1=ALU.add),
                reads=[pk, ('xs', mc)], writes=[('xs', mc)])

        xalias = hid[:].rearrange("p a b -> p (a b)").bitcast(F32)[:, 0:8 * NT].rearrange("p (c n) -> p c n", n=NT)
        akeys = lambda c: [('hid', 2 * c), ('hid', 2 * c + 1)]

        def load_x(t):
            ts = slice(t * NT, (t + 1) * NT)
            for h in range(2):
                P.dma('sp', xs[:, 4 * h:4 * h + 4, :], xTv[:, 4 * h:4 * h + 4, ts],
                      writes=[('xs', c) for c in range(4 * h, 4 * h + 4)])

        def load_y(t):
            sA = (t * NT) // 2048
            sB = (TOK + t * NT) // 2048
            cs0 = (t * NT) % 2048
            for (dst, sidx, key) in ((ysA, sA, 'ysA'), (ysB, sB, 'ysB')):
                for r_ in range(2):
                    src = dr['yall'][l][sidx][r_ * 512:(r_ + 1) * 512, cs0:cs0 + NT].rearrange("(m p) n -> p m n", p=128)
                    dstv = dst[:].rearrange("p (m r) n -> p m r n", r=2)[:, :, r_, :]
                    P.dma('sp', dstv, src, reads=[('yall', sidx)], writes=[(key, r_)])
            for c in range(8):
                P.op('dve', lambda e, c=c: e.tensor_scalar(out=ys[:, c, :], in0=ysA[:, c, :], scalar1=sel_s[:, 0:1],
                                                          scalar2=None, op0=ALU.mult),
                     reads=[('ysA', c % 2), 'sel'], writes=[('ys', c)])
                P.op('dve', lambda e, c=c: e.scalar_tensor_tensor(
                    out=ys[:, c, :], in0=ysB[:, c, :], scalar=sel_s[:, 1:2], in1=ys[:, c, :], op0=ALU.mult, op1=ALU.add),
                    reads=[('ysB', c % 2), 'sel', ('ys', c)], writes=[('ys', c)])

        def layer_norm(gcol, bcol, final=False):
            for c in range(8):
                P.op('act', lambda e, c=c: e.activation(out=zb[:, c, :], in_=xs[:, c, :], func=AF.Copy),
                     reads=[('xs', c)], writes=[('zb', c)])
            P.op('act', lambda e: e.activation(out=dmy[:, 0:1], in_=sel_s[:, 0:1], func=AF.Sqrt), reads=['sel'], writes=['dmy'])
            pm, pmk = psA.get()
            for c in range(8):
                P.op('pe', lambda e, c=c: e.matmul(pm[:], ones_m[:], zb[:, c, :], start=(c == 0), stop=(c == 7)),
                     reads=['ones_m', ('zb', c)], writes=[pmk])
            for c in range(8):
                P.op('dve', lambda e, c=c: e.tensor_tensor(out=xs[:, c, :], in0=xs[:, c, :], in1=pm[:], op=ALU.subtract),
                     reads=[pmk, ('xs', c)], writes=[('xs', c)])
                P.op('act', lambda e, c=c: e.activation(out=zb[:, c, :], in_=xs[:, c, :], func=AF.Square),
                     reads=[('xs', c)], writes=[('zb', c)])
            pv, pvk = psA.get()
            for c in range(8):
                P.op('pe', lambda e, c=c: e.matmul(pv[:], ones_m[:], zb[:, c, :], start=(c == 0), stop=(c == 7)),
                     reads=['ones_m', ('zb', c)], writes=[pvk])
            P.op('act', lambda e: e.activation(out=rstd[:, 0, :], in_=pv[:], func=AF.Sqrt, bias=LN_EPS),
                 reads=[pvk], writes=[('rstd', 0)])
            P.op('dve', lambda e: e.reciprocal(out=rstd[:, 1, :], in_=rstd[:, 0, :]),
                 reads=[('rstd', 0)], writes=[('rstd', 1)])
            for c in range(8):
                P.op('dve', lambda e, c=c: e.tensor_tensor(out=xs[:, c, :], in0=xs[:, c, :], in1=rstd[:, 1, :], op=ALU.mult),
                     reads=[('rstd', 1), ('xs', c)], writes=[('xs', c)])
                dst = xalias if final else xs
                dk = akeys(c) if final else [('xs', c)]
                P.op('act', lambda e, c=c, dst=dst: e.activation(
                    out=dst[:, c, :], in_=xs[:, c, :], func=AF.Identity,
                    scale=vec_s[:, gcol + c:gcol + c + 1], bias=vec_s[:, bcol + c:bcol + c + 1]),
                    reads=['vec', ('xs', c)], writes=dk)
            for c in range(8):
                dst = xalias if final else xs
                dk = akeys(c) if final else [('xs', c)]
                P.op('dve', lambda e, c=c, dst=dst: e.tensor_copy(out=xb[:, c, :], in_=dst[:, c, :]),
                     reads=dk, writes=[('xb', c)])

        outs = []
        for t in range(ntile):
            ts = slice(t * NT, (t + 1) * NT)
            if t == 0:
                load_x(0)
                load_y(0)
            for mc in range(2):
                ps, pk = psA.get()
                for kc in range(2):
                    P.op('pe', lambda e, ps=ps, kc=kc, mc=mc: e.matmul(
                        ps[:], wglu_s[:, kc, mc * 128:(mc + 1) * 128], ys[:, 6 + kc, :],
                        start=(kc == 0), stop=(kc == 1)), reads=['wglu', ('ys', 6), ('ys', 7)], writes=[pk])
                P.op('act', lambda e, ps=ps, mc=mc: e.activation(
                    out=sg[:, mc, :], in_=ps[:], func=AF.Sigmoid, bias=vec_s[:, mc:mc + 1]),
                    reads=[pk, 'vec'], writes=[('sg', mc)])
                P.op('dve', lambda e, mc=mc: e.tensor_tensor(
                    out=yd[:, mc, :], in0=ys[:, 6 + mc, :], in1=sg[:, mc, :], op=ALU.mult),
                    reads=[('sg', mc), ('ys', 6 + mc)], writes=[('yd', mc)])
            for mc in range(8):
                ps, pk = psA.get()
                for kc in range(8):
                    rhs = ys[:, kc, :] if kc < 6 else yd[:, kc - 6, :]
                    rk = ('ys', kc) if kc < 6 else ('yd', kc - 6)
                    P.op('pe', lambda e, ps=ps, kc=kc, mc=mc, rhs=rhs: e.matmul(
                        ps[:], wo_s[:, kc, mc * 128:(mc + 1) * 128], rhs, start=(kc == 0), stop=(kc == 7)),
                        reads=WO + [rk], writes=[pk])
                resid(ps, pk, mc)
            if t + 1 < ntile:
                load_y(t + 1)
            layer_norm(2, 10)
            for mc in range(8):
                ps, pk = psA.get()
                for kc in range(8):
                    P.op('pe', lambda e, ps=ps, kc=kc, mc=mc: e.matmul(
                        ps[:], wq_s[:, kc, mc * 128:(mc + 1) * 128], xb[:, kc, :], start=(kc == 0), stop=(kc == 7)),
                        reads=WQ + [('xb', kc)], writes=[pk])
                P.op('act', lambda e, ps=ps, mc=mc: e.activation(
                    out=qs[:, mc, :], in_=ps[:], func=AF.Copy, scale=1.0 / 16.0), reads=[pk], writes=[('qs', mc)])
            def att_head(h):
                for mm in range(2):
                    ps, pk = psAtt.get()
                    for dc in range(2):
                        P.op('pe', lambda e, ps=ps, dc=dc, mm=mm: e.matmul(
                            ps[:], kT_s[:, 2 * h + dc, mm * 128:(mm + 1) * 128], qs[:, 2 * h + dc, :],
                            start=(dc == 0), stop=(dc == 1)), reads=['kT', ('qs', 2 * h + dc)], writes=[pk])
                    pslot = (h % 2) * 2 + mm
                    P.op('act', lambda e, ps=ps, pslot=pslot: e.activation(
                        out=pt[:, pslot, :], in_=ps[:], func=AF.Exp), reads=[pk], writes=[('pt', pslot)])
                yield
                pl, plk = psAtt.get()
                for mm in range(2):
                    P.op('pe', lambda e, mm=mm: e.matmul(
                        pl[:], ones_1[:], pt[:, (h % 2) * 2 + mm, :], start=(mm == 0), stop=(mm == 1)),
                        reads=['ones_1', ('pt', (h % 2) * 2 + mm)], writes=[plk])
                P.op('dve', lambda e: e.reciprocal(out=rl[:, h % 2, :], in_=pl[:]),
                     reads=[plk], writes=[('rl', h % 2)])
                for dc in range(2):
                    ps, pk = psAtt.get()
                    for mm in range(2):
                        P.op('pe', lambda e, ps=ps, dc=dc, mm=mm: e.matmul(
                            ps[:], V_s[:, mm, h * 256 + dc * 128:h * 256 + (dc + 1) * 128], pt[:, (h % 2) * 2 + mm, :],
                            start=(mm == 0), stop=(mm == 1)), reads=['V', ('pt', (h % 2) * 2 + mm)], writes=[pk])
                    P.op('dve', lambda e, ps=ps, dc=dc: e.tensor_tensor(
                        out=os_[:, 2 * h + dc, :], in0=ps[:], in1=rl[:, h % 2, :], op=ALU.mult),
                        reads=[pk, ('rl', h % 2)], writes=[('os', 2 * h + dc)])
                yield

            prev = None
            for h in range(4):
                g = att_head(h)
                next(g)
                if prev is not None:
                    next(prev, None)
                prev = g
            next(prev, None)
            for mc in range(8):
                ps, pk = psA.get()
                for kc in range(8):
                    P.op('pe', lambda e, ps=ps, kc=kc, mc=mc: e.matmul(
                        ps[:], woo_s[:, kc, mc * 128:(mc + 1) * 128], os_[:, kc, :], start=(kc == 0), stop=(kc == 7)),
                        reads=WOO + [('os', kc)], writes=[pk])
                resid(ps, pk, mc)
            layer_norm(18, 26)
            for jp in range(11):
                slot = jp % 3
                wv = wst[:, slot, :].rearrange("p (g k n) -> p g k n", g=2, k=8)
                P.dma('pool', wv, w_gu[jp], writes=[('wst', slot)])
                for jj in range(2):
                    j = 2 * jp + jj
                    pg, pgk = psA.get()
                    pu, puk = psA.get()
                    for (pp, ppk, g) in ((pg, pgk, 0), (pu, puk, 1)):
                        for kc in range(8):
                            P.op('pe', lambda e, pp=pp, g=g, kc=kc, jj=jj, wv=wv: e.matmul(
                                pp[:], wv[:, g, kc, jj * 128:(jj + 1) * 128], xb[:, kc, :],
                                start=(kc == 0), stop=(kc == 7)), reads=[('wst', slot), ('xb', kc)], writes=[ppk])
                    P.op('act', lambda e, pg=pg, jj=jj: e.activation(out=sg[:, jj, :], in_=pg[:], func=AF.Silu),
                         reads=[pgk], writes=[('sg', jj)])
                    P.op('dve', lambda e, pu=pu, jj=jj, j=j: e.tensor_tensor(
                        out=hid[:, j, :], in0=pu[:], in1=sg[:, jj, :], op=ALU.mult),
                        reads=[puk, ('sg', jj)], writes=[('hid', j)])
            for sw in range(2):
                for jp in range(11):
                    slot = (sw * 11 + jp) % 3
                    wv = wdst[:, slot, :].rearrange("p (j n) -> p j n", j=2)
                    P.dma('pool', wv, w_d[sw, jp], writes=[('wdst', slot)])
                    for jj in range(2):
                        j = 2 * jp + jj
                        for q in range(4):
                            ps, pk = psD[q]
                            P.op('pe', lambda e, ps=ps, wv=wv, jj=jj, q=q, j=j: e.matmul(
                                ps[:], wv[:, jj, q * 128:(q + 1) * 128], hid[:, j, :],
                                start=(j == 0), stop=(j == 21)), reads=[('wdst', slot), ('hid', j)], writes=[pk])
                for q in range(4):
                    ps, pk = psD[q]
                    resid(ps, pk, sw * 4 + q)
            layer_norm(34, 42, final=True)
            if t + 1 < ntile:
                load_x(t + 1)
            for h in range(2):
                P.dma('sp', xov[:, 4 * h:4 * h + 4, ts], xalias[:, 4 * h:4 * h + 4, :],
                      reads=[k for c in range(4 * h, 4 * h + 4) for k in akeys(c)], writes=['xdst'])
            if not last:
                q = (t * NT) // 1024
                c1 = (t * NT) % 1024
                P.dma('sp', dr['xbf'][q].rearrange("(k p) n -> p k n", p=128)[:, :, c1:c1 + NT], xb[:],
                      reads=[('xb', c) for c in range(8)], writes=[('xbf', q)])
                if c1 + NT == 1024:
                    P.op('pool', lambda e, q=q: e.collective_compute(
                        "AllGather", ALU.bypass, replica_groups=PAIRS, ins=[dr['xbf'][q]], outs=[dr['xall'][q]]),
                        reads=[('xbf', q)], writes=[('xall', q)], coll=True)


def prep_D_weights(inp, l):
    f = np.float32
    kp = lambda w: np.ascontiguousarray(w.reshape(8, 128, -1).transpose(1, 0, 2))
    r = {}
    r['w_out'] = kp(inp['w_mix_out'][l])
    r['w_q'] = kp(inp['w_mem_q'][l])
    r['w_o'] = kp(inp['w_mem_o'][l])
    wkv = kp(inp['w_mem_kv'][l])
    r['w_kv'] = np.ascontiguousarray(wkv.reshape(128, 8, 4, 512).transpose(2, 0, 1, 3))
    r['w_glu'] = np.ascontiguousarray(inp['s5_w_glu'][l].reshape(2, 128, 256).transpose(1, 0, 2))
    g = kp(inp['w_ff_gate'][l]).reshape(128, 8, 11, 256)
    u = kp(inp['w_ff_up'][l]).reshape(128, 8, 11, 256)
    gu = np.stack([g, u], axis=1)
    r['w_gu'] = np.ascontiguousarray(gu.transpose(3, 0, 1, 2, 4))
    wd = inp['w_ff_down'][l].reshape(11, 2, 128, 2, 512)
    r['w_d'] = np.ascontiguousarray(wd.transpose(3, 0, 2, 1, 4))
    cols = [inp['s5_b_glu'][l].reshape(2, 128)]
    for nm in ('ln_mix_g', 'ln_mix_b', 'ln_mem_g', 'ln_mem_b', 'ln_ff_g', 'ln_ff_b'):
        cols.append(inp[nm][l].reshape(8, 128))
    r['vecs'] = np.ascontiguousarray(np.concatenate(cols, axis=0).T.astype(f))
    return r


PAIRS = [[0, 1], [2, 3], [4, 5], [6, 7]]

DBG = ''

SB = 2048
NW = 11 * 128 + 16


PAIRS = [[0, 1], [2, 3], [4, 5], [6, 7]]


def emit_A(nc, P, es, l, dr, T, stages='sABCD'):
    nsb = T // SB
    half = T // 2
    w_in, s5p, s5b, s5c, dvec = dr['w_inA'][l], dr['s5p'][l], dr['s5b'][l], dr['s5c'][l], dr['dvec'][l]
    wgate, bgate, gains = dr['wgate'][l], dr['bgate'][l], dr['gains'][l]
    abias, rtab, gmask, ident, blk, kp1, rmask = (dr[k] for k in ('abias', 'rtab', 'gmask', 'ident', 'blk', 'kp1', 'rmask'))
    xTv = dr['xT_full'].rearrange("(k p) n -> p k n", p=128)
    sfx = "_A%d" % l
    if True:
        sb = lambda name, shape, dt=BF16: es.enter_context(nc.sbuf_tensor(name + sfx, shape, dt))
        w_s = sb("w_s", [128, 8, NW])
        xsb = sb("xsb", [128, 8, SB])
        G = [sb("G%d" % i, [128, SB]) for i in range(6)]
        Y = [sb("Y%d" % i, [128, SB]) for i in range(4)]
        F = [sb("F%d" % i, [128, SB], F32) for i in range(4)]
        kA = sb("kA", [128, 2 * SB])
        vA = sb("vA", [128, 2 * SB])
        abias_s = sb("abias_s", [128, 12, 128])
        rtab_s = sb("rtab_s", [128, 5, 128], F32)
        gmask_s = sb("gmask_s", [128, 2, 128], F32)
        ident_s = sb("ident_s", [128, 128])
        identf_s = sb("identf_s", [128, 128], F32)
        blk_s = sb("blk_s", [128, 128])
        ones_s = sb("ones_s", [128, 64])
        kp1_s = sb("kp1_s", [128, 512], F32)
        rmask_s = sb("rmask_s", [128, SB])
        gains_s = sb("gains_s", [128, 2], F32)
        wgate_s = sb("wgate_s", [16, 64])
        bgate_s = sb("bgate_s", [64, 2], F32)
        dvec_s = sb("dvec_s", [128, 1], F32)
        ptA = sb("ptA", [128, 2, 4, 128])
        vblk = sb("vblk", [128, 2, 2, 128])
        ptB = sb("ptB", [128, 2, 2, 128])
        vtok = sb("vtok", [128, 2, 128])
        ktok = sb("ktok", [128, 2, 128])
        qdec = sb("qdec", [128, 2, 128])
        Rr = sb("Rr", [128, 64], F32)
        Rrb = sb("Rrb", [128, 2, 64])
        Rg = sb("Rg", [64, 64], F32)
        Rgb = sb("Rgb", [64, 2, 128])
        ebl = sb("ebl", [64, SB // 64], F32)
        ob = sb("ob", [128, 512])
        s5p_s = sb("s5p_s", [128, 4, 3], F32)
        s5b_s = sb("s5b_s", [128, 4, 2, 16], F32)
        s5c_s = sb("s5c_s", [128, 4, 2, 16], F32)
        sm = sb("sm", [128, 24, 4], F32)
        bb = sb("bb", [128, 4, 2, 16], F32)
        Zf = sb("Zf", [128, 128], F32)
        ZT = sb("ZT", [128, 4, 2, 128])
        CT = sb("CT", [128, 4, 2, 128])
        cosT = sb("cosT", [128, 4, 512], F32)
        sinT = sb("sinT", [128, 4, 512], F32)
        rho_bc = sb("rho_bc", [128, 512], F32)
        tt = [sb("tt%d" % i, [128, 512], F32) for i in range(4)]
        tg = sb("tg", [128, 512], F32)
        xri = sb("xri", [128, 2, 2, 512])
        xend = sb("xend", [128, 4, 2], F32)
        etmp = sb("etmp", [128, 4], F32)
        psb = [es.enter_context(nc.psum_tensor("ps%d" % i + sfx, [128, 512], F32)) for i in range(4)]
        pacc = es.enter_context(nc.psum_tensor("pacc" + sfx, [128, 512], F32))
        pst = [es.enter_context(nc.psum_tensor("pst%d" % i + sfx, [128, 8, 128], BF16)) for i in range(2)]
        psu = es.enter_context(nc.psum_tensor("psu" + sfx, [128, 512], F32))

        taps = []

        def tap(name, ap, keys, dt=BF16):
            if 'tap' not in DBG:
                return
            shp = list(ap.shape)
            t = nc.dram_tensor("tap_" + name, shp, dt, kind="ExternalOutput").ap()
            taps.append(P.dma('sp', t, ap, reads=keys))
        psA = Rot([(psb[i], ("ps", i)) for i in range(4)])
        psT = Rot([(pst[i], ('pst', i)) for i in range(2)])
        psAtt = Rot([(psb[i], ('ps', i)) for i in range(4)] + [(pacc, 'pacc'), (psu, 'psu')])
        ablk = [0]
        PI = math.pi

        P.dma('pool', w_s[:, 0:4, :], w_in[:, 0:4, :], writes=[('w', 0)])
        P.dma('pool', w_s[:, 4:8, :], w_in[:, 4:8, :], writes=[('w', 1)])
        WK = [('w', 0), ('w', 1)]
        P.dma('pool', abias_s[:], abias, writes=['abias'])
        P.dma('sp', rtab_s[:], rtab, writes=['rtab'])
        P.dma('sp', gmask_s[:], gmask, writes=['gmask'])
        P.dma('pool', ident_s[:], ident, writes=['ident'])
        P.dma('sp', identf_s[:], ident, writes=['identf'])
        P.dma('pool', blk_s[:], blk, writes=['blk'])
        P.dma('sp', kp1_s[:], kp1, writes=['kp1'])
        P.dma('pool', rmask_s[:], rmask, writes=['rmask'])
        P.dma('sp', gains_s[:], gains, writes=['gains'])
        P.dma('pool', wgate_s[:], wgate, writes=['wgate'])
        P.dma('sp', bgate_s[:, 0:1], bgate, writes=['bgate'])
        P.dma('sp', dvec_s[:], dvec, writes=['dvec'])
        P.dma('sp', s5p_s[:], s5p, writes=['s5p'])
        P.dma('sp', s5b_s[:], s5b, writes=['s5b'])
        P.dma('sp', s5c_s[:], s5c, writes=['s5c'])
        P.op('dve', lambda e: e.memset(ones_s[:], 1.0), writes=['ones'])
        P.op('dve', lambda e: e.tensor_scalar(out=bgate_s[:, 1:2], in0=bgate_s[:, 0:1], scalar1=-1.0, scalar2=None,
                                              op0=ALU.mult), reads=['bgate'], writes=['nbgate'])
        P.op('dve', lambda e: e.memset(Rr[:], 0.0), writes=['Rr'])
        P.op('dve', lambda e: e.memset(Rrb[:], 0.0), writes=[('Rrb', 0), ('Rrb', 1)])
        P.op('dve', lambda e: e.memset(Rg[:], 0.0), writes=['Rg'])
        P.op('dve', lambda e: e.memset(Rgb[:], 0.0), writes=[('Rgb', 0), ('Rgb', 1)])
        P.op('dve', lambda e: e.memset(xend[:], 0.0), writes=['xend'])

        P.enabled = 's' in stages
        SMK = 'sm'
        smc = lambda i: sm[:, i, :]

        def dv(fn, reads=(), writes=()):
            P.op('dve', fn, reads=list(reads) + [SMK], writes=list(writes) + [SMK])

        def av(fn, reads=(), writes=()):
            P.op('act', fn, reads=list(reads) + [SMK], writes=list(writes) + [SMK])

        def tsc(out, in0, s1, op0, s2=None, op1=None):
            if op1 is None:
                return lambda e: e.tensor_scalar(out=out, in0=in0, scalar1=s1, scalar2=None, op0=op0)
            return lambda e: e.tensor_scalar(out=out, in0=in0, scalar1=s1, scalar2=s2, op0=op0, op1=op1)

        def tten(out, a, b, op):
            return lambda e: e.tensor_tensor(out=out, in0=a, in1=b, op=op)

        def reduce_turns(dst, src, tmp, wr=dv):
            wr(lambda e: e.tensor_copy(out=tmp.bitcast(I32), in_=src))
            wr(lambda e: e.tensor_copy(out=dst, in_=tmp.bitcast(I32)))
            wr(tten(dst, src, dst, ALU.subtract))
            wr(tsc(tmp, dst, 0.5, ALU.is_gt))
            wr(tten(dst, dst, tmp, ALU.subtract))
            wr(tsc(tmp, dst, -0.5, ALU.is_lt))
            wr(tten(dst, dst, tmp, ALU.add))

        are, aim, ldt = s5p_s[:, :, 0], s5p_s[:, :, 1], s5p_s[:, :, 2]
        DT, RHO, THT, TF, TMP, TF2, SIN, COS, NR, AI, DEN, FRE, FIM, T1, T2 = [smc(i) for i in range(15)]
        av(lambda e: e.activation(out=DT, in_=ldt, func=AF.Exp), reads=['s5p'])
        dv(tten(T1, are, DT, ALU.mult), reads=['s5p'])
        av(lambda e: e.activation(out=RHO, in_=T1, func=AF.Exp))
        dv(tten(THT, aim, DT, ALU.mult), reads=['s5p'])
        dv(tsc(T2, THT, 1.0 / (2 * PI), ALU.mult))
        reduce_turns(TF, T2, TMP)
        av(lambda e: e.activation(out=SIN, in_=TF, func=AF.Sin, scale=2 * PI))
        dv(tsc(T2, T2, 0.25, ALU.add))
        reduce_turns(TF2, T2, TMP)
        av(lambda e: e.activation(out=COS, in_=TF2, func=AF.Sin, scale=2 * PI))
        dv(tten(NR, RHO, COS, ALU.mult))
        dv(tsc(NR, NR, -1.0, ALU.add))
        dv(tten(AI, RHO, SIN, ALU.mult))
        dv(tten(DEN, are, are, ALU.mult), reads=['s5p'])
        dv(tten(T1, aim, aim, ALU.mult), reads=['s5p'])
        dv(tten(DEN, DEN, T1, ALU.add))
        dv(lambda e: e.reciprocal(out=DEN, in_=DEN))
        dv(tten(FRE, NR, are, ALU.mult), reads=['s5p'])
        dv(tten(T1, AI, aim, ALU.mult), reads=['s5p'])
        dv(tten(FRE, FRE, T1, ALU.add))
        dv(tten(FRE, FRE, DEN, ALU.mult))
        dv(tten(FIM, AI, are, ALU.mult), reads=['s5p'])
        dv(tten(T1, NR, aim, ALU.mult), reads=['s5p'])
        dv(tten(FIM, FIM, T1, ALU.subtract))
        dv(tten(FIM, FIM, DEN, ALU.mult))
        for k in range(4):
            bre, bim = s5b_s[:, k, 0, :], s5b_s[:, k, 1, :]
            fre, fim = sm[:, 11, k:k + 1], sm[:, 12, k:k + 1]
            P.op('dve', tsc(bb[:, k, 0, :], bim, fim, ALU.mult), reads=['s5b', SMK], writes=[('bb', k)])
            P.op('dve', lambda e, k=k, bre=bre, fre=fre: e.scalar_tensor_tensor(
                out=bb[:, k, 0, :], in0=bre, scalar=fre, in1=bb[:, k, 0, :], op0=ALU.mult, op1=ALU.subtract),
                reads=['s5b', SMK, ('bb', k)], writes=[('bb', k)])
            P.op('dve', tsc(bb[:, k, 1, :], bre, fim, ALU.mult), reads=['s5b', SMK, ('bb', k)], writes=[('bb', k)])
            P.op('dve', lambda e, k=k, bim=bim, fre=fre: e.scalar_tensor_tensor(
                out=bb[:, k, 1, :], in0=bim, scalar=fre, in1=bb[:, k, 1, :], op0=ALU.mult, op1=ALU.add),
                reads=['s5b', SMK, ('bb', k)], writes=[('bb', k)])
            for ri in range(2):
                P.op('dve', lambda e: e.memset(Zf[:], 0.0), writes=['Zf'])
                for g2 in range(2):
                    c0 = 16 * (2 * k + g2)
                    P.op('dve', lambda e, k=k, ri=ri, g2=g2, c0=c0: e.tensor_copy(
                        out=Zf[64 * g2:64 * g2 + 64, c0:c0 + 16], in_=bb[64 * g2:64 * g2 + 64, k, ri, :]),
                        reads=[('bb', k), 'Zf'], writes=['Zf'])
                P.op('pe', lambda e: e.transpose(out=psu[:, 0:128], in_=Zf[:], identity=identf_s[:]),
                     reads=['Zf', 'identf'], writes=['psu'])
                P.op('act', lambda e, k=k, ri=ri: e.activation(out=ZT[:, k, ri, :], in_=psu[:, 0:128], func=AF.Copy),
                     reads=['psu'], writes=['ZT'])
                P.op('dve', lambda e, k=k, ri=ri: e.memset(CT[:, k, ri, :], 0.0), reads=['CT'], writes=['CT'])
                for g2 in range(2):
                    c0 = 16 * (2 * k + g2)
                    P.op('dve', tsc(CT[64 * g2:64 * g2 + 64, k, ri, c0:c0 + 16], s5c_s[64 * g2:64 * g2 + 64, k, ri, :],
                                    1.0 if ri == 0 else -1.0, ALU.mult), reads=['s5c', 'CT'], writes=['CT'])
            tf = sm[:, 3, k:k + 1]
            P.op('dve', tsc(tt[0][:], kp1_s[:], tf, ALU.mult), reads=['kp1', SMK], writes=['tt0'])
            wr = lambda fn: P.op('dve', fn, reads=['tt0', 'tt1', 'tt2'], writes=['tt0', 'tt1', 'tt2'])
            reduce_turns(tt[1][:], tt[0][:], tt[2][:], wr=wr)
            P.op('act', lambda e, k=k: e.activation(out=sinT[:, k, :], in_=tt[1][:], func=AF.Sin, scale=2 * PI),
                 reads=['tt1'], writes=['sinT'])
            wr(tsc(tt[0][:], tt[0][:], 0.25, ALU.add))
            reduce_turns(tt[1][:], tt[0][:], tt[2][:], wr=wr)
            P.op('act', lambda e, k=k: e.activation(out=cosT[:, k, :], in_=tt[1][:], func=AF.Sin, scale=2 * PI),
                 reads=['tt1'], writes=['cosT'])

        P.enabled = True
        def project(col0, width, evac):
            for nt in range(SB // 512):
                ps, pk = psA.get()
                for kc in range(8):
                    P.op('pe', lambda e, ps=ps, kc=kc, nt=nt: e.matmul(
                        ps[0:width, :], w_s[:, kc, col0:col0 + width], xsb[:, kc, nt * 512:(nt + 1) * 512],
                        start=(kc == 0), stop=(kc == 7)), reads=WK + ['xsb'], writes=[pk])
                evac(ps, pk, nt)

        def evac_copy(dst, key, rows=128, scale=1.0, func=AF.Copy, col0=0):
            def f(ps, pk, nt):
                P.op('act', lambda e: e.activation(
                    out=dst[0:rows, col0 + nt * 512:col0 + (nt + 1) * 512], in_=ps[0:rows, :], func=func, scale=scale),
                    reads=[pk], writes=[key])
            return f

        hn_done = []

        def head_norm(po, pok, gcol, gate, gatek, ydst, ykey, c0):
            P.op('act', lambda e: e.activation(out=tt[2][:], in_=po[:], func=AF.Copy), reads=[pok], writes=['tt2'])
            if c0 == 0 and not hn_done:
                hn_done.append(1)
                tap('of', tt[2][:], ['tt2'], F32)
            P.op('dve', lambda e: e.tensor_copy(out=ob[:], in_=po[:]), reads=[pok], writes=['ob'])
            pm, pmk = psA.get()
            P.op('pe', lambda e: e.matmul(pm[:], blk_s[:], ob[:], start=True, stop=True),
                 reads=['blk', 'ob'], writes=[pmk])
            P.op('dve', tten(tt[2][:], tt[2][:], pm[:], ALU.subtract), reads=[pmk, 'tt2'], writes=['tt2'])
            P.op('act', lambda e: e.activation(out=ob[:], in_=tt[2][:], func=AF.Square), reads=['tt2'], writes=['ob'])
            pv, pvk = psA.get()
            P.op('pe', lambda e: e.matmul(pv[:], blk_s[:], ob[:], start=True, stop=True),
                 reads=['blk', 'ob'], writes=[pvk])
            P.op('act', lambda e: e.activation(out=tt[0][:], in_=pv[:], func=AF.Sqrt, bias=1e-5),
                 reads=[pvk], writes=['tt0'])
            P.op('dve', lambda e: e.reciprocal(out=tt[1][:], in_=tt[0][:]), reads=['tt0'], writes=['tt1'])
            P.op('dve', tten(tt[2][:], tt[2][:], tt[1][:], ALU.mult), reads=['tt1', 'tt2'], writes=['tt2'])
            P.op('dve', lambda e: e.scalar_tensor_tensor(
                out=ydst[:, c0:c0 + 512], in0=tt[2][:], scalar=gains_s[:, gcol:gcol + 1], in1=gate[:, c0:c0 + 512],
                op0=ALU.mult, op1=ALU.mult), reads=['tt2', 'gains', gatek], writes=[ykey])

        outs = []
        for s in range(nsb):
            t0 = s * SB
            ring = (s % 2) * SB
            pring = ((s - 1) % 2) * SB
            if l == 0:
                P.dma('pool', xsb[:, 0:4, :], xTv[:, 0:4, t0:t0 + SB], writes=['xsb'])
                P.dma('pool', xsb[:, 4:8, :], xTv[:, 4:8, t0:t0 + SB], reads=['xsb'], writes=['xsb'])
            else:
                r_ = t0 // half
                q0 = (t0 - r_ * half) // 1024
                for qq in range(2):
                    src = dr['xall'][q0 + qq][r_ * 1024:(r_ + 1) * 1024, :].rearrange("(k p) n -> p k n", p=128)
                    P.dma('sp', xsb[:, :, qq * 1024:(qq + 1) * 1024], src, reads=[('xall', q0 + qq), 'xsb'], writes=['xsb'])

            for (dst, src) in dr['wconv'][l][s::nsb]:
                P.dma('pool', dst, src, writes=['wconv'])
            P.enabled = 'A' in stages
            qA = G[0]
            project(0, 128, evac_copy(qA, 'G0'))
            project(128, 128, evac_copy(kA, 'kA', scale=0.125, col0=ring))
            project(256, 128, evac_copy(vA, 'vA', col0=ring))
            accO, accL = F[0], F[1]
            P.op('dve', lambda e: e.memset(accO[:], 0.0), writes=['F0'])
            P.op('dve', lambda e: e.memset(accL[:], 0.0), writes=['F1'])
            def blk(bi, r, n, c):
                nloc = 16 // r
                base = n * 128 * r + c
                has_prev = (n > 0) or (s > 0)
                kbs = [0, 1] if has_prev else [1]
                if n > 0:
                    pbase = ring + base - 128 * r
                else:
                    pbase = pring + (nloc - 1) * 128 * r + c
                cbase = ring + base
                kcol = {0: pbase, 1: cbase}
                sl = lambda b0: slice(b0, b0 + 127 * r + 1, r)
                pT, pTk = psT.get()
                vs = (psT.i - 1) % 2
                for kb in kbs:
                    P.op('pe', lambda e, pT=pT, kb=kb, cc=kcol[kb], r=r: e.transpose(
                        out=pT[:, kb, :], in_=vA[:, slice(cc, cc + 127 * r + 1, r)], identity=ident_s[:]),
                        reads=['vA', 'ident'], writes=[pTk])
                P.op('act', lambda e, pT=pT, vs=vs, k0=kbs[0]: e.activation(
                    out=vblk[:, vs, k0:2, :], in_=pT[:, k0:2, :], func=AF.Copy),
                    reads=[pTk], writes=[('vblk', vs)])
                ps, pk = psAtt.get()
                psv = ps[:].rearrange("p (h k q) -> p h k q", h=2, k=2)
                for h in range(2):
                    for kb in kbs:
                        P.op('pe', lambda e, psv=psv, h=h, kb=kb, cc=kcol[kb], r=r, base=base: e.matmul(
                            psv[:, h, kb, :], kA[64 * h:64 * h + 64, slice(cc, cc + 127 * r + 1, r)],
                            qA[64 * h:64 * h + 64, slice(base, base + 127 * r + 1, r)], start=True, stop=False),
                            reads=['kA', 'G0'], writes=[pk])
                        P.op('pe', lambda e, psv=psv, h=h, kb=kb, bi=bi: e.matmul(
                            psv[:, h, kb, :], ident_s[:], abias_s[:, bi * 4 + h * 2 + kb, :], start=False, stop=True),
                            reads=['ident', 'abias'], writes=[pk])
                pa = ablk[0] % 2
                ablk[0] += 1
                ptv = ptA[:, pa, :, :].rearrange("p (h k) q -> p h k q", h=2)
                P.op('act', lambda e, psv=psv, ptv=ptv, k0=kbs[0]: e.activation(
                    out=ptv[:, :, k0:2, :], in_=psv[:, :, k0:2, :], func=AF.Exp),
                    reads=[pk], writes=[('ptA', pa)])
                yield
                po, pok = psAtt.get()
                for h in range(2):
                    for idx, kb in enumerate(kbs):
                        P.op('pe', lambda e, po=po, h=h, kb=kb, vs=vs, ptv=ptv, st=(idx == 0), sp=(kb == 1): e.matmul(
                            po[64 * h:64 * h + 64, 0:128], vblk[:, vs, kb, 64 * h:64 * h + 64], ptv[:, h, kb, :],
                            start=st, stop=sp), reads=[('vblk', vs), ('ptA', pa)], writes=[pok])
                    for idx, kb in enumerate(kbs):
                        P.op('pe', lambda e, po=po, h=h, kb=kb, ptv=ptv, st=(idx == 0), sp=(kb == 1): e.matmul(
                            po[64 * h:64 * h + 64, 128:256], ones_s[:, :], ptv[:, h, kb, :],
                            start=st, stop=sp), reads=['ones', ('ptA', pa)], writes=[pok])
                osl = slice(base, base + 127 * r + 1, r)
                P.op('dve', lambda e, po=po, osl=osl: e.tensor_tensor(
                    out=accO[:, osl], in0=accO[:, osl], in1=po[:, 0:128], op=ALU.add),
                    reads=[pok, 'F0'], writes=['F0'])
                P.op('dve', lambda e, po=po, osl=osl: e.tensor_tensor(
                    out=accL[:, osl], in0=accL[:, osl], in1=po[:, 128:256], op=ALU.add),
                    reads=[pok, 'F1'], writes=['F1'])

            prev = None
            for bi, r in enumerate((1, 4, 16)):
                for n in range(16 // r):
                    for c in range(r):
                        g = blk(bi, r, n, c)
                        next(g)
                        if prev is not None:
                            next(prev, None)
                        prev = g
            next(prev, None)
            P.op('dve', lambda e: e.reciprocal(out=accL[:], in_=accL[:]), reads=['F1'], writes=['F1'])
            P.op('dve', tten(Y[0][:], accO[:], accL[:], ALU.mult), reads=['F0', 'F1'], writes=['Y0'])

            P.enabled = 'B' in stages
            qB, kB, vB, sgB = G[0], G[1], G[2], G[3]
            project(384, 128, evac_copy(qB, 'G0'))
            project(512, 128, evac_copy(kB, 'G1'))
            project(640, 128, evac_copy(vB, 'G2'))
            project(768, 128, evac_copy(sgB, 'G3', func=AF.Silu))
            def ret_chunk(c):
                cs = slice(c * 128, (c + 1) * 128)
                sl2 = c % 2
                rs = c % 2
                oc = slice((c % 4) * 128, (c % 4 + 1) * 128)
                po, pok = pacc, 'pacc'
                for h in range(2):
                    ps, pk = psA.get()
                    P.op('pe', lambda e, ps=ps, h=h: e.matmul(
                        ps[:, 0:128], kB[64 * h:64 * h + 64, cs], qB[64 * h:64 * h + 64, cs], start=True, stop=True),
                        reads=['G0', 'G1'], writes=[pk])
                    P.op('dve', lambda e, ps=ps, h=h: e.tensor_tensor(
                        out=ptB[:, sl2, h, :], in0=ps[:, 0:128], in1=rtab_s[:, h, :], op=ALU.mult),
                        reads=[pk, 'rtab'], writes=[('ptB', sl2, h)])
                pT, pTk = psT.get()
                P.op('pe', lambda e: e.transpose(out=pT[:, 0, :], in_=vB[:, cs], identity=ident_s[:]),
                     reads=['G2', 'ident'], writes=[pTk])
                P.op('pe', lambda e: e.transpose(out=pT[:, 1, :], in_=kB[:, cs], identity=ident_s[:]),
                     reads=['G1', 'ident'], writes=[pTk])
                P.op('act', lambda e: e.activation(out=vtok[:, sl2, :], in_=pT[:, 0, :], func=AF.Copy),
                     reads=[pTk], writes=[('vtok', sl2)])
                P.op('act', lambda e: e.activation(out=ktok[:, sl2, :], in_=pT[:, 1, :], func=AF.Copy),
                     reads=[pTk], writes=[('ktok', sl2)])
                P.op('dve', lambda e: e.tensor_tensor(
                    out=ktok[:, sl2, :], in0=ktok[:, sl2, :], in1=rtab_s[:, 3, :], op=ALU.mult),
                    reads=[('ktok', sl2), 'rtab'], writes=[('ktok', sl2)])
                P.op('dve', lambda e: e.tensor_tensor(
                    out=qdec[:, sl2, :], in0=qB[:, cs], in1=rtab_s[:, 2, :], op=ALU.mult),
                    reads=['G0', 'rtab'], writes=[('qdec', sl2)])
                yield
                for h in range(2):
                    hs = slice(64 * h, 64 * h + 64)
                    P.op('pe', lambda e, hs=hs: e.matmul(
                        psu[hs, 0:64], ktok[:, sl2, hs], vtok[:, sl2, hs], start=True, stop=True),
                        reads=[('ktok', sl2), ('vtok', sl2)], writes=['psu'])
                P.op('dve', lambda e: e.scalar_tensor_tensor(
                    out=Rr[:], in0=Rr[:], scalar=rtab_s[:, 4, 0:1], in1=psu[:, 0:64], op0=ALU.mult, op1=ALU.add),
                    reads=['psu', 'Rr', 'rtab'], writes=['Rr'])
                P.op('act', lambda e: e.activation(out=Rrb[:, 1 - rs, :], in_=Rr[:], func=AF.Copy),
                     reads=['Rr'], writes=[('Rrb', 1 - rs)])
                for h in range(2):
                    hs = slice(64 * h, 64 * h + 64)
                    P.op('pe', lambda e, hs=hs, h=h: e.matmul(
                        po[hs, oc], vtok[:, sl2, hs], ptB[:, sl2, h, :], start=True, stop=False),
                        reads=[('vtok', sl2), ('ptB', sl2, h)], writes=[pok])
                    P.op('pe', lambda e, hs=hs: e.matmul(
                        po[hs, oc], Rrb[hs, rs, :], qdec[hs, sl2, :], start=False, stop=True),
                        reads=[('Rrb', rs), ('qdec', sl2)], writes=[pok])
                if c % 4 == 3:
                    head_norm(po, pok, 0, sgB, 'G3', Y[1], 'Y1', (c // 4) * 512)
                yield

            prev = None
            for c in range(SB // 128):
                g = ret_chunk(c)
                next(g)
                if prev is not None:
                    next(prev, None)
                prev = g
            next(prev, None)

            P.enabled = 'C' in stages
            qC, kC, laF, bcF = F[0], F[1], F[2], F[3]
            vC, sgC, glow, qin, kout, kdc = G[0], G[1], G[2], G[3], G[4], G[5]
            project(896, 64, evac_copy(qC, 'F0', rows=64))
            project(960, 64, evac_copy(kC, 'F1', rows=64))
            project(1024, 128, evac_copy(vC, 'G0'))
            project(1152, 128, evac_copy(sgC, 'G1', func=AF.Silu))
            project(1408, 16, evac_copy(glow, 'G2', rows=16))
            for nt in range(SB // 512):
                ns = slice(nt * 512, (nt + 1) * 512)
                ps, pk = psA.get()
                P.op('pe', lambda e, ps=ps, ns=ns: e.matmul(ps[0:64, :], wgate_s[:, :], glow[0:16, ns], start=True, stop=True),
                     reads=['wgate', 'G2'], writes=[pk])
                P.op('act', lambda e, ps=ps, ns=ns: e.activation(
                    out=laF[0:64, ns], in_=ps[0:64, :], func=AF.Exp, scale=-1.0, bias=bgate_s[:, 1:2]),
                    reads=[pk, 'nbgate'], writes=['F2'])
            P.op('act', lambda e: e.activation(out=laF[0:64, :], in_=laF[0:64, :], func=AF.Ln, bias=1.0),
                 reads=['F2'], writes=['F2'])
            P.op('dve', tsc(laF[0:64, :], laF[0:64, :], -1.0 / 16.0, ALU.mult), reads=['F2'], writes=['F2'])
            P.op('dve', lambda e: e.tensor_tensor_scan(
                out=bcF[0:64, :], data0=rmask_s[0:64, :], data1=laF[0:64, :], initial=0.0, op0=ALU.mult, op1=ALU.add),
                reads=['F2', 'rmask'], writes=['F3'])
            P.op('act', lambda e: e.activation(out=laF[0:64, :], in_=bcF[0:64, :], func=AF.Exp), reads=['F3'], writes=['F2'])
            P.op('dve', lambda e: e.scalar_tensor_tensor(
                out=qin[0:64, :], in0=qC[0:64, :], scalar=32.0 ** -0.5, in1=laF[0:64, :], op0=ALU.mult, op1=ALU.mult),
                reads=['F0', 'F2'], writes=['G3'])
            P.op('act', lambda e: e.activation(out=laF[0:64, :], in_=bcF[0:64, :], func=AF.Exp, scale=-1.0),
                 reads=['F3', 'G3'], writes=['F2'])
            P.op('dve', tten(kout[0:64, :], kC[0:64, :], laF[0:64, :], ALU.mult), reads=['F1', 'F2'], writes=['G4'])
            blast = bcF[0:64, :].rearrange("p (c j) -> p c j", j=64)[:, :, 63]
            P.op('act', lambda e: e.activation(out=ebl[:, :], in_=blast, func=AF.Exp), reads=['F3'], writes=['ebl'])
            P.op('dve', lambda e: e.tensor_tensor(
                out=kdc[0:64, :].rearrange("p (c j) -> p c j", j=64),
                in0=kout[0:64, :].rearrange("p (c j) -> p c j", j=64),
                in1=ebl[:, :].unsqueeze(2).to_broadcast([64, SB // 64, 64]), op=ALU.mult),
                reads=['G4', 'ebl'], writes=['G5'])
            def gla_chunk(c):
                cs = slice(c * 128, (c + 1) * 128)
                sl2 = c % 2
                oc0 = (c % 4) * 128
                po, pok = pacc, 'pacc'
                for h in range(2):
                    ps, pk = psA.get()
                    P.op('pe', lambda e, ps=ps, h=h: e.matmul(
                        ps[:, 0:128], kout[32 * h:32 * h + 32, cs], qin[32 * h:32 * h + 32, cs], start=True, stop=True),
                        reads=['G3', 'G4'], writes=[pk])
                    P.op('dve', lambda e, ps=ps, h=h: e.tensor_tensor(
                        out=ptB[:, sl2, h, :], in0=ps[:, 0:128], in1=gmask_s[:, h, :], op=ALU.mult),
                        reads=[pk, 'gmask'], writes=[('ptB', sl2, h)])
                pT, pTk = psT.get()
                P.op('pe', lambda e: e.transpose(out=pT[:, 0, :], in_=vC[:, cs], identity=ident_s[:]),
                     reads=['G0', 'ident'], writes=[pTk])
                P.op('pe', lambda e: e.transpose(out=pT[:, 1, 0:64], in_=kdc[0:64, cs], identity=ident_s[0:64, 0:64]),
                     reads=['G5', 'ident'], writes=[pTk])
                P.op('act', lambda e: e.activation(out=vtok[:, sl2, :], in_=pT[:, 0, :], func=AF.Copy),
                     reads=[pTk], writes=[('vtok', sl2)])
                P.op('dve', lambda e: e.tensor_copy(out=ktok[:, sl2, 0:64], in_=pT[:, 1, 0:64]),
                     reads=[pTk], writes=[('ktok', sl2)])
                yield

                def update(cc):
                    ci = 2 * c + cc
                    ts_ = slice(64 * cc, 64 * cc + 64)
                    for h in range(2):
                        ks = slice(32 * h, 32 * h + 32)
                        hs = slice(64 * h, 64 * h + 64)
                        P.op('pe', lambda e, ks=ks, hs=hs: e.matmul(
                            psu[ks, 64:128], ktok[ts_, sl2, ks], vtok[ts_, sl2, hs], start=True, stop=True),
                            reads=[('ktok', sl2), ('vtok', sl2)], writes=['psu'])
                    P.op('dve', lambda e: e.scalar_tensor_tensor(
                        out=Rg[:], in0=Rg[:], scalar=ebl[:, ci:ci + 1], in1=psu[0:64, 64:128], op0=ALU.mult, op1=ALU.add),
                        reads=['psu', 'Rg', 'ebl'], writes=['Rg'])
                    for h in range(2):
                        P.op('act', lambda e, h=h: e.activation(
                            out=Rgb[32 * h:32 * h + 32, 1 - cc, 64 * h:64 * h + 64], in_=Rg[32 * h:32 * h + 32, :], func=AF.Copy),
                            reads=['Rg'], writes=[('Rgb', 1 - cc)])

                def cross(cc):
                    c64 = slice(c * 128 + cc * 64, c * 128 + cc * 64 + 64)
                    P.op('pe', lambda e: e.matmul(
                        po[:, oc0 + cc * 64:oc0 + cc * 64 + 64], Rgb[:, cc, :], qin[0:64, c64], start=False, stop=(cc == 1)),
                        reads=[('Rgb', cc), 'G3'], writes=[pok])

                update(0)
                for h in range(2):
                    hs = slice(64 * h, 64 * h + 64)
                    P.op('pe', lambda e, hs=hs, h=h: e.matmul(
                        po[hs, oc0:oc0 + 128], vtok[:, sl2, hs], ptB[:, sl2, h, :], start=True, stop=False),
                        reads=[('vtok', sl2), ('ptB', sl2, h)], writes=[pok])
                cross(0)
                update(1)
                cross(1)
                if c % 4 == 3:
                    head_norm(po, pok, 1, sgC, 'G1', Y[2], 'Y2', (c // 4) * 512)
                yield

            prev = None
            for c in range(SB // 128):
                g = gla_chunk(c)
                next(g)
                if prev is not None:
                    next(prev, None)
                prev = g
            next(prev, None)

            P.enabled = 'D' in stages
            uD = G[0]
            project(1280, 128, evac_copy(uD, 'G0'))
            for seg in range(SB // 512):
                ss = slice(seg * 512, (seg + 1) * 512)
                py, pyk = psu, 'psu'
                for k in range(4):
                    pr, prk = psA.get()
                    pi_, pik = psA.get()
                    P.op('pe', lambda e, pr=pr, k=k, ss=ss: e.matmul(pr[:], ZT[:, k, 0, :], uD[:, ss], start=True, stop=True),
                         reads=['ZT', 'G0'], writes=[prk])
                    P.op('pe', lambda e, pi_=pi_, k=k, ss=ss: e.matmul(pi_[:], ZT[:, k, 1, :], uD[:, ss], start=True, stop=True),
                         reads=['ZT', 'G0'], writes=[pik])
                    cs_, sn_ = cosT[:, k, :], sinT[:, k, :]
                    T0, T1_, T2_, T3_ = tt[0][:], tt[1][:], tt[2][:], tt[3][:]
                    P.op('dve', tten(T0, pr[:], cs_, ALU.mult), reads=[prk, 'cosT'], writes=['tt0'])
                    P.op('dve', tten(T1_, pi_[:], sn_, ALU.mult), reads=[pik, 'sinT'], writes=['tt1'])
                    P.op('dve', tten(T0, T0, T1_, ALU.add), reads=['tt0', 'tt1'], writes=['tt0'])
                    P.op('dve', tten(T1_, pi_[:], cs_, ALU.mult), reads=[pik, 'cosT', 'tt0'], writes=['tt1'])
                    P.op('dve', tten(T2_, pr[:], sn_, ALU.mult), reads=[prk, 'sinT'], writes=['tt2'])
                    P.op('dve', tten(T1_, T1_, T2_, ALU.subtract), reads=['tt1', 'tt2'], writes=['tt1'])
                    rho_k = sm[:, 1, k:k + 1].to_broadcast([128, 512])
                    P.op('dve', lambda e, k=k, rho_k=rho_k: e.tensor_tensor_scan(
                        out=tt[2][:], data0=rho_k, data1=tt[0][:], initial=xend[:, k, 0:1],
                        op0=ALU.mult, op1=ALU.add), reads=['tt0', SMK, 'xend'], writes=['tt2'])
                    P.op('dve', lambda e, k=k, rho_k=rho_k: e.tensor_tensor_scan(
                        out=tt[3][:], data0=rho_k, data1=tt[1][:], initial=xend[:, k, 1:2],
                        op0=ALU.mult, op1=ALU.add), reads=['tt1', SMK, 'xend'], writes=['tt3'])
                    xs_ = k % 2
                    P.op('dve', tten(T0, T2_, cs_, ALU.mult), reads=['tt2', 'cosT'], writes=['tt0'])
                    P.op('dve', tten(T1_, T3_, sn_, ALU.mult), reads=['tt3', 'sinT'], writes=['tt1'])
                    P.op('dve', tten(xri[:, xs_, 0, :], T0, T1_, ALU.subtract), reads=['tt0', 'tt1'], writes=[('xri', xs_, 0)])
                    P.op('dve', tten(etmp[:, 0:1], T0[:, 511:512], T1_[:, 511:512], ALU.subtract),
                         reads=['tt0', 'tt1'], writes=['etmp'])
                    P.op('dve', tten(T0, T2_, sn_, ALU.mult), reads=['tt2', 'sinT', ('xri', xs_, 0), 'etmp'], writes=['tt0'])
                    P.op('dve', tten(T1_, T3_, cs_, ALU.mult), reads=['tt3', 'cosT', ('xri', xs_, 0), 'etmp'], writes=['tt1'])
                    P.op('dve', tten(xri[:, xs_, 1, :], T0, T1_, ALU.add), reads=['tt0', 'tt1'], writes=[('xri', xs_, 1)])
                    P.op('dve', tten(xend[:, k, 1:2], T0[:, 511:512], T1_[:, 511:512], ALU.add),
                         reads=['tt0', 'tt1', 'xend'], writes=['xend'])
                    P.op('dve', lambda e, k=k: e.tensor_copy(out=xend[:, k, 0:1], in_=etmp[:, 0:1]),
                         reads=['etmp', 'xend'], writes=['xend'])
                    for ri in range(2):
                        P.op('pe', lambda e, py=py, k=k, ri=ri, xs_=xs_: e.matmul(
                            py[:], CT[:, k, ri, :], xri[:, xs_, ri, :], start=(k == 0 and ri == 0), stop=(k == 3 and ri == 1)),
                            reads=['CT', ('xri', xs_, ri)], writes=[pyk])
                P.op('dve', lambda e, py=py, ss=ss: e.scalar_tensor_tensor(
                    out=tg[:], in0=uD[:, ss], scalar=dvec_s[:, 0:1], in1=py[:], op0=ALU.mult, op1=ALU.add),
                    reads=[pyk, 'G0', 'dvec'], writes=['tg'])
                P.op('act', lambda e, ss=ss: e.activation(out=Y[3][:, ss], in_=tg[:], func=AF.Gelu),
                     reads=['tg'], writes=['Y3'])

            P.enabled = True
            for m in range(4):
                P.dma('sp', dr['ybuf'][l][s][m * 128:(m + 1) * 128, :], Y[m][:], reads=['Y%d' % m], writes=[('ybuf', s)])
            P.op('pool', lambda e, s=s: e.collective_compute(
                "AllGather", ALU.bypass, replica_groups=PAIRS, ins=[dr['ybuf'][l][s]], outs=[dr['yall'][l][s]]),
                reads=[('ybuf', s)], writes=[('yall', s)], coll=True)


def prep_A_consts(p):
    f = np.float32
    q = np.arange(128)[None, :]
    k = np.arange(128)[:, None]
    ab = np.zeros((128, 12, 128), f)
    for bi, r in enumerate((1, 4, 16)):
        for h in range(2):
            slope = 2.0 ** (-2.0 * (2 * p + h + 1))
            prev = np.where(k >= q, -slope * r * (q - k + 128), -1e30)
            cur = np.where(k <= q, -slope * r * (q - k), -1e30)
            ab[:, bi * 4 + h * 2 + 0, :] = prev
            ab[:, bi * 4 + h * 2 + 1, :] = cur
    rt = np.zeros((128, 5, 128), np.float64)
    n = np.arange(128)[None, :]
    m = np.arange(128)[:, None]
    for h in range(2):
        lg = np.log(1.0 - 2.0 ** (-5.0 - (2 * p + h)))
        rt[:, h, :] = np.where(n >= m, np.exp(np.maximum(n - m, 0) * lg), 0.0) * 0.125
        rt[64 * h:64 * h + 64, 2, :] = np.exp((np.arange(128) + 1.0) * lg)[None, :]
        rt[:, 3, 64 * h:64 * h + 64] = (np.exp((127 - np.arange(128)) * lg) * 0.125)[:, None]
        rt[64 * h:64 * h + 64, 4, :] = np.exp(128 * lg)
    gm = np.where((n >= m) & ((n // 64) == (m // 64)), 1.0, 0.0)
    blk = np.zeros((128, 128), f)
    blk[:64, :64] = 1.0 / 64
    blk[64:, 64:] = 1.0 / 64
    rm = np.ones((128, SB), f)
    rm[:, ::64] = 0.0
    return dict(abias=ab, rtab=rt.astype(f), gmask=np.stack([gm, gm], 1).astype(f), ident=np.eye(128, dtype=f),
                blk=blk, kp1=np.broadcast_to(np.arange(1, 513, dtype=f), (128, 512)).copy(), rmask=rm)


def prep_A_weights(inp, l, p):
    f = np.float32
    w = inp['w_in'][l]
    offs = np.cumsum([0, 256, 256, 256, 256, 256, 256, 256, 128, 128, 256, 16, 256, 256])
    seg = lambda i, a, b: w[:, offs[i] + a:offs[i] + b]
    hp = lambda i, wd: seg(i, p * wd, (p + 1) * wd)
    cols = [hp(0, 128), hp(1, 128), hp(2, 128), hp(3, 128), hp(4, 128), hp(5, 128), hp(6, 128),
            hp(7, 64), hp(8, 64), hp(9, 128), hp(11, 128), hp(12, 128), seg(10, 0, 16)]
    wc = np.concatenate(cols, axis=1)
    assert wc.shape[1] == NW
    r = {}
    r['w_in'] = np.ascontiguousarray(wc.reshape(8, 128, NW).transpose(1, 0, 2))
    gs = slice(8 * p, 8 * p + 8)
    tile = lambda a: np.ascontiguousarray(a[gs].reshape(4, 128, *a.shape[2:]))
    are, aim = tile(inp['s5_a_re'][l]), tile(inp['s5_a_im'][l])
    ldt = tile(np.broadcast_to(inp['s5_log_dt'][l][:, None], (16, 64)))
    r['s5p'] = np.ascontiguousarray(np.stack([are, aim, ldt], -1).transpose(1, 0, 2)).astype(f)
    bre, bim = tile(inp['s5_b_re'][l]), tile(inp['s5_b_im'][l])
    r['s5b'] = np.ascontiguousarray(np.stack([bre, bim], 2).transpose(1, 0, 2, 3)).astype(f)
    cre = tile(inp['s5_c_re'][l].transpose(0, 2, 1))
    cim = tile(inp['s5_c_im'][l].transpose(0, 2, 1))
    r['s5c'] = np.ascontiguousarray(np.stack([cre, cim], 2).transpose(1, 0, 2, 3)).astype(f)
    r['dvec'] = np.ascontiguousarray(inp['s5_d'][l][gs].reshape(128, 1)).astype(f)
    r['wgate'] = np.ascontiguousarray(inp['gla_w_gate'][l][:, 64 * p:64 * p + 64])
    r['bgate'] = np.ascontiguousarray(inp['gla_b_gate'][l][64 * p:64 * p + 64].reshape(64, 1))
    r['gains'] = np.ascontiguousarray(np.stack([inp['ret_gn_g'][l][128 * p:128 * p + 128],
                                                inp['gla_gn_g'][l][128 * p:128 * p + 128]], 1)).astype(f)
    return r


def build_fused(S=8192, ncores=8, depth=2):
    nc = bass.Bass("TRN2", target_bir_lowering=False)
    pairs = [[2 * i, 2 * i + 1] for i in range(ncores // 2)]
    global PAIRS
    PAIRS = pairs
    half = S // 2
    nsb = S // SB
    nq = half // 1024
    L = depth
    di = lambda name, shape, dt=F32: nc.dram_tensor(name, shape, dt, kind="ExternalInput").ap()
    dn = lambda name, shape, dt=BF16: nc.dram_tensor(name, shape, dt, kind="Internal").ap()
    dr = {}
    dr['xT_full'] = di("xT_full", [1024, S])
    dr['xT_own'] = di("xT_own", [1024, half])
    dr['memT'] = di("memT", [128, 8, 256])
    dr['sel'] = di("sel", [128, 2])
    for k, shp in (('abias', [128, 12, 128]), ('rtab', [128, 5, 128]), ('gmask', [128, 2, 128]), ('ident', [128, 128]),
                   ('blk', [128, 128]), ('kp1', [128, 512]), ('rmask', [128, SB])):
        dr[k] = di(k, shp)
    for k, shp in (('w_inA', [128, 8, NW]), ('s5p', [128, 4, 3]), ('s5b', [128, 4, 2, 16]), ('s5c', [128, 4, 2, 16]),
                   ('dvec', [128, 1]), ('wgate', [16, 64]), ('bgate', [64, 1]), ('gains', [128, 2]),
                   ('w_out', [128, 8, 1024]), ('w_q', [128, 8, 1024]), ('w_o', [128, 8, 1024]), ('w_kv', [4, 128, 8, 512]),
                   ('w_glu', [128, 2, 256]), ('w_gu', [11, 128, 2, 8, 256]), ('w_d', [2, 11, 128, 2, 512]), ('vecs', [128, 50])):
        dr[k] = di(k, [L] + shp)
    wshapes = dict(w_out=[128, 8, 1024], w_q=[128, 8, 1024], w_o=[128, 8, 1024], w_kv=[4, 128, 8, 512],
                   w_glu=[128, 2, 256], w_gu=[11, 128, 2, 8, 256], w_d=[2, 11, 128, 2, 512])
    dr['wconv'] = [[] for _ in range(L)]
    for k, shp in wshapes.items():
        dr['wb_' + k] = dn("wb_" + k, [L] + shp)
        for l in range(L):
            src, dst = dr[k][l], dr['wb_' + k][l]
            if k in ('w_out', 'w_q', 'w_o'):
                pieces = [(dst[:, 4 * h:4 * h + 4, :], src[:, 4 * h:4 * h + 4, :]) for h in range(2)]
            elif k == 'w_kv':
                pieces = [(dst[i], src[i]) for i in range(4)]
            elif k == 'w_gu':
                pieces = [(dst[i], src[i]) for i in range(11)]
            elif k == 'w_d':
                pieces = [(dst[i], src[i]) for i in range(2)]
            else:
                pieces = [(dst, src)]
            dr['wconv'][l] += pieces
    dr['xo'] = nc.dram_tensor("xo", [1024, half], F32, kind="ExternalOutput").ap()
    dr['ybuf'] = [[dn("ybuf_%d_%d" % (l, s), [512, SB]) for s in range(nsb)] for l in range(L)]
    dr['yall'] = [[dn("yall_%d_%d" % (l, s), [1024, SB]) for s in range(nsb)] for l in range(L)]
    dr['xown'] = dn("xown", [1024, half], F32)
    dr['xbf'] = [dn("xbf_%d" % q, [1024, 1024]) for q in range(nq)]
    dr['xall'] = [dn("xall_%d" % q, [2048, 1024]) for q in range(nq)]
    with ExitStack() as es_outer:
        P = Prog(nc, es_outer)
        for l in range(L):
            with ExitStack() as es:
                emit_A(nc, P, es, l, dr, S)
                P.emit_block()
            with ExitStack() as es:
                emit_D(nc, P, es, l, dr, half, last=(l == L - 1))
                P.emit_block()
    return nc


def make_in_maps(inp, ncores=8):
    x = inp['x']
    S = x.shape[1]
    half = S // 2
    depth = inp['w_in'].shape[0]
    constsA = [prep_A_consts(p) for p in range(2)]
    wA = [[prep_A_weights(inp, l, p) for l in range(depth)] for p in range(2)]
    wD = [prep_D_weights(inp, l) for l in range(depth)]
    wDs = {k: np.stack([wD[l][k] for l in range(depth)]) for k in wD[0]}
    wAs = [{('w_inA' if k == 'w_in' else k): np.stack([wA[p][l][k] for l in range(depth)]) for k in wA[p][0]} for p in range(2)]
    maps = []
    for c in range(ncores):
        b, p = c // 2, c % 2
        xT = np.ascontiguousarray(x[b].T)
        m = {}
        m.update(constsA[p])
        m.update(wAs[p])
        m.update(wDs)
        m['xT_full'] = xT
        m['xT_own'] = np.ascontiguousarray(xT[:, p * half:(p + 1) * half])
        m['memT'] = np.ascontiguousarray(inp['mem'][b].T.reshape(8, 128, -1).transpose(1, 0, 2))
        sel = np.zeros((128, 2), np.float32)
        sel[:, p] = 1.0
        m['sel'] = sel
        maps.append(m)
    return maps


def kernel(**inputs):
    inp = {k: np.asarray(v) for k, v in inputs.items()}
    x = inp['x']
    nb, S, D = x.shape
    ncores = 2 * nb
    half = S // 2
    nc = build_fused(S, ncores, inp['w_in'].shape[0])
    maps = make_in_maps(inp, ncores)
    res = run_bass_kernel_spmd(nc, maps, core_ids=list(range(ncores)))
    out = np.zeros((nb, S, D), np.float32)
    for c in range(ncores):
        out[c // 2, (c % 2) * half:(c % 2 + 1) * half] = np.asarray(res.results[c]['xo']).T
    return out
```

```python
import math
import numpy as np
import ml_dtypes
from contextlib import ExitStack
import concourse.bass as bass
import concourse.mybir as mybir
from concourse.bass_utils import run_bass_kernel_spmd

F32 = mybir.dt.float32
BF16 = mybir.dt.bfloat16
I32 = mybir.dt.int32
AF = mybir.ActivationFunctionType
ALU = mybir.AluOpType
AX = mybir.AxisListType

ENGS = ('pe', 'act', 'dve', 'pool', 'sp')
WIN = 2000
NP_DMA = 6
WIN_D = 120


def is_psum_key(b):
    n = b[0] if isinstance(b, tuple) else b
    return isinstance(n, str) and n.startswith('ps') or n == 'pacc'


class Prog:
    def __init__(self, nc, es):
        self.nc = nc
        self.es = es
        self.ops = []
        self.last_w = {}
        self.readers = {}
        self.enabled = True
        self.start = 0
        self.sems = {}
        self.ccount = {e: 0 for e in ENGS}
        self.dcount = {e: 0 for e in ENGS}
        self.comp = {}
        self.throttle = {}
        self.waited = {e: {} for e in ENGS}

    def op(self, eng, fn, reads=(), writes=(), dma=False, coll=False):
        if not self.enabled:
            return None
        i = len(self.ops)
        reads = list(reads)
        writes = list(writes)
        for b in reads:
            if is_psum_key(b) and b not in writes:
                writes.append(b)
        deps = set()
        for b in reads:
            w = self.last_w.get(b)
            if w is not None:
                deps.add(w)
        for b in writes:
            w = self.last_w.get(b)
            if w is not None:
                deps.add(w)
            for r in self.readers.get(b, ()):
                deps.add(r)
        for b in reads:
            self.readers.setdefault(b, []).append(i)
        for b in writes:
            self.last_w[b] = i
            self.readers[b] = []
        deps.discard(i)
        self.ops.append(dict(eng=eng, fn=fn, deps=deps, dma=dma, coll=coll))
        return i

    def dma(self, eng, out, in_, reads=(), writes=(), **kw):
        return self.op(eng, lambda e: e.dma_start(out=out, in_=in_, **kw), reads, writes, dma=True)

    def _sem(self, key):
        if key not in self.sems:
            self.sems[key] = self.es.enter_context(self.nc.semaphore("s_" + "_".join(str(x) for x in key)))
        return self.sems[key]

    def emit_block(self):
        nc = self.nc
        ops = self.ops
        lo = self.start
        asyncs = [i for i in range(lo, len(ops)) if ops[i]['dma'] or ops[i]['coll']]
        ops.append(dict(eng='sp', fn=None, deps=set(asyncs), dma=False, coll=False))
        hi = len(ops)

        def skip(od, o):
            return od['eng'] == 'pe' and o['eng'] == 'pe' and not od['dma'] and not o['dma']

        need = set()
        for i in range(lo, hi):
            for d in ops[i]['deps']:
                if not skip(ops[d], ops[i]):
                    need.add(d)
        for i in range(lo, hi):
            o = ops[i]
            e = o['eng']
            if o['dma']:
                j = self.dcount[e]
                self.dcount[e] += 1
                s, r = j % NP_DMA, j // NP_DMA
                key = ('d', e, s, r // WIN_D)
                self.comp[i] = (key, 16 * (r % WIN_D + 1))
                if r >= 1 and (r % WIN_D) != 0:
                    self.throttle[i] = (key, 16 * (r % WIN_D))
                elif r >= 1:
                    self.throttle[i] = (('d', e, s, (r - 1) // WIN_D), 16 * ((r - 1) % WIN_D + 1))
            elif i in need:
                k = self.ccount[e]
                self.ccount[e] += 1
                self.comp[i] = (('c', e, k // WIN), k % WIN + 1)
        comp, throttle = self.comp, self.throttle
        per = {e: [i for i in range(lo, hi) if ops[i]['eng'] == e] for e in ENGS}
        with nc.Block() as block:
            def run(e, eng):
                waited = self.waited[e]
                for i in per[e]:
                    o = ops[i]
                    ws = [comp[d] for d in sorted(o['deps']) if not skip(ops[d], o)]
                    if i in throttle:
                        ws.append(throttle[i])
                    for key, val in ws:
                        if waited.get(key, 0) >= val:
                            continue
                        waited[key] = val
                        eng.wait_ge(self._sem(key), val)
                    if o['fn'] is None:
                        continue
                    ins = o['fn'](eng)
                    if i in comp:
                        key, val = comp[i]
                        ins.then_inc(self._sem(key), 16 if o['dma'] else 1)

            @block.tensor
            def _(eng):
                run('pe', eng)

            @block.scalar
            def _(eng):
                run('act', eng)

            @block.vector
            def _(eng):
                run('dve', eng)

            @block.gpsimd
            def _(eng):
                run('pool', eng)

            @block.sync
            def _(eng):
                run('sp', eng)
        self.start = hi
        self.last_w.clear()
        self.readers.clear()


ALPHA = float((2 * 2) ** 0.25)
LN_EPS = 1e-5
NT = 512


class Rot:
    def __init__(self, slots):
        self.slots = slots
        self.i = 0

    def get(self):
        s = self.slots[self.i % len(self.slots)]
        self.i += 1
        return s


def emit_D(nc, P, es, l, dr, TOK, last):
    ntile = TOK // NT
    memT, vecs = dr['memT'], dr['vecs'][l]
    w_out, w_q, w_o, w_kv, w_glu, w_gu, w_d = (dr['wb_' + k][l] for k in ('w_out', 'w_q', 'w_o', 'w_kv', 'w_glu', 'w_gu', 'w_d'))
    xsrc = dr['xT_own'] if l == 0 else dr['xown']
    xTv = xsrc.rearrange("(k p) n -> p k n", p=128)
    xdst = dr['xo'] if last else dr['xown']
    xov = xdst.rearrange("(k p) n -> p k n", p=128)
    sfx = "_D%d" % l
    if True:
        sb = lambda name, shape, dt=BF16: es.enter_context(nc.sbuf_tensor(name + sfx, shape, dt))
        wo_s = sb("wo_s", [128, 8, 1024])
        wq_s = sb("wq_s", [128, 8, 1024])
        woo_s = sb("woo_s", [128, 8, 1024])
        wglu_s = sb("wglu_s", [128, 2, 256])
        kT_s = sb("kT_s", [128, 8, 256])
        V_s = sb("V_s", [128, 2, 1024])
        memT_s = sb("memT_s", [128, 8, 256])
        vec_s = sb("vec_s", [128, 50], F32)
        ones_m = sb("ones_m", [128, 128])
        ones_1 = sb("ones_1", [128, 128])
        xs = sb("xs", [128, 8, NT], F32)
        xb = sb("xb", [128, 8, NT])
        ys = sb("ys", [128, 8, NT])
        ysA = sb("ysA", [128, 8, NT])
        ysB = sb("ysB", [128, 8, NT])
        sel_s = sb("sel_s", [128, 2], F32)
        dmy = sb("dmy", [128, 2], F32)
        yd = sb("yd", [128, 2, NT])
        qs = sb("qs", [128, 8, NT])
        os_ = sb("os", [128, 8, NT])
        hid = sb("hid", [128, 22, NT])
        wst = sb("wst", [128, 3, 4096])
        wdst = sb("wdst", [128, 3, 1024])
        zb = sb("zb", [128, 8, NT])
        rstd = sb("rstd", [128, 2, NT], F32)
        pt = sb("pt", [128, 4, NT])
        rl = sb("rl", [128, 2, NT], F32)
        sg = sb("sg", [128, 2, NT], F32)
        psb = [es.enter_context(nc.psum_tensor("ps%d" % i + sfx, [128, 512], F32)) for i in range(8)]

        psA = Rot([(psb[i], ('ps', i)) for i in range(4)])
        psD = [(psb[4 + i], ('ps', 4 + i)) for i in range(4)]
        psAtt = Rot([(psb[i], ('ps', i)) for i in range(8)])
        evac_i = [0]

        P.op('dve', lambda e: e.memset(ones_m[:], 1.0 / 1024.0), writes=['ones_m'])
        P.op('dve', lambda e: e.memset(ones_1[:], 1.0), writes=['ones_1'])
        P.dma('sp', vec_s[:], vecs, writes=['vec'])
        P.dma('sp', sel_s[:], dr['sel'], writes=['sel'])
        P.dma('pool', memT_s[:], memT, writes=['memT'])
        P.dma('sp', wglu_s[:], w_glu, writes=['wglu'])
        for (dst, src, key) in ((wo_s, w_out, 'wo'), (wq_s, w_q, 'wq'), (woo_s, w_o, 'woo')):
            for h in range(2):
                P.dma('sp', dst[:, 4 * h:4 * h + 4, :], src[:, 4 * h:4 * h + 4, :], writes=[(key, h)])
        WO = [('wo', 0), ('wo', 1)]
        WQ = [('wq', 0), ('wq', 1)]
        WOO = [('woo', 0), ('woo', 1)]

        for pc in range(4):
            slot = pc % 2
            wv = wst[:, slot, :].rearrange("p (k n) -> p k n", k=8)
            P.dma('sp', wv, w_kv[pc], writes=[('wst', slot)])
            if pc < 2:
                for j in range(4):
                    ps, pk = psA.get()
                    for kc in range(8):
                        P.op('pe', lambda e, ps=ps, wv=wv, kc=kc, j=j: e.matmul(
                            ps[:, 0:256], wv[:, kc, j * 128:(j + 1) * 128], memT_s[:, kc, :],
                            start=(kc == 0), stop=(kc == 7)), reads=[('wst', slot), 'memT'], writes=[pk])
                    P.op('act', lambda e, ps=ps, mc=4 * pc + j: e.activation(
                        out=kT_s[:, mc, :], in_=ps[:, 0:256], func=AF.Copy), reads=[pk], writes=['kT'])
            else:
                for mm in range(2):
                    ps, pk = psA.get()
                    for kc in range(8):
                        P.op('pe', lambda e, ps=ps, wv=wv, kc=kc, mm=mm: e.matmul(
                            ps[:], memT_s[:, kc, mm * 128:(mm + 1) * 128], wv[:, kc, :],
                            start=(kc == 0), stop=(kc == 7)), reads=[('wst', slot), 'memT'], writes=[pk])
                    P.op('act', lambda e, ps=ps, mm=mm, c0=(pc - 2) * 512: e.activation(
                        out=V_s[:, mm, c0:c0 + 512], in_=ps[:], func=AF.Copy), reads=[pk], writes=['V'])

        def resid(ps, pk, mc):
            P.op('dve', lambda e: e.scalar_tensor_tensor(
                out=xs[:, mc, :], in0=xs[:, mc, :], scalar=ALPHA, in1=ps[:], op0=ALU.mult, op1=ALU.add),
                reads=[pk, ('xs', mc)], writes=[('xs', mc)])

        xalias = hid[:].rearrange("p a b -> p (a b)").bitcast(F32)[:, 0:8 * NT].rearrange("p (c n) -> p c n", n=NT)
        akeys = lambda c: [('hid', 2 * c), ('hid', 2 * c + 1)]

        def load_x(t):
            ts = slice(t * NT, (t + 1) * NT)
            for h in range(2):
                P.dma('sp', xs[:, 4 * h:4 * h + 4, :], xTv[:, 4 * h:4 * h + 4, ts],
                      writes=[('xs', c) for c in range(4 * h, 4 * h + 4)])

        def load_y(t):
            sA = (t * NT) // 2048
            sB = (TOK + t * NT) // 2048
            cs0 = (t * NT) % 2048
            for (dst, sidx, key) in ((ysA, sA, 'ysA'), (ysB, sB, 'ysB')):
                for r_ in range(2):
                    src = dr['yall'][l][sidx][r_ * 512:(r_ + 1) * 512, cs0:cs0 + NT].rearrange("(m p) n -> p m n", p=128)
                    dstv = dst[:].rearrange("p (m r) n -> p m r n", r=2)[:, :, r_, :]
                    P.dma('sp', dstv, src, reads=[('yall', sidx)], writes=[(key, r_)])
            for c in range(8):
                P.op('dve', lambda e, c=c: e.tensor_scalar(out=ys[:, c, :], in0=ysA[:, c, :], scalar1=sel_s[:, 0:1],
                                                          scalar2=None, op0=ALU.mult),
                     reads=[('ysA', c % 2), 'sel'], writes=[('ys', c)])
                P.op('dve', lambda e, c=c: e.scalar_tensor_tensor(
                    out=ys[:, c, :], in0=ysB[:, c, :], scalar=sel_s[:, 1:2], in1=ys[:, c, :], op0=ALU.mult, op1=ALU.add),
                    reads=[('ysB', c % 2), 'sel', ('ys', c)], writes=[('ys', c)])

        def layer_norm(gcol, bcol, final=False):
            for c in range(8):
                P.op('act', lambda e, c=c: e.activation(out=zb[:, c, :], in_=xs[:, c, :], func=AF.Copy),
                     reads=[('xs', c)], writes=[('zb', c)])
            P.op('act', lambda e: e.activation(out=dmy[:, 0:1], in_=sel_s[:, 0:1], func=AF.Sqrt), reads=['sel'], writes=['dmy'])
            pm, pmk = psA.get()
            for c in range(8):
                P.op('pe', lambda e, c=c: e.matmul(pm[:], ones_m[:], zb[:, c, :], start=(c == 0), stop=(c == 7)),
                     reads=['ones_m', ('zb', c)], writes=[pmk])
            for c in range(8):
                P.op('dve', lambda e, c=c: e.tensor_tensor(out=xs[:, c, :], in0=xs[:, c, :], in1=pm[:], op=ALU.subtract),
                     reads=[pmk, ('xs', c)], writes=[('xs', c)])
                P.op('act', lambda e, c=c: e.activation(out=zb[:, c, :], in_=xs[:, c, :], func=AF.Square),
                     reads=[('xs', c)], writes=[('zb', c)])
            pv, pvk = psA.get()
            for c in range(8):
                P.op('pe', lambda e, c=c: e.matmul(pv[:], ones_m[:], zb[:, c, :], start=(c == 0), stop=(c == 7)),
                     reads=['ones_m', ('zb', c)], writes=[pvk])
            P.op('act', lambda e: e.activation(out=rstd[:, 0, :], in_=pv[:], func=AF.Sqrt, bias=LN_EPS),
                 reads=[pvk], writes=[('rstd', 0)])
            P.op('dve', lambda e: e.reciprocal(out=rstd[:, 1, :], in_=rstd[:, 0, :]),
                 reads=[('rstd', 0)], writes=[('rstd', 1)])
            for c in range(8):
                P.op('dve', lambda e, c=c: e.tensor_tensor(out=xs[:, c, :], in0=xs[:, c, :], in1=rstd[:, 1, :], op=ALU.mult),
                     reads=[('rstd', 1), ('xs', c)], writes=[('xs', c)])
                dst = xalias if final else xs
                dk = akeys(c) if final else [('xs', c)]
                P.op('act', lambda e, c=c, dst=dst: e.activation(
                    out=dst[:, c, :], in_=xs[:, c, :], func=AF.Identity,
                    scale=vec_s[:, gcol + c:gcol + c + 1], bias=vec_s[:, bcol + c:bcol + c + 1]),
                    reads=['vec', ('xs', c)], writes=dk)
            for c in range(8):
                dst = xalias if final else xs
                dk = akeys(c) if final else [('xs', c)]
                P.op('dve', lambda e, c=c, dst=dst: e.tensor_copy(out=xb[:, c, :], in_=dst[:, c, :]),
                     reads=dk, writes=[('xb', c)])

        outs = []
        for t in range(ntile):
            ts = slice(t * NT, (t + 1) * NT)
            if t == 0:
                load_x(0)
                load_y(0)
            for mc in range(2):
                ps, pk = psA.get()
                for kc in range(2):
                    P.op('pe', lambda e, ps=ps, kc=kc, mc=mc: e.matmul(
                        ps[:], wglu_s[:, kc, mc * 128:(mc + 1) * 128], ys[:, 6 + kc, :],
                        start=(kc == 0), stop=(kc == 1)), reads=['wglu', ('ys', 6), ('ys', 7)], writes=[pk])
                P.op('act', lambda e, ps=ps, mc=mc: e.activation(
                    out=sg[:, mc, :], in_=ps[:], func=AF.Sigmoid, bias=vec_s[:, mc:mc + 1]),
                    reads=[pk, 'vec'], writes=[('sg', mc)])
                P.op('dve', lambda e, mc=mc: e.tensor_tensor(
                    out=yd[:, mc, :], in0=ys[:, 6 + mc, :], in1=sg[:, mc, :], op=ALU.mult),
                    reads=[('sg', mc), ('ys', 6 + mc)], writes=[('yd', mc)])
            for mc in range(8):
                ps, pk = psA.get()
                for kc in range(8):
                    rhs = ys[:, kc, :] if kc < 6 else yd[:, kc - 6, :]
                    rk = ('ys', kc) if kc < 6 else ('yd', kc - 6)
                    P.op('pe', lambda e, ps=ps, kc=kc, mc=mc, rhs=rhs: e.matmul(
                        ps[:], wo_s[:, kc, mc * 128:(mc + 1) * 128], rhs, start=(kc == 0), stop=(kc == 7)),
                        reads=WO + [rk], writes=[pk])
                resid(ps, pk, mc)
            if t + 1 < ntile:
                load_y(t + 1)
            layer_norm(2, 10)
            for mc in range(8):
                ps, pk = psA.get()
                for kc in range(8):
                    P.op('pe', lambda e, ps=ps, kc=kc, mc=mc: e.matmul(
                        ps[:], wq_s[:, kc, mc * 128:(mc + 1) * 128], xb[:, kc, :], start=(kc == 0), stop=(kc == 7)),
                        reads=WQ + [('xb', kc)], writes=[pk])
                P.op('act', lambda e, ps=ps, mc=mc: e.activation(
                    out=qs[:, mc, :], in_=ps[:], func=AF.Copy, scale=1.0 / 16.0), reads=[pk], writes=[('qs', mc)])
            def att_head(h):
                for mm in range(2):
                    ps, pk = psAtt.get()
                    for dc in range(2):
                        P.op('pe', lambda e, ps=ps, dc=dc, mm=mm: e.matmul(
                            ps[:], kT_s[:, 2 * h + dc, mm * 128:(mm + 1) * 128], qs[:, 2 * h + dc, :],
                            start=(dc == 0), stop=(dc == 1)), reads=['kT', ('qs', 2 * h + dc)], writes=[pk])
                    pslot = (h % 2) * 2 + mm
                    P.op('act', lambda e, ps=ps, pslot=pslot: e.activation(
                        out=pt[:, pslot, :], in_=ps[:], func=AF.Exp), reads=[pk], writes=[('pt', pslot)])
                yield
                pl, plk = psAtt.get()
                for mm in range(2):
                    P.op('pe', lambda e, mm=mm: e.matmul(
                        pl[:], ones_1[:], pt[:, (h % 2) * 2 + mm, :], start=(mm == 0), stop=(mm == 1)),
                        reads=['ones_1', ('pt', (h % 2) * 2 + mm)], writes=[plk])
                P.op('dve', lambda e: e.reciprocal(out=rl[:, h % 2, :], in_=pl[:]),
                     reads=[plk], writes=[('rl', h % 2)])
                for dc in range(2):
                    ps, pk = psAtt.get()
                    for mm in range(2):
                        P.op('pe', lambda e, ps=ps, dc=dc, mm=mm: e.matmul(
                            ps[:], V_s[:, mm, h * 256 + dc * 128:h * 256 + (dc + 1) * 128], pt[:, (h % 2) * 2 + mm, :],
                            start=(mm == 0), stop=(mm == 1)), reads=['V', ('pt', (h % 2) * 2 + mm)], writes=[pk])
                    P.op('dve', lambda e, ps=ps, dc=dc: e.tensor_tensor(
                        out=os_[:, 2 * h + dc, :], in0=ps[:], in1=rl[:, h % 2, :], op=ALU.mult),
                        reads=[pk, ('rl', h % 2)], writes=[('os', 2 * h + dc)])
                yield

            prev = None
            for h in range(4):
                g = att_head(h)
                next(g)
                if prev is not None:
                    next(prev, None)
                prev = g
            next(prev, None)
            for mc in range(8):
                ps, pk = psA.get()
                for kc in range(8):
                    P.op('pe', lambda e, ps=ps, kc=kc, mc=mc: e.matmul(
                        ps[:], woo_s[:, kc, mc * 128:(mc + 1) * 128], os_[:, kc, :], start=(kc == 0), stop=(kc == 7)),
                        reads=WOO + [('os', kc)], writes=[pk])
                resid(ps, pk, mc)
            layer_norm(18, 26)
            for jp in range(11):
                slot = jp % 3
                wv = wst[:, slot, :].rearrange("p (g k n) -> p g k n", g=2, k=8)
                P.dma('pool', wv, w_gu[jp], writes=[('wst', slot)])
                for jj in range(2):
                    j = 2 * jp + jj
                    pg, pgk = psA.get()
                    pu, puk = psA.get()
                    for (pp, ppk, g) in ((pg, pgk, 0), (pu, puk, 1)):
                        for kc in range(8):
                            P.op('pe', lambda e, pp=pp, g=g, kc=kc, jj=jj, wv=wv: e.matmul(
                                pp[:], wv[:, g, kc, jj * 128:(jj + 1) * 128], xb[:, kc, :],
                                start=(kc == 0), stop=(kc == 7)), reads=[('wst', slot), ('xb', kc)], writes=[ppk])
                    P.op('act', lambda e, pg=pg, jj=jj: e.activation(out=sg[:, jj, :], in_=pg[:], func=AF.Silu),
                         reads=[pgk], writes=[('sg', jj)])
                    P.op('dve', lambda e, pu=pu, jj=jj, j=j: e.tensor_tensor(
                        out=hid[:, j, :], in0=pu[:], in1=sg[:, jj, :], op=ALU.mult),
                        reads=[puk, ('sg', jj)], writes=[('hid', j)])
            for sw in range(2):
                for jp in range(11):
                    slot = (sw * 11 + jp) % 3
                    wv = wdst[:, slot, :].rearrange("p (j n) -> p j n", j=2)
                    P.dma('pool', wv, w_d[sw, jp], writes=[('wdst', slot)])
                    for jj in range(2):
                        j = 2 * jp + jj
                        for q in range(4):
                            ps, pk = psD[q]
                            P.op('pe', lambda e, ps=ps, wv=wv, jj=jj, q=q, j=j: e.matmul(
                                ps[:], wv[:, jj, q * 128:(q + 1) * 128], hid[:, j, :],
                                start=(j == 0), stop=(j == 21)), reads=[('wdst', slot), ('hid', j)], writes=[pk])
                for q in range(4):
                    ps, pk = psD[q]
                    resid(ps, pk, sw * 4 + q)
            layer_norm(34, 42, final=True)
            if t + 1 < ntile:
                load_x(t + 1)
            for h in range(2):
                P.dma('sp', xov[:, 4 * h:4 * h + 4, ts], xalias[:, 4 * h:4 * h + 4, :],
                      reads=[k for c in range(4 * h, 4 * h + 4) for k in akeys(c)], writes=['xdst'])
            if not last:
                q = (t * NT) // 1024
                c1 = (t * NT) % 1024
                P.dma('sp', dr['xbf'][q].rearrange("(k p) n -> p k n", p=128)[:, :, c1:c1 + NT], xb[:],
                      reads=[('xb', c) for c in range(8)], writes=[('xbf', q)])
                if c1 + NT == 1024:
                    P.op('pool', lambda e, q=q: e.collective_compute(
                        "AllGather", ALU.bypass, replica_groups=PAIRS, ins=[dr['xbf'][q]], outs=[dr['xall'][q]]),
                        reads=[('xbf', q)], writes=[('xall', q)], coll=True)


def prep_D_weights(inp, l):
    f = np.float32
    kp = lambda w: np.ascontiguousarray(w.reshape(8, 128, -1).transpose(1, 0, 2))
    r = {}
    r['w_out'] = kp(inp['w_mix_out'][l])
    r['w_q'] = kp(inp['w_mem_q'][l])
    r['w_o'] = kp(inp['w_mem_o'][l])
    wkv = kp(inp['w_mem_kv'][l])
    r['w_kv'] = np.ascontiguousarray(wkv.reshape(128, 8, 4, 512).transpose(2, 0, 1, 3))
    r['w_glu'] = np.ascontiguousarray(inp['s5_w_glu'][l].reshape(2, 128, 256).transpose(1, 0, 2))
    g = kp(inp['w_ff_gate'][l]).reshape(128, 8, 11, 256)
    u = kp(inp['w_ff_up'][l]).reshape(128, 8, 11, 256)
    gu = np.stack([g, u], axis=1)
    r['w_gu'] = np.ascontiguousarray(gu.transpose(3, 0, 1, 2, 4))
    wd = inp['w_ff_down'][l].reshape(11, 2, 128, 2, 512)
    r['w_d'] = np.ascontiguousarray(wd.transpose(3, 0, 2, 1, 4))
    cols = [inp['s5_b_glu'][l].reshape(2, 128)]
    for nm in ('ln_mix_g', 'ln_mix_b', 'ln_mem_g', 'ln_mem_b', 'ln_ff_g', 'ln_ff_b'):
        cols.append(inp[nm][l].reshape(8, 128))
    r['vecs'] = np.ascontiguousarray(np.concatenate(cols, axis=0).T.astype(f))
    return r


PAIRS = [[0, 1], [2, 3], [4, 5], [6, 7]]

DBG = ''

SB = 2048
NW = 11 * 128 + 16


PAIRS = [[0, 1], [2, 3], [4, 5], [6, 7]]


def emit_A(nc, P, es, l, dr, T, stages='sABCD'):
    nsb = T // SB
    half = T // 2
    w_in, s5p, s5b, s5c, dvec = dr['w_inA'][l], dr['s5p'][l], dr['s5b'][l], dr['s5c'][l], dr['dvec'][l]
    wgate, bgate, gains = dr['wgate'][l], dr['bgate'][l], dr['gains'][l]
    abias, rtab, gmask, ident, blk, kp1, rmask = (dr[k] for k in ('abias', 'rtab', 'gmask', 'ident', 'blk', 'kp1', 'rmask'))
    xTv = dr['xT_full'].rearrange("(k p) n -> p k n", p=128)
    sfx = "_A%d" % l
    if True:
        sb = lambda name, shape, dt=BF16: es.enter_context(nc.sbuf_tensor(name + sfx, shape, dt))
        w_s = sb("w_s", [128, 8, NW])
        xsb = sb("xsb", [128, 8, SB])
        G = [sb("G%d" % i, [128, SB]) for i in range(6)]
        Y = [sb("Y%d" % i, [128, SB]) for i in range(4)]
        F = [sb("F%d" % i, [128, SB], F32) for i in range(4)]
        kA = sb("kA", [128, 2 * SB])
        vA = sb("vA", [128, 2 * SB])
        abias_s = sb("abias_s", [128, 12, 128])
        rtab_s = sb("rtab_s", [128, 5, 128], F32)
        gmask_s = sb("gmask_s", [128, 2, 128], F32)
        ident_s = sb("ident_s", [128, 128])
        identf_s = sb("identf_s", [128, 128], F32)
        blk_s = sb("blk_s", [128, 128])
        ones_s = sb("ones_s", [128, 64])
        kp1_s = sb("kp1_s", [128, 512], F32)
        rmask_s = sb("rmask_s", [128, SB])
        gains_s = sb("gains_s", [128, 2], F32)
        wgate_s = sb("wgate_s", [16, 64])
        bgate_s = sb("bgate_s", [64, 2], F32)
        dvec_s = sb("dvec_s", [128, 1], F32)
        ptA = sb("ptA", [128, 2, 4, 128])
        vblk = sb("vblk", [128, 2, 2, 128])
        ptB = sb("ptB", [128, 2, 2, 128])
        vtok = sb("vtok", [128, 2, 128])
        ktok = sb("ktok", [128, 2, 128])
        qdec = sb("qdec", [128, 2, 128])
        Rr = sb("Rr", [128, 64], F32)
        Rrb = sb("Rrb", [128, 2, 64])
        Rg = sb("Rg", [64, 64], F32)
        Rgb = sb("Rgb", [64, 2, 128])
        ebl = sb("ebl", [64, SB // 64], F32)
        ob = sb("ob", [128, 512])
        s5p_s = sb("s5p_s", [128, 4, 3], F32)
        s5b_s = sb("s5b_s", [128, 4, 2, 16], F32)
        s5c_s = sb("s5c_s", [128, 4, 2, 16], F32)
        sm = sb("sm", [128, 24, 4], F32)
        bb = sb("bb", [128, 4, 2, 16], F32)
        Zf = sb("Zf", [128, 128], F32)
        ZT = sb("ZT", [128, 4, 2, 128])
        CT = sb("CT", [128, 4, 2, 128])
        cosT = sb("cosT", [128, 4, 512], F32)
        sinT = sb("sinT", [128, 4, 512], F32)
        tb = [sb("tb%d" % i, [128, 512], F32) for i in range(2)]
        tt = [sb("tt%d" % i, [128, 512], F32) for i in range(4)]
        tg = sb("tg", [128, 512], F32)
        xri = sb("xri", [128, 2, 2, 512])
        xend = sb("xend", [128, 4, 2], F32)
        etmp = sb("etmp", [128, 4], F32)
        psb = [es.enter_context(nc.psum_tensor("ps%d" % i + sfx, [128, 512], F32)) for i in range(4)]
        pacc = es.enter_context(nc.psum_tensor("pacc" + sfx, [128, 512], F32))
        pst = [es.enter_context(nc.psum_tensor("pst%d" % i + sfx, [128, 8, 128], BF16)) for i in range(2)]
        psu = es.enter_context(nc.psum_tensor("psu" + sfx, [128, 512], F32))

        taps = []

        def tap(name, ap, keys, dt=BF16):
            if 'tap' not in DBG:
                return
            shp = list(ap.shape)
            t = nc.dram_tensor("tap_" + name, shp, dt, kind="ExternalOutput").ap()
            taps.append(P.dma('sp', t, ap, reads=keys))
        psA = Rot([(psb[i], ("ps", i)) for i in range(4)])
        psT = Rot([(pst[i], ('pst', i)) for i in range(2)])
        psAtt = Rot([(psb[i], ('ps', i)) for i in range(3)] + [(pacc, 'pacc')])
        ablk = [0]
        PI = math.pi

        P.dma('pool', w_s[:, 0:4, :], w_in[:, 0:4, :], writes=[('w', 0)])
        P.dma('pool', w_s[:, 4:8, :], w_in[:, 4:8, :], writes=[('w', 1)])
        WK = [('w', 0), ('w', 1)]
        P.dma('pool', abias_s[:], abias, writes=['abias'])
        P.dma('sp', rtab_s[:], rtab, writes=['rtab'])
        P.dma('sp', gmask_s[:], gmask, writes=['gmask'])
        P.dma('pool', ident_s[:], ident, writes=['ident'])
        P.dma('sp', identf_s[:], ident, writes=['identf'])
        P.dma('pool', blk_s[:], blk, writes=['blk'])
        P.dma('sp', kp1_s[:], kp1, writes=['kp1'])
        P.dma('pool', rmask_s[:], rmask, writes=['rmask'])
        P.dma('sp', gains_s[:], gains, writes=['gains'])
        P.dma('pool', wgate_s[:], wgate, writes=['wgate'])
        P.dma('sp', bgate_s[:, 0:1], bgate, writes=['bgate'])
        P.dma('sp', dvec_s[:], dvec, writes=['dvec'])
        P.dma('sp', s5p_s[:], s5p, writes=['s5p'])
        P.dma('sp', s5b_s[:], s5b, writes=['s5b'])
        P.dma('sp', s5c_s[:], s5c, writes=['s5c'])
        P.op('dve', lambda e: e.memset(ones_s[:], 1.0), writes=['ones'])
        P.op('dve', lambda e: e.tensor_scalar(out=bgate_s[:, 1:2], in0=bgate_s[:, 0:1], scalar1=-1.0, scalar2=None,
                                              op0=ALU.mult), reads=['bgate'], writes=['nbgate'])
        P.op('dve', lambda e: e.memset(Rr[:], 0.0), writes=['Rr'])
        P.op('dve', lambda e: e.memset(Rrb[:], 0.0), writes=[('Rrb', 0), ('Rrb', 1)])
        P.op('dve', lambda e: e.memset(Rg[:], 0.0), writes=['Rg'])
        P.op('dve', lambda e: e.memset(Rgb[:], 0.0), writes=[('Rgb', 0), ('Rgb', 1)])
        P.op('dve', lambda e: e.memset(xend[:], 0.0), writes=['xend'])

        P.enabled = 's' in stages
        SMK = 'sm'
        smc = lambda i: sm[:, i, :]

        def dv(fn, reads=(), writes=()):
            P.op('dve', fn, reads=list(reads) + [SMK], writes=list(writes) + [SMK])

        def av(fn, reads=(), writes=()):
            P.op('act', fn, reads=list(reads) + [SMK], writes=list(writes) + [SMK])

        def tsc(out, in0, s1, op0, s2=None, op1=None):
            if op1 is None:
                return lambda e: e.tensor_scalar(out=out, in0=in0, scalar1=s1, scalar2=None, op0=op0)
            return lambda e: e.tensor_scalar(out=out, in0=in0, scalar1=s1, scalar2=s2, op0=op0, op1=op1)

        def tten(out, a, b, op):
            return lambda e: e.tensor_tensor(out=out, in0=a, in1=b, op=op)

        def reduce_turns(dst, src, tmp, wr=dv):
            wr(lambda e: e.tensor_copy(out=tmp.bitcast(I32), in_=src))
            wr(lambda e: e.tensor_copy(out=dst, in_=tmp.bitcast(I32)))
            wr(tten(dst, src, dst, ALU.subtract))
            wr(tsc(tmp, dst, 0.5, ALU.is_gt))
            wr(tten(dst, dst, tmp, ALU.subtract))
            wr(tsc(tmp, dst, -0.5, ALU.is_lt))
            wr(tten(dst, dst, tmp, ALU.add))

        are, aim, ldt = s5p_s[:, :, 0], s5p_s[:, :, 1], s5p_s[:, :, 2]
        DT, RHO, THT, TF, TMP, TF2, SIN, COS, NR, AI, DEN, FRE, FIM, T1, T2 = [smc(i) for i in range(15)]
        av(lambda e: e.activation(out=DT, in_=ldt, func=AF.Exp), reads=['s5p'])
        dv(tten(T1, are, DT, ALU.mult), reads=['s5p'])
        av(lambda e: e.activation(out=RHO, in_=T1, func=AF.Exp))
        dv(tten(THT, aim, DT, ALU.mult), reads=['s5p'])
        dv(tsc(T2, THT, 1.0 / (2 * PI), ALU.mult))
        reduce_turns(TF, T2, TMP)
        av(lambda e: e.activation(out=SIN, in_=TF, func=AF.Sin, scale=2 * PI))
        dv(tsc(T2, T2, 0.25, ALU.add))
        reduce_turns(TF2, T2, TMP)
        av(lambda e: e.activation(out=COS, in_=TF2, func=AF.Sin, scale=2 * PI))
        dv(tten(NR, RHO, COS, ALU.mult))
        dv(tsc(NR, NR, -1.0, ALU.add))
        dv(tten(AI, RHO, SIN, ALU.mult))
        dv(tten(DEN, are, are, ALU.mult), reads=['s5p'])
        dv(tten(T1, aim, aim, ALU.mult), reads=['s5p'])
        dv(tten(DEN, DEN, T1, ALU.add))
        dv(lambda e: e.reciprocal(out=DEN, in_=DEN))
        dv(tten(FRE, NR, are, ALU.mult), reads=['s5p'])
        dv(tten(T1, AI, aim, ALU.mult), reads=['s5p'])
        dv(tten(FRE, FRE, T1, ALU.add))
        dv(tten(FRE, FRE, DEN, ALU.mult))
        dv(tten(FIM, AI, are, ALU.mult), reads=['s5p'])
        dv(tten(T1, NR, aim, ALU.mult), reads=['s5p'])
        dv(tten(FIM, FIM, T1, ALU.subtract))
        dv(tten(FIM, FIM, DEN, ALU.mult))
        for k in range(4):
            bre, bim = s5b_s[:, k, 0, :], s5b_s[:, k, 1, :]
            fre, fim = sm[:, 11, k:k + 1], sm[:, 12, k:k + 1]
            P.op('dve', tsc(bb[:, k, 0, :], bim, fim, ALU.mult), reads=['s5b', SMK], writes=[('bb', k)])
            P.op('dve', lambda e, k=k, bre=bre, fre=fre: e.scalar_tensor_tensor(
                out=bb[:, k, 0, :], in0=bre, scalar=fre, in1=bb[:, k, 0, :], op0=ALU.mult, op1=ALU.subtract),
                reads=['s5b', SMK, ('bb', k)], writes=[('bb', k)])
            P.op('dve', tsc(bb[:, k, 1, :], bre, fim, ALU.mult), reads=['s5b', SMK, ('bb', k)], writes=[('bb', k)])
            P.op('dve', lambda e, k=k, bim=bim, fre=fre: e.scalar_tensor_tensor(
                out=bb[:, k, 1, :], in0=bim, scalar=fre, in1=bb[:, k, 1, :], op0=ALU.mult, op1=ALU.add),
                reads=['s5b', SMK, ('bb', k)], writes=[('bb', k)])
            for ri in range(2):
                P.op('dve', lambda e: e.memset(Zf[:], 0.0), writes=['Zf'])
                for g2 in range(2):
                    c0 = 16 * (2 * k + g2)
                    P.op('dve', lambda e, k=k, ri=ri, g2=g2, c0=c0: e.tensor_copy(
                        out=Zf[64 * g2:64 * g2 + 64, c0:c0 + 16], in_=bb[64 * g2:64 * g2 + 64, k, ri, :]),
                        reads=[('bb', k), 'Zf'], writes=['Zf'])
                P.op('pe', lambda e: e.transpose(out=psu[:, 0:128], in_=Zf[:], identity=identf_s[:]),
                     reads=['Zf', 'identf'], writes=['psu'])
                P.op('act', lambda e, k=k, ri=ri: e.activation(out=ZT[:, k, ri, :], in_=psu[:, 0:128], func=AF.Copy),
                     reads=['psu'], writes=['ZT'])
                P.op('dve', lambda e, k=k, ri=ri: e.memset(CT[:, k, ri, :], 0.0), reads=['CT'], writes=['CT'])
                for g2 in range(2):
                    c0 = 16 * (2 * k + g2)
                    P.op('dve', tsc(CT[64 * g2:64 * g2 + 64, k, ri, c0:c0 + 16], s5c_s[64 * g2:64 * g2 + 64, k, ri, :],
                                    1.0 if ri == 0 else -1.0, ALU.mult), reads=['s5c', 'CT'], writes=['CT'])
            tf = sm[:, 3, k:k + 1]
            P.op('dve', tsc(tt[0][:], kp1_s[:], tf, ALU.mult), reads=['kp1', SMK], writes=['tt0'])
            wr = lambda fn: P.op('dve', fn, reads=['tt0', 'tt1', 'tt2'], writes=['tt0', 'tt1', 'tt2'])
            reduce_turns(tt[1][:], tt[0][:], tt[2][:], wr=wr)
            P.op('act', lambda e, k=k: e.activation(out=sinT[:, k, :], in_=tt[1][:], func=AF.Sin, scale=2 * PI),
                 reads=['tt1'], writes=['sinT'])
            wr(tsc(tt[0][:], tt[0][:], 0.25, ALU.add))
            reduce_turns(tt[1][:], tt[0][:], tt[2][:], wr=wr)
            P.op('act', lambda e, k=k: e.activation(out=cosT[:, k, :], in_=tt[1][:], func=AF.Sin, scale=2 * PI),
                 reads=['tt1'], writes=['cosT'])

        P.enabled = True
        def project(col0, width, evac):
            for nt in range(SB // 512):
                ps, pk = psA.get()
                for kc in range(8):
                    P.op('pe', lambda e, ps=ps, kc=kc, nt=nt: e.matmul(
                        ps[0:width, :], w_s[:, kc, col0:col0 + width], xsb[:, kc, nt * 512:(nt + 1) * 512],
                        start=(kc == 0), stop=(kc == 7)), reads=WK + ['xsb'], writes=[pk])
                evac(ps, pk, nt)

        def evac_copy(dst, key, rows=128, scale=1.0, func=AF.Copy, col0=0):
            def f(ps, pk, nt):
                P.op('act', lambda e: e.activation(
                    out=dst[0:rows, col0 + nt * 512:col0 + (nt + 1) * 512], in_=ps[0:rows, :], func=func, scale=scale),
                    reads=[pk], writes=[key])
            return f

        hn_done = []

        def head_norm(po, pok, gcol, gate, gatek, ydst, ykey, c0):
            P.op('act', lambda e: e.activation(out=tt[2][:], in_=po[:], func=AF.Copy), reads=[pok], writes=['tt2'])
            if c0 == 0 and not hn_done:
                hn_done.append(1)
                tap('of', tt[2][:], ['tt2'], F32)
            P.op('dve', lambda e: e.tensor_copy(out=ob[:], in_=po[:]), reads=[pok], writes=['ob'])
            pm, pmk = psA.get()
            P.op('pe', lambda e: e.matmul(pm[:], blk_s[:], ob[:], start=True, stop=True),
                 reads=['blk', 'ob'], writes=[pmk])
            P.op('dve', tten(tt[2][:], tt[2][:], pm[:], ALU.subtract), reads=[pmk, 'tt2'], writes=['tt2'])
            P.op('act', lambda e: e.activation(out=ob[:], in_=tt[2][:], func=AF.Square), reads=['tt2'], writes=['ob'])
            pv, pvk = psA.get()
            P.op('pe', lambda e: e.matmul(pv[:], blk_s[:], ob[:], start=True, stop=True),
                 reads=['blk', 'ob'], writes=[pvk])
            P.op('act', lambda e: e.activation(out=tt[0][:], in_=pv[:], func=AF.Sqrt, bias=1e-5),
                 reads=[pvk], writes=['tt0'])
            P.op('dve', lambda e: e.reciprocal(out=tt[1][:], in_=tt[0][:]), reads=['tt0'], writes=['tt1'])
            P.op('dve', tten(tt[2][:], tt[2][:], tt[1][:], ALU.mult), reads=['tt1', 'tt2'], writes=['tt2'])
            P.op('dve', lambda e: e.scalar_tensor_tensor(
                out=ydst[:, c0:c0 + 512], in0=tt[2][:], scalar=gains_s[:, gcol:gcol + 1], in1=gate[:, c0:c0 + 512],
                op0=ALU.mult, op1=ALU.mult), reads=['tt2', 'gains', gatek], writes=[ykey])

        outs = []
        for s in range(nsb):
            t0 = s * SB
            ring = (s % 2) * SB
            pring = ((s - 1) % 2) * SB
            if l == 0:
                P.dma('pool', xsb[:, 0:4, :], xTv[:, 0:4, t0:t0 + SB], writes=['xsb'])
                P.dma('pool', xsb[:, 4:8, :], xTv[:, 4:8, t0:t0 + SB], reads=['xsb'], writes=['xsb'])
            else:
                r_ = t0 // half
                q0 = (t0 - r_ * half) // 1024
                for qq in range(2):
                    src = dr['xall'][q0 + qq][r_ * 1024:(r_ + 1) * 1024, :].rearrange("(k p) n -> p k n", p=128)
                    P.dma('sp', xsb[:, :, qq * 1024:(qq + 1) * 1024], src, reads=[('xall', q0 + qq), 'xsb'], writes=['xsb'])

            for (dst, src) in dr['wconv'][l][s::nsb]:
                P.dma('pool', dst, src, writes=['wconv'])
            P.enabled = 'A' in stages
            qA = G[0]
            project(0, 128, evac_copy(qA, 'G0'))
            project(128, 128, evac_copy(kA, 'kA', scale=0.125, col0=ring))
            project(256, 128, evac_copy(vA, 'vA', col0=ring))
            accO, accL = F[0], F[1]
            P.op('dve', lambda e: e.memset(accO[:], 0.0), writes=['F0'])
            P.op('dve', lambda e: e.memset(accL[:], 0.0), writes=['F1'])
            def blk(bi, r, n, c):
                nloc = 16 // r
                base = n * 128 * r + c
                has_prev = (n > 0) or (s > 0)
                kbs = [0, 1] if has_prev else [1]
                if n > 0:
                    pbase = ring + base - 128 * r
                else:
                    pbase = pring + (nloc - 1) * 128 * r + c
                cbase = ring + base
                kcol = {0: pbase, 1: cbase}
                sl = lambda b0: slice(b0, b0 + 127 * r + 1, r)
                pT, pTk = psT.get()
                vs = (psT.i - 1) % 2
                for kb in kbs:
                    P.op('pe', lambda e, pT=pT, kb=kb, cc=kcol[kb], r=r: e.transpose(
                        out=pT[:, kb, :], in_=vA[:, slice(cc, cc + 127 * r + 1, r)], identity=ident_s[:]),
                        reads=['vA', 'ident'], writes=[pTk])
                P.op('act', lambda e, pT=pT, vs=vs, k0=kbs[0]: e.activation(
                    out=vblk[:, vs, k0:2, :], in_=pT[:, k0:2, :], func=AF.Copy),
                    reads=[pTk], writes=[('vblk', vs)])
                ps, pk = psAtt.get()
                psv = ps[:].rearrange("p (h k q) -> p h k q", h=2, k=2)
                for h in range(2):
                    for kb in kbs:
                        P.op('pe', lambda e, psv=psv, h=h, kb=kb, cc=kcol[kb], r=r, base=base: e.matmul(
                            psv[:, h, kb, :], kA[64 * h:64 * h + 64, slice(cc, cc + 127 * r + 1, r)],
                            qA[64 * h:64 * h + 64, slice(base, base + 127 * r + 1, r)], start=True, stop=False),
                            reads=['kA', 'G0'], writes=[pk])
                        P.op('pe', lambda e, psv=psv, h=h, kb=kb, bi=bi: e.matmul(
                            psv[:, h, kb, :], ident_s[:], abias_s[:, bi * 4 + h * 2 + kb, :], start=False, stop=True),
                            reads=['ident', 'abias'], writes=[pk])
                pa = ablk[0] % 2
                ablk[0] += 1
                ptv = ptA[:, pa, :, :].rearrange("p (h k) q -> p h k q", h=2)
                P.op('act', lambda e, psv=psv, ptv=ptv, k0=kbs[0]: e.activation(
                    out=ptv[:, :, k0:2, :], in_=psv[:, :, k0:2, :], func=AF.Exp),
                    reads=[pk], writes=[('ptA', pa)])
                yield
                po, pok = psAtt.get()
                for h in range(2):
                    for idx, kb in enumerate(kbs):
                        P.op('pe', lambda e, po=po, h=h, kb=kb, vs=vs, ptv=ptv, st=(idx == 0), sp=(kb == 1): e.matmul(
                            po[64 * h:64 * h + 64, 0:128], vblk[:, vs, kb, 64 * h:64 * h + 64], ptv[:, h, kb, :],
                            start=st, stop=sp), reads=[('vblk', vs), ('ptA', pa)], writes=[pok])
                    for idx, kb in enumerate(kbs):
                        P.op('pe', lambda e, po=po, h=h, kb=kb, ptv=ptv, st=(idx == 0), sp=(kb == 1): e.matmul(
                            po[64 * h:64 * h + 64, 128:256], ones_s[:, :], ptv[:, h, kb, :],
                            start=st, stop=sp), reads=['ones', ('ptA', pa)], writes=[pok])
                osl = slice(base, base + 127 * r + 1, r)
                P.op('dve', lambda e, po=po, osl=osl: e.tensor_tensor(
                    out=accO[:, osl], in0=accO[:, osl], in1=po[:, 0:128], op=ALU.add),
                    reads=[pok, 'F0'], writes=['F0'])
                P.op('dve', lambda e, po=po, osl=osl: e.tensor_tensor(
                    out=accL[:, osl], in0=accL[:, osl], in1=po[:, 128:256], op=ALU.add),
                    reads=[pok, 'F1'], writes=['F1'])

            def attn_steps():
                prev = None
                for bi, r in enumerate((1, 4, 16)):
                    for n in range(16 // r):
                        for c in range(r):
                            g = blk(bi, r, n, c)
                            next(g)
                            if prev is not None:
                                next(prev, None)
                            prev = g
                            yield
                next(prev, None)
                yield

            uD = G[5]
            project(1280, 128, evac_copy(uD, 'G5'))
            PS5, PS5K = psb[3], ('ps', 3)

            def s5_front(u):
                seg, k = divmod(u, 4)
                ss = slice(seg * 512, (seg + 1) * 512)
                for ri in range(2):
                    P.op('pe', lambda e, ri=ri: e.matmul(PS5[:], ZT[:, k, ri, :], uD[:, ss], start=True, stop=True),
                         reads=['ZT', 'G5'], writes=[PS5K])
                    P.op('act', lambda e, ri=ri: e.activation(out=tb[ri][:], in_=PS5[:], func=AF.Copy),
                         reads=[PS5K], writes=['tb%d' % ri])

            def s5_dve(u):
                seg, k = divmod(u, 4)
                cs_, sn_ = cosT[:, k, :], sinT[:, k, :]
                T0, T1_, T2_, T3_ = tt[0][:], tt[1][:], tt[2][:], tt[3][:]
                br, bi_ = tb[0][:], tb[1][:]
                rho_k = sm[:, 1, k:k + 1].to_broadcast([128, 512])
                xs_ = k % 2
                P.op('dve', tten(T0, br, cs_, ALU.mult), reads=['tb0', 'cosT'], writes=['tt0'])
                P.op('dve', tten(T1_, bi_, sn_, ALU.mult), reads=['tb1', 'sinT'], writes=['tt1'])
                P.op('dve', tten(T0, T0, T1_, ALU.add), reads=['tt0', 'tt1'], writes=['tt0'])
                P.op('dve', tten(T1_, bi_, cs_, ALU.mult), reads=['tb1', 'cosT', 'tt0'], writes=['tt1'])
                P.op('dve', tten(T2_, br, sn_, ALU.mult), reads=['tb0', 'sinT'], writes=['tt2'])
                P.op('dve', tten(T1_, T1_, T2_, ALU.subtract), reads=['tt1', 'tt2'], writes=['tt1'])
                P.op('dve', lambda e: e.tensor_tensor_scan(
                    out=tt[2][:], data0=rho_k, data1=tt[0][:], initial=xend[:, k, 0:1],
                    op0=ALU.mult, op1=ALU.add), reads=['tt0', SMK, 'xend'], writes=['tt2'])
                P.op('dve', lambda e: e.tensor_tensor_scan(
                    out=tt[3][:], data0=rho_k, data1=tt[1][:], initial=xend[:, k, 1:2],
                    op0=ALU.mult, op1=ALU.add), reads=['tt1', SMK, 'xend'], writes=['tt3'])
                P.op('dve', tten(T0, T2_, cs_, ALU.mult), reads=['tt2', 'cosT'], writes=['tt0'])
                P.op('dve', tten(T1_, T3_, sn_, ALU.mult), reads=['tt3', 'sinT'], writes=['tt1'])
                P.op('dve', tten(xri[:, xs_, 0, :], T0, T1_, ALU.subtract), reads=['tt0', 'tt1'], writes=[('xri', xs_, 0)])
                P.op('dve', tten(etmp[:, 0:1], T0[:, 511:512], T1_[:, 511:512], ALU.subtract),
                     reads=['tt0', 'tt1'], writes=['etmp'])
                P.op('dve', tten(T0, T2_, sn_, ALU.mult), reads=['tt2', 'sinT', ('xri', xs_, 0), 'etmp'], writes=['tt0'])
                P.op('dve', tten(T1_, T3_, cs_, ALU.mult), reads=['tt3', 'cosT', ('xri', xs_, 0), 'etmp'], writes=['tt1'])
                P.op('dve', tten(xri[:, xs_, 1, :], T0, T1_, ALU.add), reads=['tt0', 'tt1'], writes=[('xri', xs_, 1)])
                P.op('dve', tten(xend[:, k, 1:2], T0[:, 511:512], T1_[:, 511:512], ALU.add),
                     reads=['tt0', 'tt1', 'xend'], writes=['xend'])
                P.op('dve', lambda e: e.tensor_copy(out=xend[:, k, 0:1], in_=etmp[:, 0:1]),
                     reads=['etmp', 'xend'], writes=['xend'])

            def s5_back(u):
                seg, k = divmod(u, 4)
                ss = slice(seg * 512, (seg + 1) * 512)
                xs_ = k % 2
                for ri in range(2):
                    P.op('pe', lambda e, ri=ri: e.matmul(
                        psu[:], CT[:, k, ri, :], xri[:, xs_, ri, :], start=(k == 0 and ri == 0), stop=(k == 3 and ri == 1)),
                        reads=['CT', ('xri', xs_, ri)], writes=['psu'])
                if k == 3:
                    P.op('dve', lambda e: e.scalar_tensor_tensor(
                        out=tg[:], in0=uD[:, ss], scalar=dvec_s[:, 0:1], in1=psu[:], op0=ALU.mult, op1=ALU.add),
                        reads=['psu', 'G5', 'dvec'], writes=['tg'])
                    P.op('act', lambda e: e.activation(out=Y[3][:, ss], in_=tg[:], func=AF.Gelu),
                         reads=['tg'], writes=['Y3'])

            ga = attn_steps()
            nunit = (SB // 512) * 4
            for u in range(nunit):
                s5_front(u)
                next(ga, None)
                s5_dve(u)
                if u > 0:
                    s5_back(u - 1)
                next(ga, None)
                next(ga, None)
            s5_back(nunit - 1)
            for _ in ga:
                pass
            P.op('dve', lambda e: e.reciprocal(out=accL[:], in_=accL[:]), reads=['F1'], writes=['F1'])
            P.op('dve', tten(Y[0][:], accO[:], accL[:], ALU.mult), reads=['F0', 'F1'], writes=['Y0'])

            P.enabled = 'B' in stages
            qB, kB, vB, sgB = G[0], G[1], G[2], G[3]
            project(384, 128, evac_copy(qB, 'G0'))
            project(512, 128, evac_copy(kB, 'G1'))
            project(640, 128, evac_copy(vB, 'G2'))
            project(768, 128, evac_copy(sgB, 'G3', func=AF.Silu))
            def ret_chunk(c):
                cs = slice(c * 128, (c + 1) * 128)
                sl2 = c % 2
                rs = c % 2
                oc = slice((c % 4) * 128, (c % 4 + 1) * 128)
                po, pok = pacc, 'pacc'
                for h in range(2):
                    ps, pk = psA.get()
                    P.op('pe', lambda e, ps=ps, h=h: e.matmul(
                        ps[:, 0:128], kB[64 * h:64 * h + 64, cs], qB[64 * h:64 * h + 64, cs], start=True, stop=True),
                        reads=['G0', 'G1'], writes=[pk])
                    P.op('dve', lambda e, ps=ps, h=h: e.tensor_tensor(
                        out=ptB[:, sl2, h, :], in0=ps[:, 0:128], in1=rtab_s[:, h, :], op=ALU.mult),
                        reads=[pk, 'rtab'], writes=[('ptB', sl2, h)])
                pT, pTk = psT.get()
                P.op('pe', lambda e: e.transpose(out=pT[:, 0, :], in_=vB[:, cs], identity=ident_s[:]),
                     reads=['G2', 'ident'], writes=[pTk])
                P.op('pe', lambda e: e.transpose(out=pT[:, 1, :], in_=kB[:, cs], identity=ident_s[:]),
                     reads=['G1', 'ident'], writes=[pTk])
                P.op('act', lambda e: e.activation(out=vtok[:, sl2, :], in_=pT[:, 0, :], func=AF.Copy),
                     reads=[pTk], writes=[('vtok', sl2)])
                P.op('act', lambda e: e.activation(out=ktok[:, sl2, :], in_=pT[:, 1, :], func=AF.Copy),
                     reads=[pTk], writes=[('ktok', sl2)])
                P.op('dve', lambda e: e.tensor_tensor(
                    out=ktok[:, sl2, :], in0=ktok[:, sl2, :], in1=rtab_s[:, 3, :], op=ALU.mult),
                    reads=[('ktok', sl2), 'rtab'], writes=[('ktok', sl2)])
                P.op('dve', lambda e: e.tensor_tensor(
                    out=qdec[:, sl2, :], in0=qB[:, cs], in1=rtab_s[:, 2, :], op=ALU.mult),
                    reads=['G0', 'rtab'], writes=[('qdec', sl2)])
                yield
                for h in range(2):
                    hs = slice(64 * h, 64 * h + 64)
                    P.op('pe', lambda e, hs=hs: e.matmul(
                        psu[hs, 0:64], ktok[:, sl2, hs], vtok[:, sl2, hs], start=True, stop=True),
                        reads=[('ktok', sl2), ('vtok', sl2)], writes=['psu'])
                P.op('dve', lambda e: e.scalar_tensor_tensor(
                    out=Rr[:], in0=Rr[:], scalar=rtab_s[:, 4, 0:1], in1=psu[:, 0:64], op0=ALU.mult, op1=ALU.add),
                    reads=['psu', 'Rr', 'rtab'], writes=['Rr'])
                P.op('act', lambda e: e.activation(out=Rrb[:, 1 - rs, :], in_=Rr[:], func=AF.Copy),
                     reads=['Rr'], writes=[('Rrb', 1 - rs)])
                for h in range(2):
                    hs = slice(64 * h, 64 * h + 64)
                    P.op('pe', lambda e, hs=hs, h=h: e.matmul(
                        po[hs, oc], vtok[:, sl2, hs], ptB[:, sl2, h, :], start=True, stop=False),
                        reads=[('vtok', sl2), ('ptB', sl2, h)], writes=[pok])
                    P.op('pe', lambda e, hs=hs: e.matmul(
                        po[hs, oc], Rrb[hs, rs, :], qdec[hs, sl2, :], start=False, stop=True),
                        reads=[('Rrb', rs), ('qdec', sl2)], writes=[pok])
                if c % 4 == 3:
                    head_norm(po, pok, 0, sgB, 'G3', Y[1], 'Y1', (c // 4) * 512)
                yield

            prev = None
            for c in range(SB // 128):
                g = ret_chunk(c)
                next(g)
                if prev is not None:
                    next(prev, None)
                prev = g
            next(prev, None)

            P.enabled = 'C' in stages
            qC, kC, laF, bcF = F[0], F[1], F[2], F[3]
            vC, sgC, glow, qin, kout, kdc = G[0], G[1], G[2], G[3], G[4], G[5]
            project(896, 64, evac_copy(qC, 'F0', rows=64))
            project(960, 64, evac_copy(kC, 'F1', rows=64))
            project(1024, 128, evac_copy(vC, 'G0'))
            project(1152, 128, evac_copy(sgC, 'G1', func=AF.Silu))
            project(1408, 16, evac_copy(glow, 'G2', rows=16))
            for nt in range(SB // 512):
                ns = slice(nt * 512, (nt + 1) * 512)
                ps, pk = psA.get()
                P.op('pe', lambda e, ps=ps, ns=ns: e.matmul(ps[0:64, :], wgate_s[:, :], glow[0:16, ns], start=True, stop=True),
                     reads=['wgate', 'G2'], writes=[pk])
                P.op('act', lambda e, ps=ps, ns=ns: e.activation(
                    out=laF[0:64, ns], in_=ps[0:64, :], func=AF.Exp, scale=-1.0, bias=bgate_s[:, 1:2]),
                    reads=[pk, 'nbgate'], writes=['F2'])
            P.op('act', lambda e: e.activation(out=laF[0:64, :], in_=laF[0:64, :], func=AF.Ln, bias=1.0),
                 reads=['F2'], writes=['F2'])
            P.op('dve', tsc(laF[0:64, :], laF[0:64, :], -1.0 / 16.0, ALU.mult), reads=['F2'], writes=['F2'])
            P.op('dve', lambda e: e.tensor_tensor_scan(
                out=bcF[0:64, :], data0=rmask_s[0:64, :], data1=laF[0:64, :], initial=0.0, op0=ALU.mult, op1=ALU.add),
                reads=['F2', 'rmask'], writes=['F3'])
            P.op('act', lambda e: e.activation(out=laF[0:64, :], in_=bcF[0:64, :], func=AF.Exp), reads=['F3'], writes=['F2'])
            P.op('dve', lambda e: e.scalar_tensor_tensor(
                out=qin[0:64, :], in0=qC[0:64, :], scalar=32.0 ** -0.5, in1=laF[0:64, :], op0=ALU.mult, op1=ALU.mult),
                reads=['F0', 'F2'], writes=['G3'])
            P.op('act', lambda e: e.activation(out=laF[0:64, :], in_=bcF[0:64, :], func=AF.Exp, scale=-1.0),
                 reads=['F3', 'G3'], writes=['F2'])
            P.op('dve', tten(kout[0:64, :], kC[0:64, :], laF[0:64, :], ALU.mult), reads=['F1', 'F2'], writes=['G4'])
            blast = bcF[0:64, :].rearrange("p (c j) -> p c j", j=64)[:, :, 63]
            P.op('act', lambda e: e.activation(out=ebl[:, :], in_=blast, func=AF.Exp), reads=['F3'], writes=['ebl'])
            P.op('dve', lambda e: e.tensor_tensor(
                out=kdc[0:64, :].rearrange("p (c j) -> p c j", j=64),
                in0=kout[0:64, :].rearrange("p (c j) -> p c j", j=64),
                in1=ebl[:, :].unsqueeze(2).to_broadcast([64, SB // 64, 64]), op=ALU.mult),
                reads=['G4', 'ebl'], writes=['G5'])
            def gla_chunk(c):
                cs = slice(c * 128, (c + 1) * 128)
                sl2 = c % 2
                oc0 = (c % 4) * 128
                po, pok = pacc, 'pacc'
                for h in range(2):
                    ps, pk = psA.get()
                    P.op('pe', lambda e, ps=ps, h=h: e.matmul(
                        ps[:, 0:128], kout[32 * h:32 * h + 32, cs], qin[32 * h:32 * h + 32, cs], start=True, stop=True),
                        reads=['G3', 'G4'], writes=[pk])
                    P.op('dve', lambda e, ps=ps, h=h: e.tensor_tensor(
                        out=ptB[:, sl2, h, :], in0=ps[:, 0:128], in1=gmask_s[:, h, :], op=ALU.mult),
                        reads=[pk, 'gmask'], writes=[('ptB', sl2, h)])
                pT, pTk = psT.get()
                P.op('pe', lambda e: e.transpose(out=pT[:, 0, :], in_=vC[:, cs], identity=ident_s[:]),
                     reads=['G0', 'ident'], writes=[pTk])
                P.op('pe', lambda e: e.transpose(out=pT[:, 1, 0:64], in_=kdc[0:64, cs], identity=ident_s[0:64, 0:64]),
                     reads=['G5', 'ident'], writes=[pTk])
                P.op('act', lambda e: e.activation(out=vtok[:, sl2, :], in_=pT[:, 0, :], func=AF.Copy),
                     reads=[pTk], writes=[('vtok', sl2)])
                P.op('dve', lambda e: e.tensor_copy(out=ktok[:, sl2, 0:64], in_=pT[:, 1, 0:64]),
                     reads=[pTk], writes=[('ktok', sl2)])
                yield

                def update(cc):
                    ci = 2 * c + cc
                    ts_ = slice(64 * cc, 64 * cc + 64)
                    for h in range(2):
                        ks = slice(32 * h, 32 * h + 32)
                        hs = slice(64 * h, 64 * h + 64)
                        P.op('pe', lambda e, ks=ks, hs=hs: e.matmul(
                            psu[ks, 64:128], ktok[ts_, sl2, ks], vtok[ts_, sl2, hs], start=True, stop=True),
                            reads=[('ktok', sl2), ('vtok', sl2)], writes=['psu'])
                    P.op('dve', lambda e: e.scalar_tensor_tensor(
                        out=Rg[:], in0=Rg[:], scalar=ebl[:, ci:ci + 1], in1=psu[0:64, 64:128], op0=ALU.mult, op1=ALU.add),
                        reads=['psu', 'Rg', 'ebl'], writes=['Rg'])
                    for h in range(2):
                        P.op('act', lambda e, h=h: e.activation(
                            out=Rgb[32 * h:32 * h + 32, 1 - cc, 64 * h:64 * h + 64], in_=Rg[32 * h:32 * h + 32, :], func=AF.Copy),
                            reads=['Rg'], writes=[('Rgb', 1 - cc)])

                def cross(cc):
                    c64 = slice(c * 128 + cc * 64, c * 128 + cc * 64 + 64)
                    P.op('pe', lambda e: e.matmul(
                        po[:, oc0 + cc * 64:oc0 + cc * 64 + 64], Rgb[:, cc, :], qin[0:64, c64], start=False, stop=(cc == 1)),
                        reads=[('Rgb', cc), 'G3'], writes=[pok])

                update(0)
                for h in range(2):
                    hs = slice(64 * h, 64 * h + 64)
                    P.op('pe', lambda e, hs=hs, h=h: e.matmul(
                        po[hs, oc0:oc0 + 128], vtok[:, sl2, hs], ptB[:, sl2, h, :], start=True, stop=False),
                        reads=[('vtok', sl2), ('ptB', sl2, h)], writes=[pok])
                cross(0)
                update(1)
                cross(1)
                if c % 4 == 3:
                    head_norm(po, pok, 1, sgC, 'G1', Y[2], 'Y2', (c // 4) * 512)
                yield

            prev = None
            for c in range(SB // 128):
                g = gla_chunk(c)
                next(g)
                if prev is not None:
                    next(prev, None)
                prev = g
            next(prev, None)

            P.enabled = True
            for m in range(4):
                P.dma('sp', dr['ybuf'][l][s][m * 128:(m + 1) * 128, :], Y[m][:], reads=['Y%d' % m], writes=[('ybuf', s)])
            P.op('pool', lambda e, s=s: e.collective_compute(
                "AllGather", ALU.bypass, replica_groups=PAIRS, ins=[dr['ybuf'][l][s]], outs=[dr['yall'][l][s]]),
                reads=[('ybuf', s)], writes=[('yall', s)], coll=True)


def prep_A_consts(p):
    f = np.float32
    q = np.arange(128)[None, :]
    k = np.arange(128)[:, None]
    ab = np.zeros((128, 12, 128), f)
    for bi, r in enumerate((1, 4, 16)):
        for h in range(2):
            slope = 2.0 ** (-2.0 * (2 * p + h + 1))
            prev = np.where(k >= q, -slope * r * (q - k + 128), -1e30)
            cur = np.where(k <= q, -slope * r * (q - k), -1e30)
            ab[:, bi * 4 + h * 2 + 0, :] = prev
            ab[:, bi * 4 + h * 2 + 1, :] = cur
    rt = np.zeros((128, 5, 128), np.float64)
    n = np.arange(128)[None, :]
    m = np.arange(128)[:, None]
    for h in range(2):
        lg = np.log(1.0 - 2.0 ** (-5.0 - (2 * p + h)))
        rt[:, h, :] = np.where(n >= m, np.exp(np.maximum(n - m, 0) * lg), 0.0) * 0.125
        rt[64 * h:64 * h + 64, 2, :] = np.exp((np.arange(128) + 1.0) * lg)[None, :]
        rt[:, 3, 64 * h:64 * h + 64] = (np.exp((127 - np.arange(128)) * lg) * 0.125)[:, None]
        rt[64 * h:64 * h + 64, 4, :] = np.exp(128 * lg)
    gm = np.where((n >= m) & ((n // 64) == (m // 64)), 1.0, 0.0)
    blk = np.zeros((128, 128), f)
    blk[:64, :64] = 1.0 / 64
    blk[64:, 64:] = 1.0 / 64
    rm = np.ones((128, SB), f)
    rm[:, ::64] = 0.0
    return dict(abias=ab, rtab=rt.astype(f), gmask=np.stack([gm, gm], 1).astype(f), ident=np.eye(128, dtype=f),
                blk=blk, kp1=np.broadcast_to(np.arange(1, 513, dtype=f), (128, 512)).copy(), rmask=rm)


def prep_A_weights(inp, l, p):
    f = np.float32
    w = inp['w_in'][l]
    offs = np.cumsum([0, 256, 256, 256, 256, 256, 256, 256, 128, 128, 256, 16, 256, 256])
    seg = lambda i, a, b: w[:, offs[i] + a:offs[i] + b]
    hp = lambda i, wd: seg(i, p * wd, (p + 1) * wd)
    cols = [hp(0, 128), hp(1, 128), hp(2, 128), hp(3, 128), hp(4, 128), hp(5, 128), hp(6, 128),
            hp(7, 64), hp(8, 64), hp(9, 128), hp(11, 128), hp(12, 128), seg(10, 0, 16)]
    wc = np.concatenate(cols, axis=1)
    assert wc.shape[1] == NW
    r = {}
    r['w_in'] = np.ascontiguousarray(wc.reshape(8, 128, NW).transpose(1, 0, 2))
    gs = slice(8 * p, 8 * p + 8)
    tile = lambda a: np.ascontiguousarray(a[gs].reshape(4, 128, *a.shape[2:]))
    are, aim = tile(inp['s5_a_re'][l]), tile(inp['s5_a_im'][l])
    ldt = tile(np.broadcast_to(inp['s5_log_dt'][l][:, None], (16, 64)))
    r['s5p'] = np.ascontiguousarray(np.stack([are, aim, ldt], -1).transpose(1, 0, 2)).astype(f)
    bre, bim = tile(inp['s5_b_re'][l]), tile(inp['s5_b_im'][l])
    r['s5b'] = np.ascontiguousarray(np.stack([bre, bim], 2).transpose(1, 0, 2, 3)).astype(f)
    cre = tile(inp['s5_c_re'][l].transpose(0, 2, 1))
    cim = tile(inp['s5_c_im'][l].transpose(0, 2, 1))
    r['s5c'] = np.ascontiguousarray(np.stack([cre, cim], 2).transpose(1, 0, 2, 3)).astype(f)
    r['dvec'] = np.ascontiguousarray(inp['s5_d'][l][gs].reshape(128, 1)).astype(f)
    r['wgate'] = np.ascontiguousarray(inp['gla_w_gate'][l][:, 64 * p:64 * p + 64])
    r['bgate'] = np.ascontiguousarray(inp['gla_b_gate'][l][64 * p:64 * p + 64].reshape(64, 1))
    r['gains'] = np.ascontiguousarray(np.stack([inp['ret_gn_g'][l][128 * p:128 * p + 128],
                                                inp['gla_gn_g'][l][128 * p:128 * p + 128]], 1)).astype(f)
    return r


def build_fused(S=8192, ncores=8, depth=2):
    nc = bass.Bass("TRN2", target_bir_lowering=False)
    pairs = [[2 * i, 2 * i + 1] for i in range(ncores // 2)]
    global PAIRS
    PAIRS = pairs
    half = S // 2
    nsb = S // SB
    nq = half // 1024
    L = depth
    di = lambda name, shape, dt=F32: nc.dram_tensor(name, shape, dt, kind="ExternalInput").ap()
    dn = lambda name, shape, dt=BF16: nc.dram_tensor(name, shape, dt, kind="Internal").ap()
    dr = {}
    dr['xT_full'] = di("xT_full", [1024, S])
    dr['xT_own'] = di("xT_own", [1024, half])
    dr['memT'] = di("memT", [128, 8, 256])
    dr['sel'] = di("sel", [128, 2])
    for k, shp in (('abias', [128, 12, 128]), ('rtab', [128, 5, 128]), ('gmask', [128, 2, 128]), ('ident', [128, 128]),
                   ('blk', [128, 128]), ('kp1', [128, 512]), ('rmask', [128, SB])):
        dr[k] = di(k, shp)
    for k, shp in (('w_inA', [128, 8, NW]), ('s5p', [128, 4, 3]), ('s5b', [128, 4, 2, 16]), ('s5c', [128, 4, 2, 16]),
                   ('dvec', [128, 1]), ('wgate', [16, 64]), ('bgate', [64, 1]), ('gains', [128, 2]),
                   ('w_out', [128, 8, 1024]), ('w_q', [128, 8, 1024]), ('w_o', [128, 8, 1024]), ('w_kv', [4, 128, 8, 512]),
                   ('w_glu', [128, 2, 256]), ('w_gu', [11, 128, 2, 8, 256]), ('w_d', [2, 11, 128, 2, 512]), ('vecs', [128, 50])):
        dr[k] = di(k, [L] + shp)
    wshapes = dict(w_out=[128, 8, 1024], w_q=[128, 8, 1024], w_o=[128, 8, 1024], w_kv=[4, 128, 8, 512],
                   w_glu=[128, 2, 256], w_gu=[11, 128, 2, 8, 256], w_d=[2, 11, 128, 2, 512])
    dr['wconv'] = [[] for _ in range(L)]
    for k, shp in wshapes.items():
        dr['wb_' + k] = dn("wb_" + k, [L] + shp)
        for l in range(L):
            src, dst = dr[k][l], dr['wb_' + k][l]
            if k in ('w_out', 'w_q', 'w_o'):
                pieces = [(dst[:, 4 * h:4 * h + 4, :], src[:, 4 * h:4 * h + 4, :]) for h in range(2)]
            elif k == 'w_kv':
                pieces = [(dst[i], src[i]) for i in range(4)]
            elif k == 'w_gu':
                pieces = [(dst[i], src[i]) for i in range(11)]
            elif k == 'w_d':
                pieces = [(dst[i], src[i]) for i in range(2)]
            else:
                pieces = [(dst, src)]
            dr['wconv'][l] += pieces
    dr['xo'] = nc.dram_tensor("xo", [1024, half], F32, kind="ExternalOutput").ap()
    dr['ybuf'] = [[dn("ybuf_%d_%d" % (l, s), [512, SB]) for s in range(nsb)] for l in range(L)]
    dr['yall'] = [[dn("yall_%d_%d" % (l, s), [1024, SB]) for s in range(nsb)] for l in range(L)]
    dr['xown'] = dn("xown", [1024, half], F32)
    dr['xbf'] = [dn("xbf_%d" % q, [1024, 1024]) for q in range(nq)]
    dr['xall'] = [dn("xall_%d" % q, [2048, 1024]) for q in range(nq)]
    with ExitStack() as es_outer:
        P = Prog(nc, es_outer)
        for l in range(L):
            with ExitStack() as es:
                emit_A(nc, P, es, l, dr, S)
                P.emit_block()
            with ExitStack() as es:
                emit_D(nc, P, es, l, dr, half, last=(l == L - 1))
                P.emit_block()
    return nc


def make_in_maps(inp, ncores=8):
    x = inp['x']
    S = x.shape[1]
    half = S // 2
    depth = inp['w_in'].shape[0]
    constsA = [prep_A_consts(p) for p in range(2)]
    wA = [[prep_A_weights(inp, l, p) for l in range(depth)] for p in range(2)]
    wD = [prep_D_weights(inp, l) for l in range(depth)]
    wDs = {k: np.stack([wD[l][k] for l in range(depth)]) for k in wD[0]}
    wAs = [{('w_inA' if k == 'w_in' else k): np.stack([wA[p][l][k] for l in range(depth)]) for k in wA[p][0]} for p in range(2)]
    maps = []
    for c in range(ncores):
        b, p = c // 2, c % 2
        xT = np.ascontiguousarray(x[b].T)
        m = {}
        m.update(constsA[p])
        m.update(wAs[p])
        m.update(wDs)
        m['xT_full'] = xT
        m['xT_own'] = np.ascontiguousarray(xT[:, p * half:(p + 1) * half])
        m['memT'] = np.ascontiguousarray(inp['mem'][b].T.reshape(8, 128, -1).transpose(1, 0, 2))
        sel = np.zeros((128, 2), np.float32)
        sel[:, p] = 1.0
        m['sel'] = sel
        maps.append(m)
    return maps


def kernel(**inputs):
    inp = {k: np.asarray(v) for k, v in inputs.items()}
    x = inp['x']
    nb, S, D = x.shape
    ncores = 2 * nb
    half = S // 2
    nc = build_fused(S, ncores, inp['w_in'].shape[0])
    maps = make_in_maps(inp, ncores)
    res = run_bass_kernel_spmd(nc, maps, core_ids=list(range(ncores)))
    out = np.zeros((nb, S, D), np.float32)
    for c in range(ncores):
        out[c // 2, (c % 2) * half:(c % 2 + 1) * half] = np.asarray(res.results[c]['xo']).T
    return out
```

```python
import math
import numpy as np
import ml_dtypes
from contextlib import ExitStack
import concourse.bass as bass
import concourse.mybir as mybir
from concourse.bass_utils import run_bass_kernel_spmd

F32 = mybir.dt.float32
BF16 = mybir.dt.bfloat16
I32 = mybir.dt.int32
AF = mybir.ActivationFunctionType
ALU = mybir.AluOpType
AX = mybir.AxisListType

ENGS = ('pe', 'act', 'dve', 'pool', 'sp')
WIN = 2000
NP_DMA = 6
WIN_D = 120


def is_psum_key(b):
    n = b[0] if isinstance(b, tuple) else b
    return isinstance(n, str) and n.startswith('ps') or n == 'pacc'


class Prog:
    def __init__(self, nc, es):
        self.nc = nc
        self.es = es
        self.ops = []
        self.last_w = {}
        self.readers = {}
        self.enabled = True
        self.start = 0
        self.sems = {}
        self.ccount = {e: 0 for e in ENGS}
        self.dcount = {e: 0 for e in ENGS}
        self.comp = {}
        self.throttle = {}
        self.waited = {e: {} for e in ENGS}

    def op(self, eng, fn, reads=(), writes=(), dma=False, coll=False):
        if not self.enabled:
            return None
        i = len(self.ops)
        reads = list(reads)
        writes = list(writes)
        for b in reads:
            if is_psum_key(b) and b not in writes:
                writes.append(b)
        deps = set()
        for b in reads:
            w = self.last_w.get(b)
            if w is not None:
                deps.add(w)
        for b in writes:
            w = self.last_w.get(b)
            if w is not None:
                deps.add(w)
            for r in self.readers.get(b, ()):
                deps.add(r)
        for b in reads:
            self.readers.setdefault(b, []).append(i)
        for b in writes:
            self.last_w[b] = i
            self.readers[b] = []
        deps.discard(i)
        self.ops.append(dict(eng=eng, fn=fn, deps=deps, dma=dma, coll=coll))
        return i

    def dma(self, eng, out, in_, reads=(), writes=(), **kw):
        return self.op(eng, lambda e: e.dma_start(out=out, in_=in_, **kw), reads, writes, dma=True)

    def _sem(self, key):
        if key not in self.sems:
            self.sems[key] = self.es.enter_context(self.nc.semaphore("s_" + "_".join(str(x) for x in key)))
        return self.sems[key]

    def emit_block(self):
        nc = self.nc
        ops = self.ops
        lo = self.start
        asyncs = [i for i in range(lo, len(ops)) if ops[i]['dma'] or ops[i]['coll']]
        ops.append(dict(eng='sp', fn=None, deps=set(asyncs), dma=False, coll=False))
        hi = len(ops)

        def skip(od, o):
            return od['eng'] == 'pe' and o['eng'] == 'pe' and not od['dma'] and not o['dma']

        need = set()
        for i in range(lo, hi):
            for d in ops[i]['deps']:
                if not skip(ops[d], ops[i]):
                    need.add(d)
        for i in range(lo, hi):
            o = ops[i]
            e = o['eng']
            if o['dma']:
                j = self.dcount[e]
                self.dcount[e] += 1
                s, r = j % NP_DMA, j // NP_DMA
                key = ('d', e, s, r // WIN_D)
                self.comp[i] = (key, 16 * (r % WIN_D + 1))
                if r >= 1 and (r % WIN_D) != 0:
                    self.throttle[i] = (key, 16 * (r % WIN_D))
                elif r >= 1:
                    self.throttle[i] = (('d', e, s, (r - 1) // WIN_D), 16 * ((r - 1) % WIN_D + 1))
            elif i in need:
                k = self.ccount[e]
                self.ccount[e] += 1
                self.comp[i] = (('c', e, k // WIN), k % WIN + 1)
        comp, throttle = self.comp, self.throttle
        per = {e: [i for i in range(lo, hi) if ops[i]['eng'] == e] for e in ENGS}
        with nc.Block() as block:
            def run(e, eng):
                waited = self.waited[e]
                for i in per[e]:
                    o = ops[i]
                    ws = [comp[d] for d in sorted(o['deps']) if not skip(ops[d], o)]
                    if i in throttle:
                        ws.append(throttle[i])
                    for key, val in ws:
                        if waited.get(key, 0) >= val:
                            continue
                        waited[key] = val
                        eng.wait_ge(self._sem(key), val)
                    if o['fn'] is None:
                        continue
                    ins = o['fn'](eng)
                    if i in comp:
                        key, val = comp[i]
                        ins.then_inc(self._sem(key), 16 if o['dma'] else 1)

            @block.tensor
            def _(eng):
                run('pe', eng)

            @block.scalar
            def _(eng):
                run('act', eng)

            @block.vector
            def _(eng):
                run('dve', eng)

            @block.gpsimd
            def _(eng):
                run('pool', eng)

            @block.sync
            def _(eng):
                run('sp', eng)
        self.start = hi
        self.last_w.clear()
        self.readers.clear()


ALPHA = float((2 * 2) ** 0.25)
LN_EPS = 1e-5
NT = 512


class Rot:
    def __init__(self, slots):
        self.slots = slots
        self.i = 0

    def get(self):
        s = self.slots[self.i % len(self.slots)]
        self.i += 1
        return s


def emit_D(nc, P, es, l, dr, TOK, last):
    ntile = TOK // NT
    memT, vecs = dr['memT'], dr['vecs'][l]
    w_out, w_q, w_o, w_kv, w_glu, w_gu, w_d = (dr['wb_' + k][l] for k in ('w_out', 'w_q', 'w_o', 'w_kv', 'w_glu', 'w_gu', 'w_d'))
    xsrc = dr['xT_own'] if l == 0 else dr['xown']
    xTv = xsrc.rearrange("(k p) n -> p k n", p=128)
    xdst = dr['xo'] if last else dr['xown']
    xov = xdst.rearrange("(k p) n -> p k n", p=128)
    sfx = "_D%d" % l
    if True:
        sb = lambda name, shape, dt=BF16: es.enter_context(nc.sbuf_tensor(name + sfx, shape, dt))
        wo_s = sb("wo_s", [128, 8, 1024])
        wq_s = sb("wq_s", [128, 8, 1024])
        woo_s = sb("woo_s", [128, 8, 1024])
        wglu_s = sb("wglu_s", [128, 2, 256])
        kT_s = sb("kT_s", [128, 8, 256])
        V_s = sb("V_s", [128, 2, 1024])
        memT_s = sb("memT_s", [128, 8, 256])
        vec_s = sb("vec_s", [128, 50], F32)
        ones_m = sb("ones_m", [128, 128])
        ones_1 = sb("ones_1", [128, 128])
        xs = sb("xs", [128, 8, NT], F32)
        xb = sb("xb", [128, 8, NT])
        ys = sb("ys", [128, 8, NT])
        ysA = sb("ysA", [128, 8, NT])
        ysB = sb("ysB", [128, 8, NT])
        sel_s = sb("sel_s", [128, 2], F32)
        dmy = sb("dmy", [128, 2], F32)
        yd = sb("yd", [128, 2, NT])
        qs = sb("qs", [128, 8, NT])
        os_ = sb("os", [128, 8, NT])
        hid = sb("hid", [128, 22, NT])
        wst = sb("wst", [128, 3, 4096])
        wdst = sb("wdst", [128, 3, 1024])
        zb = sb("zb", [128, 8, NT])
        rstd = sb("rstd", [128, 2, NT], F32)
        pt = sb("pt", [128, 4, NT])
        rl = sb("rl", [128, 2, NT], F32)
        sg = sb("sg", [128, 2, NT], F32)
        psb = [es.enter_context(nc.psum_tensor("ps%d" % i + sfx, [128, 512], F32)) for i in range(8)]

        psA = Rot([(psb[i], ('ps', i)) for i in range(4)])
        psD = [(psb[4 + i], ('ps', 4 + i)) for i in range(4)]
        psAtt = Rot([(psb[i], ('ps', i)) for i in range(8)])
        evac_i = [0]

        P.op('dve', lambda e: e.memset(ones_m[:], 1.0 / 1024.0), writes=['ones_m'])
        P.op('dve', lambda e: e.memset(ones_1[:], 1.0), writes=['ones_1'])
        P.dma('sp', vec_s[:], vecs, writes=['vec'])
        P.dma('sp', sel_s[:], dr['sel'], writes=['sel'])
        P.dma('pool', memT_s[:], memT, writes=['memT'])
        P.dma('sp', wglu_s[:], w_glu, writes=['wglu'])
        for (dst, src, key) in ((wo_s, w_out, 'wo'), (wq_s, w_q, 'wq'), (woo_s, w_o, 'woo')):
            for h in range(2):
                P.dma('sp', dst[:, 4 * h:4 * h + 4, :], src[:, 4 * h:4 * h + 4, :], writes=[(key, h)])
        WO = [('wo', 0), ('wo', 1)]
        WQ = [('wq', 0), ('wq', 1)]
        WOO = [('woo', 0), ('woo', 1)]

        for pc in range(4):
            slot = pc % 2
            wv = wst[:, slot, :].rearrange("p (k n) -> p k n", k=8)
            P.dma('sp', wv, w_kv[pc], writes=[('wst', slot)])
            if pc < 2:
                for j in range(4):
                    ps, pk = psA.get()
                    for kc in range(8):
                        P.op('pe', lambda e, ps=ps, wv=wv, kc=kc, j=j: e.matmul(
                            ps[:, 0:256], wv[:, kc, j * 128:(j + 1) * 128], memT_s[:, kc, :],
                            start=(kc == 0), stop=(kc == 7)), reads=[('wst', slot), 'memT'], writes=[pk])
                    P.op('act', lambda e, ps=ps, mc=4 * pc + j: e.activation(
                        out=kT_s[:, mc, :], in_=ps[:, 0:256], func=AF.Copy), reads=[pk], writes=['kT'])
            else:
                for mm in range(2):
                    ps, pk = psA.get()
                    for kc in range(8):
                        P.op('pe', lambda e, ps=ps, wv=wv, kc=kc, mm=mm: e.matmul(
                            ps[:], memT_s[:, kc, mm * 128:(mm + 1) * 128], wv[:, kc, :],
                            start=(kc == 0), stop=(kc == 7)), reads=[('wst', slot), 'memT'], writes=[pk])
                    P.op('act', lambda e, ps=ps, mm=mm, c0=(pc - 2) * 512: e.activation(
                        out=V_s[:, mm, c0:c0 + 512], in_=ps[:], func=AF.Copy), reads=[pk], writes=['V'])

        def resid(ps, pk, mc):
            P.op('dve', lambda e: e.scalar_tensor_tensor(
                out=xs[:, mc, :], in0=xs[:, mc, :], scalar=ALPHA, in1=ps[:], op0=ALU.mult, op1=ALU.add),
                reads=[pk, ('xs', mc)], writes=[('xs', mc)])

        xalias = hid[:].rearrange("p a b -> p (a b)").bitcast(F32)[:, 0:8 * NT].rearrange("p (c n) -> p c n", n=NT)
        akeys = lambda c: [('hid', 2 * c), ('hid', 2 * c + 1)]

        def load_x(t):
            ts = slice(t * NT, (t + 1) * NT)
            for h in range(2):
                P.dma('sp', xs[:, 4 * h:4 * h + 4, :], xTv[:, 4 * h:4 * h + 4, ts],
                      writes=[('xs', c) for c in range(4 * h, 4 * h + 4)])

        def load_y(t):
            sA = (t * NT) // 2048
            sB = (TOK + t * NT) // 2048
            cs0 = (t * NT) % 2048
            for (dst, sidx, key) in ((ysA, sA, 'ysA'), (ysB, sB, 'ysB')):
                for r_ in range(2):
                    src = dr['yall'][l][sidx][r_ * 512:(r_ + 1) * 512, cs0:cs0 + NT].rearrange("(m p) n -> p m n", p=128)
                    dstv = dst[:].rearrange("p (m r) n -> p m r n", r=2)[:, :, r_, :]
                    P.dma('sp', dstv, src, reads=[('yall', sidx)], writes=[(key, r_)])
            for c in range(8):
                P.op('dve', lambda e, c=c: e.tensor_scalar(out=ys[:, c, :], in0=ysA[:, c, :], scalar1=sel_s[:, 0:1],
                                                          scalar2=None, op0=ALU.mult),
                     reads=[('ysA', c % 2), 'sel'], writes=[('ys', c)])
                P.op('dve', lambda e, c=c: e.scalar_tensor_tensor(
                    out=ys[:, c, :], in0=ysB[:, c, :], scalar=sel_s[:, 1:2], in1=ys[:, c, :], op0=ALU.mult, op1=ALU.add),
                    reads=[('ysB', c % 2), 'sel', ('ys', c)], writes=[('ys', c)])

        def layer_norm(gcol, bcol, final=False):
            for c in range(8):
                P.op('act', lambda e, c=c: e.activation(out=zb[:, c, :], in_=xs[:, c, :], func=AF.Copy),
                     reads=[('xs', c)], writes=[('zb', c)])
            P.op('act', lambda e: e.activation(out=dmy[:, 0:1], in_=sel_s[:, 0:1], func=AF.Sqrt), reads=['sel'], writes=['dmy'])
            pm, pmk = psA.get()
            for c in range(8):
                P.op('pe', lambda e, c=c: e.matmul(pm[:], ones_m[:], zb[:, c, :], start=(c == 0), stop=(c == 7)),
                     reads=['ones_m', ('zb', c)], writes=[pmk])
            for c in range(8):
                P.op('dve', lambda e, c=c: e.tensor_tensor(out=xs[:, c, :], in0=xs[:, c, :], in1=pm[:], op=ALU.subtract),
                     reads=[pmk, ('xs', c)], writes=[('xs', c)])
                P.op('act', lambda e, c=c: e.activation(out=zb[:, c, :], in_=xs[:, c, :], func=AF.Square),
                     reads=[('xs', c)], writes=[('zb', c)])
            pv, pvk = psA.get()
            for c in range(8):
                P.op('pe', lambda e, c=c: e.matmul(pv[:], ones_m[:], zb[:, c, :], start=(c == 0), stop=(c == 7)),
                     reads=['ones_m', ('zb', c)], writes=[pvk])
            P.op('act', lambda e: e.activation(out=rstd[:, 0, :], in_=pv[:], func=AF.Sqrt, bias=LN_EPS),
                 reads=[pvk], writes=[('rstd', 0)])
            P.op('dve', lambda e: e.reciprocal(out=rstd[:, 1, :], in_=rstd[:, 0, :]),
                 reads=[('rstd', 0)], writes=[('rstd', 1)])
            for c in range(8):
                P.op('dve', lambda e, c=c: e.tensor_tensor(out=xs[:, c, :], in0=xs[:, c, :], in1=rstd[:, 1, :], op=ALU.mult),
                     reads=[('rstd', 1), ('xs', c)], writes=[('xs', c)])
                dst = xalias if final else xs
                dk = akeys(c) if final else [('xs', c)]
                P.op('act', lambda e, c=c, dst=dst: e.activation(
                    out=dst[:, c, :], in_=xs[:, c, :], func=AF.Identity,
                    scale=vec_s[:, gcol + c:gcol + c + 1], bias=vec_s[:, bcol + c:bcol + c + 1]),
                    reads=['vec', ('xs', c)], writes=dk)
            for c in range(8):
                dst = xalias if final else xs
                dk = akeys(c) if final else [('xs', c)]
                P.op('dve', lambda e, c=c, dst=dst: e.tensor_copy(out=xb[:, c, :], in_=dst[:, c, :]),
                     reads=dk, writes=[('xb', c)])

        outs = []
        for t in range(ntile):
            ts = slice(t * NT, (t + 1) * NT)
            if t == 0:
                load_x(0)
                load_y(0)
            for mc in range(2):
                ps, pk = psA.get()
                for kc in range(2):
                    P.op('pe', lambda e, ps=ps, kc=kc, mc=mc: e.matmul(
                        ps[:], wglu_s[:, kc, mc * 128:(mc + 1) * 128], ys[:, 6 + kc, :],
                        start=(kc == 0), stop=(kc == 1)), reads=['wglu', ('ys', 6), ('ys', 7)], writes=[pk])
                P.op('act', lambda e, ps=ps, mc=mc: e.activation(
                    out=sg[:, mc, :], in_=ps[:], func=AF.Sigmoid, bias=vec_s[:, mc:mc + 1]),
                    reads=[pk, 'vec'], writes=[('sg', mc)])
                P.op('dve', lambda e, mc=mc: e.tensor_tensor(
                    out=yd[:, mc, :], in0=ys[:, 6 + mc, :], in1=sg[:, mc, :], op=ALU.mult),
                    reads=[('sg', mc), ('ys', 6 + mc)], writes=[('yd', mc)])
            for mc in range(8):
                ps, pk = psA.get()
                for kc in range(8):
                    rhs = ys[:, kc, :] if kc < 6 else yd[:, kc - 6, :]
                    rk = ('ys', kc) if kc < 6 else ('yd', kc - 6)
                    P.op('pe', lambda e, ps=ps, kc=kc, mc=mc, rhs=rhs: e.matmul(
                        ps[:], wo_s[:, kc, mc * 128:(mc + 1) * 128], rhs, start=(kc == 0), stop=(kc == 7)),
                        reads=WO + [rk], writes=[pk])
                resid(ps, pk, mc)
            if t + 1 < ntile:
                load_y(t + 1)
            layer_norm(2, 10)
            for mc in range(8):
                ps, pk = psA.get()
                for kc in range(8):
                    P.op('pe', lambda e, ps=ps, kc=kc, mc=mc: e.matmul(
                        ps[:], wq_s[:, kc, mc * 128:(mc + 1) * 128], xb[:, kc, :], start=(kc == 0), stop=(kc == 7)),
                        reads=WQ + [('xb', kc)], writes=[pk])
                P.op('act', lambda e, ps=ps, mc=mc: e.activation(
                    out=qs[:, mc, :], in_=ps[:], func=AF.Copy, scale=1.0 / 16.0), reads=[pk], writes=[('qs', mc)])
            def att_head(h):
                for mm in range(2):
                    ps, pk = psAtt.get()
                    for dc in range(2):
                        P.op('pe', lambda e, ps=ps, dc=dc, mm=mm: e.matmul(
                            ps[:], kT_s[:, 2 * h + dc, mm * 128:(mm + 1) * 128], qs[:, 2 * h + dc, :],
                            start=(dc == 0), stop=(dc == 1)), reads=['kT', ('qs', 2 * h + dc)], writes=[pk])
                    pslot = (h % 2) * 2 + mm
                    P.op('act', lambda e, ps=ps, pslot=pslot: e.activation(
                        out=pt[:, pslot, :], in_=ps[:], func=AF.Exp), reads=[pk], writes=[('pt', pslot)])
                yield
                pl, plk = psAtt.get()
                for mm in range(2):
                    P.op('pe', lambda e, mm=mm: e.matmul(
                        pl[:], ones_1[:], pt[:, (h % 2) * 2 + mm, :], start=(mm == 0), stop=(mm == 1)),
                        reads=['ones_1', ('pt', (h % 2) * 2 + mm)], writes=[plk])
                P.op('dve', lambda e: e.reciprocal(out=rl[:, h % 2, :], in_=pl[:]),
                     reads=[plk], writes=[('rl', h % 2)])
                for dc in range(2):
                    ps, pk = psAtt.get()
                    for mm in range(2):
                        P.op('pe', lambda e, ps=ps, dc=dc, mm=mm: e.matmul(
                            ps[:], V_s[:, mm, h * 256 + dc * 128:h * 256 + (dc + 1) * 128], pt[:, (h % 2) * 2 + mm, :],
                            start=(mm == 0), stop=(mm == 1)), reads=['V', ('pt', (h % 2) * 2 + mm)], writes=[pk])
                    P.op('dve', lambda e, ps=ps, dc=dc: e.tensor_tensor(
                        out=os_[:, 2 * h + dc, :], in0=ps[:], in1=rl[:, h % 2, :], op=ALU.mult),
                        reads=[pk, ('rl', h % 2)], writes=[('os', 2 * h + dc)])
                yield

            prev = None
            for h in range(4):
                g = att_head(h)
                next(g)
                if prev is not None:
                    next(prev, None)
                prev = g
            next(prev, None)
            for mc in range(8):
                ps, pk = psA.get()
                for kc in range(8):
                    P.op('pe', lambda e, ps=ps, kc=kc, mc=mc: e.matmul(
                        ps[:], woo_s[:, kc, mc * 128:(mc + 1) * 128], os_[:, kc, :], start=(kc == 0), stop=(kc == 7)),
                        reads=WOO + [('os', kc)], writes=[pk])
                resid(ps, pk, mc)
            layer_norm(18, 26)
            for jp in range(11):
                slot = jp % 3
                wv = wst[:, slot, :].rearrange("p (g k n) -> p g k n", g=2, k=8)
                P.dma('pool', wv, w_gu[jp], writes=[('wst', slot)])
                for jj in range(2):
                    j = 2 * jp + jj
                    pg, pgk = psA.get()
                    pu, puk = psA.get()
                    for (pp, ppk, g) in ((pg, pgk, 0), (pu, puk, 1)):
                        for kc in range(8):
                            P.op('pe', lambda e, pp=pp, g=g, kc=kc, jj=jj, wv=wv: e.matmul(
                                pp[:], wv[:, g, kc, jj * 128:(jj + 1) * 128], xb[:, kc, :],
                                start=(kc == 0), stop=(kc == 7)), reads=[('wst', slot), ('xb', kc)], writes=[ppk])
                    P.op('act', lambda e, pg=pg, jj=jj: e.activation(out=sg[:, jj, :], in_=pg[:], func=AF.Silu),
                         reads=[pgk], writes=[('sg', jj)])
                    P.op('dve', lambda e, pu=pu, jj=jj, j=j: e.tensor_tensor(
                        out=hid[:, j, :], in0=pu[:], in1=sg[:, jj, :], op=ALU.mult),
                        reads=[puk, ('sg', jj)], writes=[('hid', j)])
            for sw in range(2):
                for jp in range(11):
                    slot = (sw * 11 + jp) % 3
                    wv = wdst[:, slot, :].rearrange("p (j n) -> p j n", j=2)
                    P.dma('pool', wv, w_d[sw, jp], writes=[('wdst', slot)])
                    for jj in range(2):
                        j = 2 * jp + jj
                        for q in range(4):
                            ps, pk = psD[q]
                            P.op('pe', lambda e, ps=ps, wv=wv, jj=jj, q=q, j=j: e.matmul(
                                ps[:], wv[:, jj, q * 128:(q + 1) * 128], hid[:, j, :],
                                start=(j == 0), stop=(j == 21)), reads=[('wdst', slot), ('hid', j)], writes=[pk])
                for q in range(4):
                    ps, pk = psD[q]
                    resid(ps, pk, sw * 4 + q)
            layer_norm(34, 42, final=True)
            if t + 1 < ntile:
                load_x(t + 1)
            for h in range(2):
                P.dma('sp', xov[:, 4 * h:4 * h + 4, ts], xalias[:, 4 * h:4 * h + 4, :],
                      reads=[k for c in range(4 * h, 4 * h + 4) for k in akeys(c)], writes=['xdst'])
            if not last:
                q = (t * NT) // 1024
                c1 = (t * NT) % 1024
                P.dma('sp', dr['xbf'][q].rearrange("(k p) n -> p k n", p=128)[:, :, c1:c1 + NT], xb[:],
                      reads=[('xb', c) for c in range(8)], writes=[('xbf', q)])
                if c1 + NT == 1024:
                    P.op('pool', lambda e, q=q: e.collective_compute(
                        "AllGather", ALU.bypass, replica_groups=PAIRS, ins=[dr['xbf'][q]], outs=[dr['xall'][q]]),
                        reads=[('xbf', q)], writes=[('xall', q)], coll=True)


def prep_D_weights(inp, l):
    f = np.float32
    kp = lambda w: np.ascontiguousarray(w.reshape(8, 128, -1).transpose(1, 0, 2))
    r = {}
    r['w_out'] = kp(inp['w_mix_out'][l])
    r['w_q'] = kp(inp['w_mem_q'][l])
    r['w_o'] = kp(inp['w_mem_o'][l])
    wkv = kp(inp['w_mem_kv'][l])
    r['w_kv'] = np.ascontiguousarray(wkv.reshape(128, 8, 4, 512).transpose(2, 0, 1, 3))
    r['w_glu'] = np.ascontiguousarray(inp['s5_w_glu'][l].reshape(2, 128, 256).transpose(1, 0, 2))
    g = kp(inp['w_ff_gate'][l]).reshape(128, 8, 11, 256)
    u = kp(inp['w_ff_up'][l]).reshape(128, 8, 11, 256)
    gu = np.stack([g, u], axis=1)
    r['w_gu'] = np.ascontiguousarray(gu.transpose(3, 0, 1, 2, 4))
    wd = inp['w_ff_down'][l].reshape(11, 2, 128, 2, 512)
    r['w_d'] = np.ascontiguousarray(wd.transpose(3, 0, 2, 1, 4))
    cols = [inp['s5_b_glu'][l].reshape(2, 128)]
    for nm in ('ln_mix_g', 'ln_mix_b', 'ln_mem_g', 'ln_mem_b', 'ln_ff_g', 'ln_ff_b'):
        cols.append(inp[nm][l].reshape(8, 128))
    r['vecs'] = np.ascontiguousarray(np.concatenate(cols, axis=0).T.astype(f))
    return r


PAIRS = [[0, 1], [2, 3], [4, 5], [6, 7]]

DBG = ''

SB = 2048
NW = 11 * 128 + 16


PAIRS = [[0, 1], [2, 3], [4, 5], [6, 7]]


def emit_A(nc, P, es, l, dr, T, stages='sABCD'):
    nsb = T // SB
    half = T // 2
    w_in, s5p, s5b, s5c, dvec = dr['w_inA'][l], dr['s5p'][l], dr['s5b'][l], dr['s5c'][l], dr['dvec'][l]
    wgate, bgate, gains = dr['wgate'][l], dr['bgate'][l], dr['gains'][l]
    abias, rtab, gmask, ident, blk, kp1, rmask = (dr[k] for k in ('abias', 'rtab', 'gmask', 'ident', 'blk', 'kp1', 'rmask'))
    xTv = dr['xT_full'].rearrange("(k p) n -> p k n", p=128)
    sfx = "_A%d" % l
    if True:
        sb = lambda name, shape, dt=BF16: es.enter_context(nc.sbuf_tensor(name + sfx, shape, dt))
        w_s = sb("w_s", [128, 8, NW])
        xsb = sb("xsb", [128, 8, SB])
        G = [sb("G%d" % i, [128, SB]) for i in range(6)]
        Y = [sb("Y%d" % i, [128, SB]) for i in range(4)]
        F = [sb("F%d" % i, [128, SB], F32) for i in range(4)]
        kA = sb("kA", [128, 2 * SB])
        vA = sb("vA", [128, 2 * SB])
        abias_s = sb("abias_s", [128, 12, 128])
        rtab_s = sb("rtab_s", [128, 5, 128], F32)
        gmask_s = sb("gmask_s", [128, 2, 128], F32)
        ident_s = sb("ident_s", [128, 128])
        identf_s = sb("identf_s", [128, 128], F32)
        blk_s = sb("blk_s", [128, 128])
        ones_s = sb("ones_s", [128, 64])
        kp1_s = sb("kp1_s", [128, 512], F32)
        rmask_s = sb("rmask_s", [128, SB])
        gains_s = sb("gains_s", [128, 2], F32)
        wgate_s = sb("wgate_s", [16, 64])
        bgate_s = sb("bgate_s", [64, 2], F32)
        dvec_s = sb("dvec_s", [128, 1], F32)
        ptA = sb("ptA", [128, 2, 4, 128])
        vblk = sb("vblk", [128, 2, 2, 128])
        ptB = sb("ptB", [128, 2, 2, 128])
        vtok = sb("vtok", [128, 2, 128])
        ktok = sb("ktok", [128, 2, 128])
        qdec = sb("qdec", [128, 2, 128])
        Rr = sb("Rr", [128, 64], F32)
        Rrb = sb("Rrb", [128, 2, 64])
        Rg = sb("Rg", [64, 64], F32)
        Rgb = sb("Rgb", [64, 2, 128])
        ebl = sb("ebl", [64, SB // 64], F32)
        ob = sb("ob", [128, 512])
        s5p_s = sb("s5p_s", [128, 4, 3], F32)
        s5b_s = sb("s5b_s", [128, 4, 2, 16], F32)
        s5c_s = sb("s5c_s", [128, 4, 2, 16], F32)
        sm = sb("sm", [128, 24, 4], F32)
        bb = sb("bb", [128, 4, 2, 16], F32)
        Zf = sb("Zf", [128, 128], F32)
        ZT = sb("ZT", [128, 4, 2, 128])
        CT = sb("CT", [128, 4, 2, 128])
        cosT = sb("cosT", [128, 4, 512], F32)
        sinT = sb("sinT", [128, 4, 512], F32)
        tb = [sb("tb%d" % i, [128, 512], F32) for i in range(2)]
        tt = [sb("tt%d" % i, [128, 512], F32) for i in range(4)]
        tg = sb("tg", [128, 512], F32)
        xri = sb("xri", [128, 2, 2, 512])
        xend = sb("xend", [128, 4, 2], F32)
        etmp = sb("etmp", [128, 4], F32)
        psb = [es.enter_context(nc.psum_tensor("ps%d" % i + sfx, [128, 512], F32)) for i in range(4)]
        pacc = es.enter_context(nc.psum_tensor("pacc" + sfx, [128, 512], F32))
        pst = [es.enter_context(nc.psum_tensor("pst%d" % i + sfx, [128, 8, 128], BF16)) for i in range(2)]
        psu = es.enter_context(nc.psum_tensor("psu" + sfx, [128, 512], F32))

        taps = []

        def tap(name, ap, keys, dt=BF16):
            if 'tap' not in DBG:
                return
            shp = list(ap.shape)
            t = nc.dram_tensor("tap_" + name, shp, dt, kind="ExternalOutput").ap()
            taps.append(P.dma('sp', t, ap, reads=keys))
        psA = Rot([(psb[i], ("ps", i)) for i in range(4)])
        psT = Rot([(pst[i], ('pst', i)) for i in range(2)])
        psAtt = Rot([(psb[i], ('ps', i)) for i in range(3)] + [(pacc, 'pacc')])
        ablk = [0]
        PI = math.pi

        P.dma('pool', w_s[:, 0:4, :], w_in[:, 0:4, :], writes=[('w', 0)])
        P.dma('pool', w_s[:, 4:8, :], w_in[:, 4:8, :], writes=[('w', 1)])
        WK = [('w', 0), ('w', 1)]
        P.dma('pool', abias_s[:], abias, writes=['abias'])
        P.dma('sp', rtab_s[:], rtab, writes=['rtab'])
        P.dma('sp', gmask_s[:], gmask, writes=['gmask'])
        P.dma('pool', ident_s[:], ident, writes=['ident'])
        P.dma('sp', identf_s[:], ident, writes=['identf'])
        P.dma('pool', blk_s[:], blk, writes=['blk'])
        P.dma('sp', kp1_s[:], kp1, writes=['kp1'])
        P.dma('pool', rmask_s[:], rmask, writes=['rmask'])
        P.dma('sp', gains_s[:], gains, writes=['gains'])
        P.dma('pool', wgate_s[:], wgate, writes=['wgate'])
        P.dma('sp', bgate_s[:, 0:1], bgate, writes=['bgate'])
        P.dma('sp', dvec_s[:], dvec, writes=['dvec'])
        P.dma('sp', s5p_s[:], s5p, writes=['s5p'])
        P.dma('sp', s5b_s[:], s5b, writes=['s5b'])
        P.dma('sp', s5c_s[:], s5c, writes=['s5c'])
        P.op('dve', lambda e: e.memset(ones_s[:], 1.0), writes=['ones'])
        P.op('dve', lambda e: e.tensor_scalar(out=bgate_s[:, 1:2], in0=bgate_s[:, 0:1], scalar1=-1.0, scalar2=None,
                                              op0=ALU.mult), reads=['bgate'], writes=['nbgate'])
        P.op('dve', lambda e: e.memset(Rr[:], 0.0), writes=['Rr'])
        P.op('dve', lambda e: e.memset(Rrb[:], 0.0), writes=[('Rrb', 0), ('Rrb', 1)])
        P.op('dve', lambda e: e.memset(Rg[:], 0.0), writes=['Rg'])
        P.op('dve', lambda e: e.memset(Rgb[:], 0.0), writes=[('Rgb', 0), ('Rgb', 1)])
        P.op('dve', lambda e: e.memset(xend[:], 0.0), writes=['xend'])

        P.enabled = 's' in stages
        SMK = 'sm'
        smc = lambda i: sm[:, i, :]

        def dv(fn, reads=(), writes=()):
            P.op('dve', fn, reads=list(reads) + [SMK], writes=list(writes) + [SMK])

        def av(fn, reads=(), writes=()):
            P.op('act', fn, reads=list(reads) + [SMK], writes=list(writes) + [SMK])

        def tsc(out, in0, s1, op0, s2=None, op1=None):
            if op1 is None:
                return lambda e: e.tensor_scalar(out=out, in0=in0, scalar1=s1, scalar2=None, op0=op0)
            return lambda e: e.tensor_scalar(out=out, in0=in0, scalar1=s1, scalar2=s2, op0=op0, op1=op1)

        def tten(out, a, b, op):
            return lambda e: e.tensor_tensor(out=out, in0=a, in1=b, op=op)

        def reduce_turns(dst, src, tmp, wr=dv):
            wr(lambda e: e.tensor_copy(out=tmp.bitcast(I32), in_=src))
            wr(lambda e: e.tensor_copy(out=dst, in_=tmp.bitcast(I32)))
            wr(tten(dst, src, dst, ALU.subtract))
            wr(tsc(tmp, dst, 0.5, ALU.is_gt))
            wr(tten(dst, dst, tmp, ALU.subtract))
            wr(tsc(tmp, dst, -0.5, ALU.is_lt))
            wr(tten(dst, dst, tmp, ALU.add))

        are, aim, ldt = s5p_s[:, :, 0], s5p_s[:, :, 1], s5p_s[:, :, 2]
        DT, RHO, THT, TF, TMP, TF2, SIN, COS, NR, AI, DEN, FRE, FIM, T1, T2 = [smc(i) for i in range(15)]
        av(lambda e: e.activation(out=DT, in_=ldt, func=AF.Exp), reads=['s5p'])
        dv(tten(T1, are, DT, ALU.mult), reads=['s5p'])
        av(lambda e: e.activation(out=RHO, in_=T1, func=AF.Exp))
        dv(tten(THT, aim, DT, ALU.mult), reads=['s5p'])
        dv(tsc(T2, THT, 1.0 / (2 * PI), ALU.mult))
        reduce_turns(TF, T2, TMP)
        av(lambda e: e.activation(out=SIN, in_=TF, func=AF.Sin, scale=2 * PI))
        dv(tsc(T2, T2, 0.25, ALU.add))
        reduce_turns(TF2, T2, TMP)
        av(lambda e: e.activation(out=COS, in_=TF2, func=AF.Sin, scale=2 * PI))
        dv(tten(NR, RHO, COS, ALU.mult))
        dv(tsc(NR, NR, -1.0, ALU.add))
        dv(tten(AI, RHO, SIN, ALU.mult))
        dv(tten(DEN, are, are, ALU.mult), reads=['s5p'])
        dv(tten(T1, aim, aim, ALU.mult), reads=['s5p'])
        dv(tten(DEN, DEN, T1, ALU.add))
        dv(lambda e: e.reciprocal(out=DEN, in_=DEN))
        dv(tten(FRE, NR, are, ALU.mult), reads=['s5p'])
        dv(tten(T1, AI, aim, ALU.mult), reads=['s5p'])
        dv(tten(FRE, FRE, T1, ALU.add))
        dv(tten(FRE, FRE, DEN, ALU.mult))
        dv(tten(FIM, AI, are, ALU.mult), reads=['s5p'])
        dv(tten(T1, NR, aim, ALU.mult), reads=['s5p'])
        dv(tten(FIM, FIM, T1, ALU.subtract))
        dv(tten(FIM, FIM, DEN, ALU.mult))
        for k in range(4):
            bre, bim = s5b_s[:, k, 0, :], s5b_s[:, k, 1, :]
            fre, fim = sm[:, 11, k:k + 1], sm[:, 12, k:k + 1]
            P.op('dve', tsc(bb[:, k, 0, :], bim, fim, ALU.mult), reads=['s5b', SMK], writes=[('bb', k)])
            P.op('dve', lambda e, k=k, bre=bre, fre=fre: e.scalar_tensor_tensor(
                out=bb[:, k, 0, :], in0=bre, scalar=fre, in1=bb[:, k, 0, :], op0=ALU.mult, op1=ALU.subtract),
                reads=['s5b', SMK, ('bb', k)], writes=[('bb', k)])
            P.op('dve', tsc(bb[:, k, 1, :], bre, fim, ALU.mult), reads=['s5b', SMK, ('bb', k)], writes=[('bb', k)])
            P.op('dve', lambda e, k=k, bim=bim, fre=fre: e.scalar_tensor_tensor(
                out=bb[:, k, 1, :], in0=bim, scalar=fre, in1=bb[:, k, 1, :], op0=ALU.mult, op1=ALU.add),
                reads=['s5b', SMK, ('bb', k)], writes=[('bb', k)])
            for ri in range(2):
                P.op('dve', lambda e: e.memset(Zf[:], 0.0), writes=['Zf'])
                for g2 in range(2):
                    c0 = 16 * (2 * k + g2)
                    P.op('dve', lambda e, k=k, ri=ri, g2=g2, c0=c0: e.tensor_copy(
                        out=Zf[64 * g2:64 * g2 + 64, c0:c0 + 16], in_=bb[64 * g2:64 * g2 + 64, k, ri, :]),
                        reads=[('bb', k), 'Zf'], writes=['Zf'])
                P.op('pe', lambda e: e.transpose(out=psu[:, 0:128], in_=Zf[:], identity=identf_s[:]),
                     reads=['Zf', 'identf'], writes=['psu'])
                P.op('act', lambda e, k=k, ri=ri: e.activation(out=ZT[:, k, ri, :], in_=psu[:, 0:128], func=AF.Copy),
                     reads=['psu'], writes=['ZT'])
                P.op('dve', lambda e, k=k, ri=ri: e.memset(CT[:, k, ri, :], 0.0), reads=['CT'], writes=['CT'])
                for g2 in range(2):
                    c0 = 16 * (2 * k + g2)
                    P.op('dve', tsc(CT[64 * g2:64 * g2 + 64, k, ri, c0:c0 + 16], s5c_s[64 * g2:64 * g2 + 64, k, ri, :],
                                    1.0 if ri == 0 else -1.0, ALU.mult), reads=['s5c', 'CT'], writes=['CT'])
            tf = sm[:, 3, k:k + 1]
            P.op('dve', tsc(tt[0][:], kp1_s[:], tf, ALU.mult), reads=['kp1', SMK], writes=['tt0'])
            wr = lambda fn: P.op('dve', fn, reads=['tt0', 'tt1', 'tt2'], writes=['tt0', 'tt1', 'tt2'])
            reduce_turns(tt[1][:], tt[0][:], tt[2][:], wr=wr)
            P.op('act', lambda e, k=k: e.activation(out=sinT[:, k, :], in_=tt[1][:], func=AF.Sin, scale=2 * PI),
                 reads=['tt1'], writes=['sinT'])
            wr(tsc(tt[0][:], tt[0][:], 0.25, ALU.add))
            reduce_turns(tt[1][:], tt[0][:], tt[2][:], wr=wr)
            P.op('act', lambda e, k=k: e.activation(out=cosT[:, k, :], in_=tt[1][:], func=AF.Sin, scale=2 * PI),
                 reads=['tt1'], writes=['cosT'])

        P.enabled = True
        def project(col0, width, evac):
            for nt in range(SB // 512):
                ps, pk = psA.get()
                for kc in range(8):
                    P.op('pe', lambda e, ps=ps, kc=kc, nt=nt: e.matmul(
                        ps[0:width, :], w_s[:, kc, col0:col0 + width], xsb[:, kc, nt * 512:(nt + 1) * 512],
                        start=(kc == 0), stop=(kc == 7)), reads=WK + ['xsb'], writes=[pk])
                evac(ps, pk, nt)

        def evac_copy(dst, key, rows=128, scale=1.0, func=AF.Copy, col0=0):
            def f(ps, pk, nt):
                P.op('act', lambda e: e.activation(
                    out=dst[0:rows, col0 + nt * 512:col0 + (nt + 1) * 512], in_=ps[0:rows, :], func=func, scale=scale),
                    reads=[pk], writes=[key])
            return f

        hn_done = []

        pend = []

        def step_pending():
            for g_ in list(pend):
                if next(g_, 'done') == 'done':
                    pend.remove(g_)

        def flush_pending():
            while pend:
                step_pending()

        def head_norm(po, pok, gcol, gate, gatek, ydst, ykey, c0):
            P.op('act', lambda e: e.activation(out=tt[2][:], in_=po[:], func=AF.Copy), reads=[pok], writes=['tt2'])
            if c0 == 0 and not hn_done:
                hn_done.append(1)
                tap('of', tt[2][:], ['tt2'], F32)
            P.op('dve', lambda e: e.tensor_copy(out=ob[:], in_=po[:]), reads=[pok], writes=['ob'])
            pm, pmk = psA.get()
            P.op('pe', lambda e: e.matmul(pm[:], blk_s[:], ob[:], start=True, stop=True),
                 reads=['blk', 'ob'], writes=[pmk])
            yield
            P.op('dve', tten(tt[2][:], tt[2][:], pm[:], ALU.subtract), reads=[pmk, 'tt2'], writes=['tt2'])
            P.op('act', lambda e: e.activation(out=ob[:], in_=tt[2][:], func=AF.Square), reads=['tt2'], writes=['ob'])
            pv, pvk = psA.get()
            P.op('pe', lambda e: e.matmul(pv[:], blk_s[:], ob[:], start=True, stop=True),
                 reads=['blk', 'ob'], writes=[pvk])
            P.op('act', lambda e: e.activation(out=tt[0][:], in_=pv[:], func=AF.Sqrt, bias=1e-5),
                 reads=[pvk], writes=['tt0'])
            yield
            P.op('dve', lambda e: e.reciprocal(out=tt[1][:], in_=tt[0][:]), reads=['tt0'], writes=['tt1'])
            P.op('dve', tten(tt[2][:], tt[2][:], tt[1][:], ALU.mult), reads=['tt1', 'tt2'], writes=['tt2'])
            P.op('dve', lambda e: e.scalar_tensor_tensor(
                out=ydst[:, c0:c0 + 512], in0=tt[2][:], scalar=gains_s[:, gcol:gcol + 1], in1=gate[:, c0:c0 + 512],
                op0=ALU.mult, op1=ALU.mult), reads=['tt2', 'gains', gatek], writes=[ykey])

        outs = []
        for s in range(nsb):
            t0 = s * SB
            ring = (s % 2) * SB
            pring = ((s - 1) % 2) * SB
            if l == 0:
                P.dma('pool', xsb[:, 0:4, :], xTv[:, 0:4, t0:t0 + SB], writes=['xsb'])
                P.dma('pool', xsb[:, 4:8, :], xTv[:, 4:8, t0:t0 + SB], reads=['xsb'], writes=['xsb'])
            else:
                r_ = t0 // half
                q0 = (t0 - r_ * half) // 1024
                for qq in range(2):
                    src = dr['xall'][q0 + qq][r_ * 1024:(r_ + 1) * 1024, :].rearrange("(k p) n -> p k n", p=128)
                    P.dma('sp', xsb[:, :, qq * 1024:(qq + 1) * 1024], src, reads=[('xall', q0 + qq), 'xsb'], writes=['xsb'])

            for (dst, src) in dr['wconv'][l][s::nsb]:
                P.dma('pool', dst, src, writes=['wconv'])
            P.enabled = 'A' in stages
            qA = G[0]
            project(0, 128, evac_copy(qA, 'G0'))
            project(128, 128, evac_copy(kA, 'kA', scale=0.125, col0=ring))
            project(256, 128, evac_copy(vA, 'vA', col0=ring))
            accO, accL = F[0], F[1]
            P.op('dve', lambda e: e.memset(accO[:], 0.0), writes=['F0'])
            P.op('dve', lambda e: e.memset(accL[:], 0.0), writes=['F1'])
            def blk(bi, r, n, c):
                nloc = 16 // r
                base = n * 128 * r + c
                has_prev = (n > 0) or (s > 0)
                kbs = [0, 1] if has_prev else [1]
                if n > 0:
                    pbase = ring + base - 128 * r
                else:
                    pbase = pring + (nloc - 1) * 128 * r + c
                cbase = ring + base
                kcol = {0: pbase, 1: cbase}
                sl = lambda b0: slice(b0, b0 + 127 * r + 1, r)
                pT, pTk = psT.get()
                vs = (psT.i - 1) % 2
                for kb in kbs:
                    P.op('pe', lambda e, pT=pT, kb=kb, cc=kcol[kb], r=r: e.transpose(
                        out=pT[:, kb, :], in_=vA[:, slice(cc, cc + 127 * r + 1, r)], identity=ident_s[:]),
                        reads=['vA', 'ident'], writes=[pTk])
                P.op('act', lambda e, pT=pT, vs=vs, k0=kbs[0]: e.activation(
                    out=vblk[:, vs, k0:2, :], in_=pT[:, k0:2, :], func=AF.Copy),
                    reads=[pTk], writes=[('vblk', vs)])
                ps, pk = psAtt.get()
                psv = ps[:].rearrange("p (h k q) -> p h k q", h=2, k=2)
                for h in range(2):
                    for kb in kbs:
                        P.op('pe', lambda e, psv=psv, h=h, kb=kb, cc=kcol[kb], r=r, base=base: e.matmul(
                            psv[:, h, kb, :], kA[64 * h:64 * h + 64, slice(cc, cc + 127 * r + 1, r)],
                            qA[64 * h:64 * h + 64, slice(base, base + 127 * r + 1, r)], start=True, stop=False),
                            reads=['kA', 'G0'], writes=[pk])
                        P.op('pe', lambda e, psv=psv, h=h, kb=kb, bi=bi: e.matmul(
                            psv[:, h, kb, :], ident_s[:], abias_s[:, bi * 4 + h * 2 + kb, :], start=False, stop=True),
                            reads=['ident', 'abias'], writes=[pk])
                pa = ablk[0] % 2
                ablk[0] += 1
                ptv = ptA[:, pa, :, :].rearrange("p (h k) q -> p h k q", h=2)
                P.op('act', lambda e, psv=psv, ptv=ptv, k0=kbs[0]: e.activation(
                    out=ptv[:, :, k0:2, :], in_=psv[:, :, k0:2, :], func=AF.Exp),
                    reads=[pk], writes=[('ptA', pa)])
                yield
                po, pok = psAtt.get()
                for h in range(2):
                    for idx, kb in enumerate(kbs):
                        P.op('pe', lambda e, po=po, h=h, kb=kb, vs=vs, ptv=ptv, st=(idx == 0), sp=(kb == 1): e.matmul(
                            po[64 * h:64 * h + 64, 0:128], vblk[:, vs, kb, 64 * h:64 * h + 64], ptv[:, h, kb, :],
                            start=st, stop=sp), reads=[('vblk', vs), ('ptA', pa)], writes=[pok])
                    for idx, kb in enumerate(kbs):
                        P.op('pe', lambda e, po=po, h=h, kb=kb, ptv=ptv, st=(idx == 0), sp=(kb == 1): e.matmul(
                            po[64 * h:64 * h + 64, 128:256], ones_s[:, :], ptv[:, h, kb, :],
                            start=st, stop=sp), reads=['ones', ('ptA', pa)], writes=[pok])
                osl = slice(base, base + 127 * r + 1, r)
                P.op('dve', lambda e, po=po, osl=osl: e.tensor_tensor(
                    out=accO[:, osl], in0=accO[:, osl], in1=po[:, 0:128], op=ALU.add),
                    reads=[pok, 'F0'], writes=['F0'])
                P.op('dve', lambda e, po=po, osl=osl: e.tensor_tensor(
                    out=accL[:, osl], in0=accL[:, osl], in1=po[:, 128:256], op=ALU.add),
                    reads=[pok, 'F1'], writes=['F1'])

            def attn_steps():
                prev = None
                for bi, r in enumerate((1, 4, 16)):
                    for n in range(16 // r):
                        for c in range(r):
                            g = blk(bi, r, n, c)
                            next(g)
                            if prev is not None:
                                next(prev, None)
                            prev = g
                            yield
                next(prev, None)
                yield

            uD = G[5]
            project(1280, 128, evac_copy(uD, 'G5'))
            PS5, PS5K = psb[3], ('ps', 3)

            def s5_front(u):
                seg, k = divmod(u, 4)
                ss = slice(seg * 512, (seg + 1) * 512)
                for ri in range(2):
                    P.op('pe', lambda e, ri=ri: e.matmul(PS5[:], ZT[:, k, ri, :], uD[:, ss], start=True, stop=True),
                         reads=['ZT', 'G5'], writes=[PS5K])
                    P.op('act', lambda e, ri=ri: e.activation(out=tb[ri][:], in_=PS5[:], func=AF.Copy),
                         reads=[PS5K], writes=['tb%d' % ri])

            def s5_dve(u):
                seg, k = divmod(u, 4)
                cs_, sn_ = cosT[:, k, :], sinT[:, k, :]
                T0, T1_, T2_, T3_ = tt[0][:], tt[1][:], tt[2][:], tt[3][:]
                br, bi_ = tb[0][:], tb[1][:]
                rho_k = sm[:, 1, k:k + 1].to_broadcast([128, 512])
                xs_ = k % 2
                P.op('dve', tten(T0, br, cs_, ALU.mult), reads=['tb0', 'cosT'], writes=['tt0'])
                P.op('dve', tten(T1_, bi_, sn_, ALU.mult), reads=['tb1', 'sinT'], writes=['tt1'])
                P.op('dve', tten(T0, T0, T1_, ALU.add), reads=['tt0', 'tt1'], writes=['tt0'])
                P.op('dve', tten(T1_, bi_, cs_, ALU.mult), reads=['tb1', 'cosT', 'tt0'], writes=['tt1'])
                P.op('dve', tten(T2_, br, sn_, ALU.mult), reads=['tb0', 'sinT'], writes=['tt2'])
                P.op('dve', tten(T1_, T1_, T2_, ALU.subtract), reads=['tt1', 'tt2'], writes=['tt1'])
                P.op('dve', lambda e: e.tensor_tensor_scan(
                    out=tt[2][:], data0=rho_k, data1=tt[0][:], initial=xend[:, k, 0:1],
                    op0=ALU.mult, op1=ALU.add), reads=['tt0', SMK, 'xend'], writes=['tt2'])
                P.op('dve', lambda e: e.tensor_tensor_scan(
                    out=tt[3][:], data0=rho_k, data1=tt[1][:], initial=xend[:, k, 1:2],
                    op0=ALU.mult, op1=ALU.add), reads=['tt1', SMK, 'xend'], writes=['tt3'])
                P.op('dve', tten(T0, T2_, cs_, ALU.mult), reads=['tt2', 'cosT'], writes=['tt0'])
                P.op('dve', tten(T1_, T3_, sn_, ALU.mult), reads=['tt3', 'sinT'], writes=['tt1'])
                P.op('dve', tten(xri[:, xs_, 0, :], T0, T1_, ALU.subtract), reads=['tt0', 'tt1'], writes=[('xri', xs_, 0)])
                P.op('dve', tten(etmp[:, 0:1], T0[:, 511:512], T1_[:, 511:512], ALU.subtract),
                     reads=['tt0', 'tt1'], writes=['etmp'])
                P.op('dve', tten(T0, T2_, sn_, ALU.mult), reads=['tt2', 'sinT', ('xri', xs_, 0), 'etmp'], writes=['tt0'])
                P.op('dve', tten(T1_, T3_, cs_, ALU.mult), reads=['tt3', 'cosT', ('xri', xs_, 0), 'etmp'], writes=['tt1'])
                P.op('dve', tten(xri[:, xs_, 1, :], T0, T1_, ALU.add), reads=['tt0', 'tt1'], writes=[('xri', xs_, 1)])
                P.op('dve', tten(xend[:, k, 1:2], T0[:, 511:512], T1_[:, 511:512], ALU.add),
                     reads=['tt0', 'tt1', 'xend'], writes=['xend'])
                P.op('dve', lambda e: e.tensor_copy(out=xend[:, k, 0:1], in_=etmp[:, 0:1]),
                     reads=['etmp', 'xend'], writes=['xend'])

            def s5_back(u):
                seg, k = divmod(u, 4)
                ss = slice(seg * 512, (seg + 1) * 512)
                xs_ = k % 2
                for ri in range(2):
                    P.op('pe', lambda e, ri=ri: e.matmul(
                        psu[:], CT[:, k, ri, :], xri[:, xs_, ri, :], start=(k == 0 and ri == 0), stop=(k == 3 and ri == 1)),
                        reads=['CT', ('xri', xs_, ri)], writes=['psu'])
                if k == 3:
                    P.op('dve', lambda e: e.scalar_tensor_tensor(
                        out=tg[:], in0=uD[:, ss], scalar=dvec_s[:, 0:1], in1=psu[:], op0=ALU.mult, op1=ALU.add),
                        reads=['psu', 'G5', 'dvec'], writes=['tg'])
                    P.op('act', lambda e: e.activation(out=Y[3][:, ss], in_=tg[:], func=AF.Gelu),
                         reads=['tg'], writes=['Y3'])

            ga = attn_steps()
            nunit = (SB // 512) * 4
            for u in range(nunit):
                s5_front(u)
                next(ga, None)
                s5_dve(u)
                if u > 0:
                    s5_back(u - 1)
                next(ga, None)
                next(ga, None)
            s5_back(nunit - 1)
            for _ in ga:
                pass
            P.op('dve', lambda e: e.reciprocal(out=accL[:], in_=accL[:]), reads=['F1'], writes=['F1'])
            P.op('dve', tten(Y[0][:], accO[:], accL[:], ALU.mult), reads=['F0', 'F1'], writes=['Y0'])

            P.enabled = 'B' in stages
            qB, kB, vB, sgB = G[0], G[1], G[2], G[3]
            project(384, 128, evac_copy(qB, 'G0'))
            project(512, 128, evac_copy(kB, 'G1'))
            project(640, 128, evac_copy(vB, 'G2'))
            project(768, 128, evac_copy(sgB, 'G3', func=AF.Silu))
            def ret_chunk(c):
                cs = slice(c * 128, (c + 1) * 128)
                sl2 = c % 2
                rs = c % 2
                oc = slice((c % 4) * 128, (c % 4 + 1) * 128)
                po, pok = pacc, 'pacc'
                for h in range(2):
                    ps, pk = psA.get()
                    P.op('pe', lambda e, ps=ps, h=h: e.matmul(
                        ps[:, 0:128], kB[64 * h:64 * h + 64, cs], qB[64 * h:64 * h + 64, cs], start=True, stop=True),
                        reads=['G0', 'G1'], writes=[pk])
                    P.op('dve', lambda e, ps=ps, h=h: e.tensor_tensor(
                        out=ptB[:, sl2, h, :], in0=ps[:, 0:128], in1=rtab_s[:, h, :], op=ALU.mult),
                        reads=[pk, 'rtab'], writes=[('ptB', sl2, h)])
                pT, pTk = psT.get()
                P.op('pe', lambda e: e.transpose(out=pT[:, 0, :], in_=vB[:, cs], identity=ident_s[:]),
                     reads=['G2', 'ident'], writes=[pTk])
                P.op('pe', lambda e: e.transpose(out=pT[:, 1, :], in_=kB[:, cs], identity=ident_s[:]),
                     reads=['G1', 'ident'], writes=[pTk])
                P.op('act', lambda e: e.activation(out=vtok[:, sl2, :], in_=pT[:, 0, :], func=AF.Copy),
                     reads=[pTk], writes=[('vtok', sl2)])
                P.op('act', lambda e: e.activation(out=ktok[:, sl2, :], in_=pT[:, 1, :], func=AF.Copy),
                     reads=[pTk], writes=[('ktok', sl2)])
                P.op('dve', lambda e: e.tensor_tensor(
                    out=ktok[:, sl2, :], in0=ktok[:, sl2, :], in1=rtab_s[:, 3, :], op=ALU.mult),
                    reads=[('ktok', sl2), 'rtab'], writes=[('ktok', sl2)])
                P.op('dve', lambda e: e.tensor_tensor(
                    out=qdec[:, sl2, :], in0=qB[:, cs], in1=rtab_s[:, 2, :], op=ALU.mult),
                    reads=['G0', 'rtab'], writes=[('qdec', sl2)])
                yield
                for h in range(2):
                    hs = slice(64 * h, 64 * h + 64)
                    P.op('pe', lambda e, hs=hs: e.matmul(
                        psu[hs, 0:64], ktok[:, sl2, hs], vtok[:, sl2, hs], start=True, stop=True),
                        reads=[('ktok', sl2), ('vtok', sl2)], writes=['psu'])
                P.op('dve', lambda e: e.scalar_tensor_tensor(
                    out=Rr[:], in0=Rr[:], scalar=rtab_s[:, 4, 0:1], in1=psu[:, 0:64], op0=ALU.mult, op1=ALU.add),
                    reads=['psu', 'Rr', 'rtab'], writes=['Rr'])
                P.op('act', lambda e: e.activation(out=Rrb[:, 1 - rs, :], in_=Rr[:], func=AF.Copy),
                     reads=['Rr'], writes=[('Rrb', 1 - rs)])
                for h in range(2):
                    hs = slice(64 * h, 64 * h + 64)
                    P.op('pe', lambda e, hs=hs, h=h: e.matmul(
                        po[hs, oc], vtok[:, sl2, hs], ptB[:, sl2, h, :], start=True, stop=False),
                        reads=[('vtok', sl2), ('ptB', sl2, h)], writes=[pok])
                    P.op('pe', lambda e, hs=hs: e.matmul(
                        po[hs, oc], Rrb[hs, rs, :], qdec[hs, sl2, :], start=False, stop=True),
                        reads=[('Rrb', rs), ('qdec', sl2)], writes=[pok])
                if c % 4 == 3:
                    hn = head_norm(po, pok, 0, sgB, 'G3', Y[1], 'Y1', (c // 4) * 512)
                    next(hn)
                    pend.append(hn)
                yield

            prev = None
            for c in range(SB // 128):
                g = ret_chunk(c)
                next(g)
                step_pending()
                if prev is not None:
                    next(prev, None)
                prev = g
            next(prev, None)
            flush_pending()

            P.enabled = 'C' in stages
            qC, kC, laF, bcF = F[0], F[1], F[2], F[3]
            vC, sgC, glow, qin, kout, kdc = G[0], G[1], G[2], G[3], G[4], G[5]
            project(896, 64, evac_copy(qC, 'F0', rows=64))
            project(960, 64, evac_copy(kC, 'F1', rows=64))
            project(1024, 128, evac_copy(vC, 'G0'))
            project(1152, 128, evac_copy(sgC, 'G1', func=AF.Silu))
            project(1408, 16, evac_copy(glow, 'G2', rows=16))
            for nt in range(SB // 512):
                ns = slice(nt * 512, (nt + 1) * 512)
                ps, pk = psA.get()
                P.op('pe', lambda e, ps=ps, ns=ns: e.matmul(ps[0:64, :], wgate_s[:, :], glow[0:16, ns], start=True, stop=True),
                     reads=['wgate', 'G2'], writes=[pk])
                P.op('act', lambda e, ps=ps, ns=ns: e.activation(
                    out=laF[0:64, ns], in_=ps[0:64, :], func=AF.Exp, scale=-1.0, bias=bgate_s[:, 1:2]),
                    reads=[pk, 'nbgate'], writes=['F2'])
            P.op('act', lambda e: e.activation(out=laF[0:64, :], in_=laF[0:64, :], func=AF.Ln, bias=1.0),
                 reads=['F2'], writes=['F2'])
            P.op('dve', tsc(laF[0:64, :], laF[0:64, :], -1.0 / 16.0, ALU.mult), reads=['F2'], writes=['F2'])
            P.op('dve', lambda e: e.tensor_tensor_scan(
                out=bcF[0:64, :], data0=rmask_s[0:64, :], data1=laF[0:64, :], initial=0.0, op0=ALU.mult, op1=ALU.add),
                reads=['F2', 'rmask'], writes=['F3'])
            P.op('act', lambda e: e.activation(out=laF[0:64, :], in_=bcF[0:64, :], func=AF.Exp), reads=['F3'], writes=['F2'])
            P.op('dve', lambda e: e.scalar_tensor_tensor(
                out=qin[0:64, :], in0=qC[0:64, :], scalar=32.0 ** -0.5, in1=laF[0:64, :], op0=ALU.mult, op1=ALU.mult),
                reads=['F0', 'F2'], writes=['G3'])
            P.op('act', lambda e: e.activation(out=laF[0:64, :], in_=bcF[0:64, :], func=AF.Exp, scale=-1.0),
                 reads=['F3', 'G3'], writes=['F2'])
            P.op('dve', tten(kout[0:64, :], kC[0:64, :], laF[0:64, :], ALU.mult), reads=['F1', 'F2'], writes=['G4'])
            blast = bcF[0:64, :].rearrange("p (c j) -> p c j", j=64)[:, :, 63]
            P.op('act', lambda e: e.activation(out=ebl[:, :], in_=blast, func=AF.Exp), reads=['F3'], writes=['ebl'])
            P.op('dve', lambda e: e.tensor_tensor(
                out=kdc[0:64, :].rearrange("p (c j) -> p c j", j=64),
                in0=kout[0:64, :].rearrange("p (c j) -> p c j", j=64),
                in1=ebl[:, :].unsqueeze(2).to_broadcast([64, SB // 64, 64]), op=ALU.mult),
                reads=['G4', 'ebl'], writes=['G5'])
            def gla_chunk(c):
                cs = slice(c * 128, (c + 1) * 128)
                sl2 = c % 2
                oc0 = (c % 4) * 128
                po, pok = pacc, 'pacc'
                for h in range(2):
                    ps, pk = psA.get()
                    P.op('pe', lambda e, ps=ps, h=h: e.matmul(
                        ps[:, 0:128], kout[32 * h:32 * h + 32, cs], qin[32 * h:32 * h + 32, cs], start=True, stop=True),
                        reads=['G3', 'G4'], writes=[pk])
                    P.op('dve', lambda e, ps=ps, h=h: e.tensor_tensor(
                        out=ptB[:, sl2, h, :], in0=ps[:, 0:128], in1=gmask_s[:, h, :], op=ALU.mult),
                        reads=[pk, 'gmask'], writes=[('ptB', sl2, h)])
                pT, pTk = psT.get()
                P.op('pe', lambda e: e.transpose(out=pT[:, 0, :], in_=vC[:, cs], identity=ident_s[:]),
                     reads=['G0', 'ident'], writes=[pTk])
                P.op('pe', lambda e: e.transpose(out=pT[:, 1, 0:64], in_=kdc[0:64, cs], identity=ident_s[0:64, 0:64]),
                     reads=['G5', 'ident'], writes=[pTk])
                P.op('act', lambda e: e.activation(out=vtok[:, sl2, :], in_=pT[:, 0, :], func=AF.Copy),
                     reads=[pTk], writes=[('vtok', sl2)])
                P.op('dve', lambda e: e.tensor_copy(out=ktok[:, sl2, 0:64], in_=pT[:, 1, 0:64]),
                     reads=[pTk], writes=[('ktok', sl2)])
                yield

                def update(cc):
                    ci = 2 * c + cc
                    ts_ = slice(64 * cc, 64 * cc + 64)
                    for h in range(2):
                        ks = slice(32 * h, 32 * h + 32)
                        hs = slice(64 * h, 64 * h + 64)
                        P.op('pe', lambda e, ks=ks, hs=hs: e.matmul(
                            psu[ks, 64:128], ktok[ts_, sl2, ks], vtok[ts_, sl2, hs], start=True, stop=True),
                            reads=[('ktok', sl2), ('vtok', sl2)], writes=['psu'])
                    P.op('dve', lambda e: e.scalar_tensor_tensor(
                        out=Rg[:], in0=Rg[:], scalar=ebl[:, ci:ci + 1], in1=psu[0:64, 64:128], op0=ALU.mult, op1=ALU.add),
                        reads=['psu', 'Rg', 'ebl'], writes=['Rg'])
                    for h in range(2):
                        P.op('act', lambda e, h=h: e.activation(
                            out=Rgb[32 * h:32 * h + 32, 1 - cc, 64 * h:64 * h + 64], in_=Rg[32 * h:32 * h + 32, :], func=AF.Copy),
                            reads=['Rg'], writes=[('Rgb', 1 - cc)])

                def cross(cc):
                    c64 = slice(c * 128 + cc * 64, c * 128 + cc * 64 + 64)
                    P.op('pe', lambda e: e.matmul(
                        po[:, oc0 + cc * 64:oc0 + cc * 64 + 64], Rgb[:, cc, :], qin[0:64, c64], start=False, stop=(cc == 1)),
                        reads=[('Rgb', cc), 'G3'], writes=[pok])

                update(0)
                for h in range(2):
                    hs = slice(64 * h, 64 * h + 64)
                    P.op('pe', lambda e, hs=hs, h=h: e.matmul(
                        po[hs, oc0:oc0 + 128], vtok[:, sl2, hs], ptB[:, sl2, h, :], start=True, stop=False),
                        reads=[('vtok', sl2), ('ptB', sl2, h)], writes=[pok])
                cross(0)
                update(1)
                cross(1)
                if c % 4 == 3:
                    hn = head_norm(po, pok, 1, sgC, 'G1', Y[2], 'Y2', (c // 4) * 512)
                    next(hn)
                    pend.append(hn)
                yield

            prev = None
            for c in range(SB // 128):
                g = gla_chunk(c)
                next(g)
                step_pending()
                if prev is not None:
                    next(prev, None)
                prev = g
            next(prev, None)
            flush_pending()

            P.enabled = True
            for m in range(4):
                P.dma('sp', dr['ybuf'][l][s][m * 128:(m + 1) * 128, :], Y[m][:], reads=['Y%d' % m], writes=[('ybuf', s)])
            P.op('pool', lambda e, s=s: e.collective_compute(
                "AllGather", ALU.bypass, replica_groups=PAIRS, ins=[dr['ybuf'][l][s]], outs=[dr['yall'][l][s]]),
                reads=[('ybuf', s)], writes=[('yall', s)], coll=True)


def prep_A_consts(p):
    f = np.float32
    q = np.arange(128)[None, :]
    k = np.arange(128)[:, None]
    ab = np.zeros((128, 12, 128), f)
    for bi, r in enumerate((1, 4, 16)):
        for h in range(2):
            slope = 2.0 ** (-2.0 * (2 * p + h + 1))
            prev = np.where(k >= q, -slope * r * (q - k + 128), -1e30)
            cur = np.where(k <= q, -slope * r * (q - k), -1e30)
            ab[:, bi * 4 + h * 2 + 0, :] = prev
            ab[:, bi * 4 + h * 2 + 1, :] = cur
    rt = np.zeros((128, 5, 128), np.float64)
    n = np.arange(128)[None, :]
    m = np.arange(128)[:, None]
    for h in range(2):
        lg = np.log(1.0 - 2.0 ** (-5.0 - (2 * p + h)))
        rt[:, h, :] = np.where(n >= m, np.exp(np.maximum(n - m, 0) * lg), 0.0) * 0.125
        rt[64 * h:64 * h + 64, 2, :] = np.exp((np.arange(128) + 1.0) * lg)[None, :]
        rt[:, 3, 64 * h:64 * h + 64] = (np.exp((127 - np.arange(128)) * lg) * 0.125)[:, None]
        rt[64 * h:64 * h + 64, 4, :] = np.exp(128 * lg)
    gm = np.where((n >= m) & ((n // 64) == (m // 64)), 1.0, 0.0)
    blk = np.zeros((128, 128), f)
    blk[:64, :64] = 1.0 / 64
    blk[64:, 64:] = 1.0 / 64
    rm = np.ones((128, SB), f)
    rm[:, ::64] = 0.0
    return dict(abias=ab, rtab=rt.astype(f), gmask=np.stack([gm, gm], 1).astype(f), ident=np.eye(128, dtype=f),
                blk=blk, kp1=np.broadcast_to(np.arange(1, 513, dtype=f), (128, 512)).copy(), rmask=rm)


def prep_A_weights(inp, l, p):
    f = np.float32
    w = inp['w_in'][l]
    offs = np.cumsum([0, 256, 256, 256, 256, 256, 256, 256, 128, 128, 256, 16, 256, 256])
    seg = lambda i, a, b: w[:, offs[i] + a:offs[i] + b]
    hp = lambda i, wd: seg(i, p * wd, (p + 1) * wd)
    cols = [hp(0, 128), hp(1, 128), hp(2, 128), hp(3, 128), hp(4, 128), hp(5, 128), hp(6, 128),
            hp(7, 64), hp(8, 64), hp(9, 128), hp(11, 128), hp(12, 128), seg(10, 0, 16)]
    wc = np.concatenate(cols, axis=1)
    assert wc.shape[1] == NW
    r = {}
    r['w_in'] = np.ascontiguousarray(wc.reshape(8, 128, NW).transpose(1, 0, 2))
    gs = slice(8 * p, 8 * p + 8)
    tile = lambda a: np.ascontiguousarray(a[gs].reshape(4, 128, *a.shape[2:]))
    are, aim = tile(inp['s5_a_re'][l]), tile(inp['s5_a_im'][l])
    ldt = tile(np.broadcast_to(inp['s5_log_dt'][l][:, None], (16, 64)))
    r['s5p'] = np.ascontiguousarray(np.stack([are, aim, ldt], -1).transpose(1, 0, 2)).astype(f)
    bre, bim = tile(inp['s5_b_re'][l]), tile(inp['s5_b_im'][l])
    r['s5b'] = np.ascontiguousarray(np.stack([bre, bim], 2).transpose(1, 0, 2, 3)).astype(f)
    cre = tile(inp['s5_c_re'][l].transpose(0, 2, 1))
    cim = tile(inp['s5_c_im'][l].transpose(0, 2, 1))
    r['s5c'] = np.ascontiguousarray(np.stack([cre, cim], 2).transpose(1, 0, 2, 3)).astype(f)
    r['dvec'] = np.ascontiguousarray(inp['s5_d'][l][gs].reshape(128, 1)).astype(f)
    r['wgate'] = np.ascontiguousarray(inp['gla_w_gate'][l][:, 64 * p:64 * p + 64])
    r['bgate'] = np.ascontiguousarray(inp['gla_b_gate'][l][64 * p:64 * p + 64].reshape(64, 1))
    r['gains'] = np.ascontiguousarray(np.stack([inp['ret_gn_g'][l][128 * p:128 * p + 128],
                                                inp['gla_gn_g'][l][128 * p:128 * p + 128]], 1)).astype(f)
    return r


def build_fused(S=8192, ncores=8, depth=2):
    nc = bass.Bass("TRN2", target_bir_lowering=False)
    pairs = [[2 * i, 2 * i + 1] for i in range(ncores // 2)]
    global PAIRS
    PAIRS = pairs
    half = S // 2
    nsb = S // SB
    nq = half // 1024
    L = depth
    di = lambda name, shape, dt=F32: nc.dram_tensor(name, shape, dt, kind="ExternalInput").ap()
    dn = lambda name, shape, dt=BF16: nc.dram_tensor(name, shape, dt, kind="Internal").ap()
    dr = {}
    dr['xT_full'] = di("xT_full", [1024, S])
    dr['xT_own'] = di("xT_own", [1024, half])
    dr['memT'] = di("memT", [128, 8, 256])
    dr['sel'] = di("sel", [128, 2])
    for k, shp in (('abias', [128, 12, 128]), ('rtab', [128, 5, 128]), ('gmask', [128, 2, 128]), ('ident', [128, 128]),
                   ('blk', [128, 128]), ('kp1', [128, 512]), ('rmask', [128, SB])):
        dr[k] = di(k, shp)
    for k, shp in (('w_inA', [128, 8, NW]), ('s5p', [128, 4, 3]), ('s5b', [128, 4, 2, 16]), ('s5c', [128, 4, 2, 16]),
                   ('dvec', [128, 1]), ('wgate', [16, 64]), ('bgate', [64, 1]), ('gains', [128, 2]),
                   ('w_out', [128, 8, 1024]), ('w_q', [128, 8, 1024]), ('w_o', [128, 8, 1024]), ('w_kv', [4, 128, 8, 512]),
                   ('w_glu', [128, 2, 256]), ('w_gu', [11, 128, 2, 8, 256]), ('w_d', [2, 11, 128, 2, 512]), ('vecs', [128, 50])):
        dr[k] = di(k, [L] + shp)
    wshapes = dict(w_out=[128, 8, 1024], w_q=[128, 8, 1024], w_o=[128, 8, 1024], w_kv=[4, 128, 8, 512],
                   w_glu=[128, 2, 256], w_gu=[11, 128, 2, 8, 256], w_d=[2, 11, 128, 2, 512])
    dr['wconv'] = [[] for _ in range(L)]
    for k, shp in wshapes.items():
        dr['wb_' + k] = dn("wb_" + k, [L] + shp)
        for l in range(L):
            src, dst = dr[k][l], dr['wb_' + k][l]
            if k in ('w_out', 'w_q', 'w_o'):
                pieces = [(dst[:, 4 * h:4 * h + 4, :], src[:, 4 * h:4 * h + 4, :]) for h in range(2)]
            elif k == 'w_kv':
                pieces = [(dst[i], src[i]) for i in range(4)]
            elif k == 'w_gu':
                pieces = [(dst[i], src[i]) for i in range(11)]
            elif k == 'w_d':
                pieces = [(dst[i], src[i]) for i in range(2)]
            else:
                pieces = [(dst, src)]
            dr['wconv'][l] += pieces
    dr['xo'] = nc.dram_tensor("xo", [1024, half], F32, kind="ExternalOutput").ap()
    dr['ybuf'] = [[dn("ybuf_%d_%d" % (l, s), [512, SB]) for s in range(nsb)] for l in range(L)]
    dr['yall'] = [[dn("yall_%d_%d" % (l, s), [1024, SB]) for s in range(nsb)] for l in range(L)]
    dr['xown'] = dn("xown", [1024, half], F32)
    dr['xbf'] = [dn("xbf_%d" % q, [1024, 1024]) for q in range(nq)]
    dr['xall'] = [dn("xall_%d" % q, [2048, 1024]) for q in range(nq)]
    with ExitStack() as es_outer:
        P = Prog(nc, es_outer)
        for l in range(L):
            with ExitStack() as es:
                emit_A(nc, P, es, l, dr, S)
                P.emit_block()
            with ExitStack() as es:
                emit_D(nc, P, es, l, dr, half, last=(l == L - 1))
                P.emit_block()
    return nc


def make_in_maps(inp, ncores=8):
    x = inp['x']
    S = x.shape[1]
    half = S // 2
    depth = inp['w_in'].shape[0]
    constsA = [prep_A_consts(p) for p in range(2)]
    wA = [[prep_A_weights(inp, l, p) for l in range(depth)] for p in range(2)]
    wD = [prep_D_weights(inp, l) for l in range(depth)]
    wDs = {k: np.stack([wD[l][k] for l in range(depth)]) for k in wD[0]}
    wAs = [{('w_inA' if k == 'w_in' else k): np.stack([wA[p][l][k] for l in range(depth)]) for k in wA[p][0]} for p in range(2)]
    maps = []
    for c in range(ncores):
        b, p = c // 2, c % 2
        xT = np.ascontiguousarray(x[b].T)
        m = {}
        m.update(constsA[p])
        m.update(wAs[p])
        m.update(wDs)
        m['xT_full'] = xT
        m['xT_own'] = np.ascontiguousarray(xT[:, p * half:(p + 1) * half])
        m['memT'] = np.ascontiguousarray(inp['mem'][b].T.reshape(8, 128, -1).transpose(1, 0, 2))
        sel = np.zeros((128, 2), np.float32)
        sel[:, p] = 1.0
        m['sel'] = sel
        maps.append(m)
    return maps


def kernel(**inputs):
    inp = {k: np.asarray(v) for k, v in inputs.items()}
    x = inp['x']
    nb, S, D = x.shape
    ncores = 2 * nb
    half = S // 2
    nc = build_fused(S, ncores, inp['w_in'].shape[0])
    maps = make_in_maps(inp, ncores)
    res = run_bass_kernel_spmd(nc, maps, core_ids=list(range(ncores)))
    out = np.zeros((nb, S, D), np.float32)
    for c in range(ncores):
        out[c // 2, (c % 2) * half:(c % 2 + 1) * half] = np.asarray(res.results[c]['xo']).T
    return out
```

```python
import math
import numpy as np
import ml_dtypes
from contextlib import ExitStack
import concourse.bass as bass
import concourse.mybir as mybir
from concourse.bass_utils import run_bass_kernel_spmd

F32 = mybir.dt.float32
BF16 = mybir.dt.bfloat16
I32 = mybir.dt.int32
AF = mybir.ActivationFunctionType
ALU = mybir.AluOpType
AX = mybir.AxisListType

ENGS = ('pe', 'act', 'dve', 'pool', 'sp')
WIN = 2000
NP_DMA = 6
WIN_D = 120


def is_psum_key(b):
    n = b[0] if isinstance(b, tuple) else b
    return isinstance(n, str) and n.startswith('ps') or n == 'pacc'


class Prog:
    def __init__(self, nc, es):
        self.nc = nc
        self.es = es
        self.ops = []
        self.last_w = {}
        self.readers = {}
        self.enabled = True
        self.start = 0
        self.sems = {}
        self.ccount = {e: 0 for e in ENGS}
        self.dcount = {e: 0 for e in ENGS}
        self.comp = {}
        self.throttle = {}
        self.waited = {e: {} for e in ENGS}

    def op(self, eng, fn, reads=(), writes=(), dma=False, coll=False):
        if not self.enabled:
            return None
        i = len(self.ops)
        reads = list(reads)
        writes = list(writes)
        for b in reads:
            if is_psum_key(b) and b not in writes:
                writes.append(b)
        deps = set()
        for b in reads:
            w = self.last_w.get(b)
            if w is not None:
                deps.add(w)
        for b in writes:
            w = self.last_w.get(b)
            if w is not None:
                deps.add(w)
            for r in self.readers.get(b, ()):
                deps.add(r)
        for b in reads:
            self.readers.setdefault(b, []).append(i)
        for b in writes:
            self.last_w[b] = i
            self.readers[b] = []
        deps.discard(i)
        self.ops.append(dict(eng=eng, fn=fn, deps=deps, dma=dma, coll=coll))
        return i

    def dma(self, eng, out, in_, reads=(), writes=(), **kw):
        return self.op(eng, lambda e: e.dma_start(out=out, in_=in_, **kw), reads, writes, dma=True)

    def _sem(self, key):
        if key not in self.sems:
            self.sems[key] = self.es.enter_context(self.nc.semaphore("s_" + "_".join(str(x) for x in key)))
        return self.sems[key]

    def emit_block(self):
        nc = self.nc
        ops = self.ops
        lo = self.start
        asyncs = [i for i in range(lo, len(ops)) if ops[i]['dma'] or ops[i]['coll']]
        ops.append(dict(eng='sp', fn=None, deps=set(asyncs), dma=False, coll=False))
        hi = len(ops)

        def skip(od, o):
            return od['eng'] == 'pe' and o['eng'] == 'pe' and not od['dma'] and not o['dma']

        need = set()
        for i in range(lo, hi):
            for d in ops[i]['deps']:
                if not skip(ops[d], ops[i]):
                    need.add(d)
        for i in range(lo, hi):
            o = ops[i]
            e = o['eng']
            if o['dma']:
                j = self.dcount[e]
                self.dcount[e] += 1
                s, r = j % NP_DMA, j // NP_DMA
                key = ('d', e, s, r // WIN_D)
                self.comp[i] = (key, 16 * (r % WIN_D + 1))
                if r >= 1 and (r % WIN_D) != 0:
                    self.throttle[i] = (key, 16 * (r % WIN_D))
                elif r >= 1:
                    self.throttle[i] = (('d', e, s, (r - 1) // WIN_D), 16 * ((r - 1) % WIN_D + 1))
            elif i in need:
                k = self.ccount[e]
                self.ccount[e] += 1
                self.comp[i] = (('c', e, k // WIN), k % WIN + 1)
        comp, throttle = self.comp, self.throttle
        per = {e: [i for i in range(lo, hi) if ops[i]['eng'] == e] for e in ENGS}
        with nc.Block() as block:
            def run(e, eng):
                waited = self.waited[e]
                for i in per[e]:
                    o = ops[i]
                    ws = [comp[d] for d in sorted(o['deps']) if not skip(ops[d], o)]
                    if i in throttle:
                        ws.append(throttle[i])
                    for key, val in ws:
                        if waited.get(key, 0) >= val:
                            continue
                        waited[key] = val
                        eng.wait_ge(self._sem(key), val)
                    if o['fn'] is None:
                        continue
                    ins = o['fn'](eng)
                    if i in comp:
                        key, val = comp[i]
                        ins.then_inc(self._sem(key), 16 if o['dma'] else 1)

            @block.tensor
            def _(eng):
                run('pe', eng)

            @block.scalar
            def _(eng):
                run('act', eng)

            @block.vector
            def _(eng):
                run('dve', eng)

            @block.gpsimd
            def _(eng):
                run('pool', eng)

            @block.sync
            def _(eng):
                run('sp', eng)
        self.start = hi
        self.last_w.clear()
        self.readers.clear()


ALPHA = float((2 * 2) ** 0.25)
LN_EPS = 1e-5
NT = 512


class Rot:
    def __init__(self, slots):
        self.slots = slots
        self.i = 0

    def get(self):
        s = self.slots[self.i % len(self.slots)]
        self.i += 1
        return s


def emit_D(nc, P, es, l, dr, TOK, last):
    ntile = TOK // NT
    memT, vecs = dr['memT'], dr['vecs'][l]
    w_out, w_q, w_o, w_kv, w_glu, w_gu, w_d = (dr['wb_' + k][l] for k in ('w_out', 'w_q', 'w_o', 'w_kv', 'w_glu', 'w_gu', 'w_d'))
    xsrc = dr['xT_own'] if l == 0 else dr['xown']
    xTv = xsrc.rearrange("(k p) n -> p k n", p=128)
    xdst = dr['xo'] if last else dr['xown']
    xov = xdst.rearrange("(k p) n -> p k n", p=128)
    sfx = "_D%d" % l
    if True:
        sb = lambda name, shape, dt=BF16: es.enter_context(nc.sbuf_tensor(name + sfx, shape, dt))
        wo_s = sb("wo_s", [128, 8, 1024])
        wq_s = sb("wq_s", [128, 8, 1024])
        woo_s = sb("woo_s", [128, 8, 1024])
        wglu_s = sb("wglu_s", [128, 2, 256])
        kT_s = sb("kT_s", [128, 8, 256])
        V_s = sb("V_s", [128, 2, 1024])
        memT_s = sb("memT_s", [128, 8, 256])
        vec_s = sb("vec_s", [128, 50], F32)
        ones_m = sb("ones_m", [128, 128])
        ones_1 = sb("ones_1", [128, 128])
        xs = sb("xs", [128, 8, NT], F32)
        xb = sb("xb", [128, 8, NT])
        ys = sb("ys", [128, 8, NT])
        ysA = sb("ysA", [128, 8, NT])
        ysB = sb("ysB", [128, 8, NT])
        sel_s = sb("sel_s", [128, 2], F32)
        dmy = sb("dmy", [128, 2], F32)
        yd = sb("yd", [128, 2, NT])
        qs = sb("qs", [128, 8, NT])
        os_ = sb("os", [128, 8, NT])
        hid = sb("hid", [128, 22, NT])
        wst = sb("wst", [128, 3, 4096])
        wdst = sb("wdst", [128, 3, 1024])
        zb = sb("zb", [128, 8, NT])
        rstd = sb("rstd", [128, 2, NT], F32)
        pt = sb("pt", [128, 4, NT])
        rl = sb("rl", [128, 2, NT], F32)
        sg = sb("sg", [128, 2, NT], F32)
        psb = [es.enter_context(nc.psum_tensor("ps%d" % i + sfx, [128, 512], F32)) for i in range(8)]

        psA = Rot([(psb[i], ('ps', i)) for i in range(4)])
        psD = [(psb[4 + i], ('ps', 4 + i)) for i in range(4)]
        psAtt = Rot([(psb[i], ('ps', i)) for i in range(8)])
        evac_i = [0]

        P.op('dve', lambda e: e.memset(ones_m[:], 1.0 / 1024.0), writes=['ones_m'])
        P.op('dve', lambda e: e.memset(ones_1[:], 1.0), writes=['ones_1'])
        P.dma('sp', vec_s[:], vecs, writes=['vec'])
        P.dma('sp', sel_s[:], dr['sel'], writes=['sel'])
        P.dma('pool', memT_s[:], memT, writes=['memT'])
        P.dma('sp', wglu_s[:], w_glu, writes=['wglu'])
        for (dst, src, key) in ((wo_s, w_out, 'wo'), (wq_s, w_q, 'wq'), (woo_s, w_o, 'woo')):
            for h in range(2):
                P.dma('sp', dst[:, 4 * h:4 * h + 4, :], src[:, 4 * h:4 * h + 4, :], writes=[(key, h)])
        WO = [('wo', 0), ('wo', 1)]
        WQ = [('wq', 0), ('wq', 1)]
        WOO = [('woo', 0), ('woo', 1)]

        for pc in range(4):
            slot = pc % 2
            wv = wst[:, slot, :].rearrange("p (k n) -> p k n", k=8)
            P.dma('sp', wv, w_kv[pc], writes=[('wst', slot)])
            if pc < 2:
                for j in range(4):
                    ps, pk = psA.get()
                    for kc in range(8):
                        P.op('pe', lambda e, ps=ps, wv=wv, kc=kc, j=j: e.matmul(
                            ps[:, 0:256], wv[:, kc, j * 128:(j + 1) * 128], memT_s[:, kc, :],
                            start=(kc == 0), stop=(kc == 7)), reads=[('wst', slot), 'memT'], writes=[pk])
                    P.op('act', lambda e, ps=ps, mc=4 * pc + j: e.activation(
                        out=kT_s[:, mc, :], in_=ps[:, 0:256], func=AF.Copy), reads=[pk], writes=['kT'])
            else:
                for mm in range(2):
                    ps, pk = psA.get()
                    for kc in range(8):
                        P.op('pe', lambda e, ps=ps, wv=wv, kc=kc, mm=mm: e.matmul(
                            ps[:], memT_s[:, kc, mm * 128:(mm + 1) * 128], wv[:, kc, :],
                            start=(kc == 0), stop=(kc == 7)), reads=[('wst', slot), 'memT'], writes=[pk])
                    P.op('act', lambda e, ps=ps, mm=mm, c0=(pc - 2) * 512: e.activation(
                        out=V_s[:, mm, c0:c0 + 512], in_=ps[:], func=AF.Copy), reads=[pk], writes=['V'])

        def resid(ps, pk, mc):
            P.op('dve', lambda e: e.scalar_tensor_tensor(
                out=xs[:, mc, :], in0=xs[:, mc, :], scalar=ALPHA, in1=ps[:], op0=ALU.mult, op1=ALU.add),
                reads=[pk, ('xs', mc)], writes=[('xs', mc)])

        xalias = hid[:].rearrange("p a b -> p (a b)").bitcast(F32)[:, 0:8 * NT].rearrange("p (c n) -> p c n", n=NT)
        akeys = lambda c: [('hid', 2 * c), ('hid', 2 * c + 1)]

        def load_x(t):
            ts = slice(t * NT, (t + 1) * NT)
            for h in range(2):
                P.dma('sp', xs[:, 4 * h:4 * h + 4, :], xTv[:, 4 * h:4 * h + 4, ts],
                      writes=[('xs', c) for c in range(4 * h, 4 * h + 4)])

        def load_y(t):
            sA = (t * NT) // 2048
            sB = (TOK + t * NT) // 2048
            cs0 = (t * NT) % 2048
            for (dst, sidx, key) in ((ysA, sA, 'ysA'), (ysB, sB, 'ysB')):
                for r_ in range(2):
                    src = dr['yall'][l][sidx][r_ * 512:(r_ + 1) * 512, cs0:cs0 + NT].rearrange("(m p) n -> p m n", p=128)
                    dstv = dst[:].rearrange("p (m r) n -> p m r n", r=2)[:, :, r_, :]
                    P.dma('sp', dstv, src, reads=[('yall', sidx)], writes=[(key, r_)])
            for c in range(8):
                P.op('dve', lambda e, c=c: e.tensor_scalar(out=ys[:, c, :], in0=ysA[:, c, :], scalar1=sel_s[:, 0:1],
                                                          scalar2=None, op0=ALU.mult),
                     reads=[('ysA', c % 2), 'sel'], writes=[('ys', c)])
                P.op('dve', lambda e, c=c: e.scalar_tensor_tensor(
                    out=ys[:, c, :], in0=ysB[:, c, :], scalar=sel_s[:, 1:2], in1=ys[:, c, :], op0=ALU.mult, op1=ALU.add),
                    reads=[('ysB', c % 2), 'sel', ('ys', c)], writes=[('ys', c)])

        def layer_norm(gcol, bcol, final=False):
            for c in range(8):
                P.op('act', lambda e, c=c: e.activation(out=zb[:, c, :], in_=xs[:, c, :], func=AF.Copy),
                     reads=[('xs', c)], writes=[('zb', c)])
            P.op('act', lambda e: e.activation(out=dmy[:, 0:1], in_=sel_s[:, 0:1], func=AF.Sqrt), reads=['sel'], writes=['dmy'])
            pm, pmk = psA.get()
            for c in range(8):
                P.op('pe', lambda e, c=c: e.matmul(pm[:], ones_m[:], zb[:, c, :], start=(c == 0), stop=(c == 7)),
                     reads=['ones_m', ('zb', c)], writes=[pmk])
            for c in range(8):
                P.op('dve', lambda e, c=c: e.tensor_tensor(out=xs[:, c, :], in0=xs[:, c, :], in1=pm[:], op=ALU.subtract),
                     reads=[pmk, ('xs', c)], writes=[('xs', c)])
                P.op('act', lambda e, c=c: e.activation(out=zb[:, c, :], in_=xs[:, c, :], func=AF.Square),
                     reads=[('xs', c)], writes=[('zb', c)])
            pv, pvk = psA.get()
            for c in range(8):
                P.op('pe', lambda e, c=c: e.matmul(pv[:], ones_m[:], zb[:, c, :], start=(c == 0), stop=(c == 7)),
                     reads=['ones_m', ('zb', c)], writes=[pvk])
            P.op('act', lambda e: e.activation(out=rstd[:, 0, :], in_=pv[:], func=AF.Sqrt, bias=LN_EPS),
                 reads=[pvk], writes=[('rstd', 0)])
            P.op('dve', lambda e: e.reciprocal(out=rstd[:, 1, :], in_=rstd[:, 0, :]),
                 reads=[('rstd', 0)], writes=[('rstd', 1)])
            for c in range(8):
                P.op('dve', lambda e, c=c: e.tensor_tensor(out=xs[:, c, :], in0=xs[:, c, :], in1=rstd[:, 1, :], op=ALU.mult),
                     reads=[('rstd', 1), ('xs', c)], writes=[('xs', c)])
                dst = xalias if final else xs
                dk = akeys(c) if final else [('xs', c)]
                P.op('act', lambda e, c=c, dst=dst: e.activation(
                    out=dst[:, c, :], in_=xs[:, c, :], func=AF.Identity,
                    scale=vec_s[:, gcol + c:gcol + c + 1], bias=vec_s[:, bcol + c:bcol + c + 1]),
                    reads=['vec', ('xs', c)], writes=dk)
            for c in range(8):
                dst = xalias if final else xs
                dk = akeys(c) if final else [('xs', c)]
                P.op('dve', lambda e, c=c, dst=dst: e.tensor_copy(out=xb[:, c, :], in_=dst[:, c, :]),
                     reads=dk, writes=[('xb', c)])

        outs = []
        for t in range(ntile):
            ts = slice(t * NT, (t + 1) * NT)
            if t == 0:
                load_x(0)
                load_y(0)
            for mc in range(2):
                ps, pk = psA.get()
                for kc in range(2):
                    P.op('pe', lambda e, ps=ps, kc=kc, mc=mc: e.matmul(
                        ps[:], wglu_s[:, kc, mc * 128:(mc + 1) * 128], ys[:, 6 + kc, :],
                        start=(kc == 0), stop=(kc == 1)), reads=['wglu', ('ys', 6), ('ys', 7)], writes=[pk])
                P.op('act', lambda e, ps=ps, mc=mc: e.activation(
                    out=sg[:, mc, :], in_=ps[:], func=AF.Sigmoid, bias=vec_s[:, mc:mc + 1]),
                    reads=[pk, 'vec'], writes=[('sg', mc)])
                P.op('dve', lambda e, mc=mc: e.tensor_tensor(
                    out=yd[:, mc, :], in0=ys[:, 6 + mc, :], in1=sg[:, mc, :], op=ALU.mult),
                    reads=[('sg', mc), ('ys', 6 + mc)], writes=[('yd', mc)])
            for mc in range(8):
                ps, pk = psA.get()
                for kc in range(8):
                    rhs = ys[:, kc, :] if kc < 6 else yd[:, kc - 6, :]
                    rk = ('ys', kc) if kc < 6 else ('yd', kc - 6)
                    P.op('pe', lambda e, ps=ps, kc=kc, mc=mc, rhs=rhs: e.matmul(
                        ps[:], wo_s[:, kc, mc * 128:(mc + 1) * 128], rhs, start=(kc == 0), stop=(kc == 7)),
                        reads=WO + [rk], writes=[pk])
                resid(ps, pk, mc)
            if t + 1 < ntile:
                load_y(t + 1)
            layer_norm(2, 10)
            for mc in range(8):
                ps, pk = psA.get()
                for kc in range(8):
                    P.op('pe', lambda e, ps=ps, kc=kc, mc=mc: e.matmul(
                        ps[:], wq_s[:, kc, mc * 128:(mc + 1) * 128], xb[:, kc, :], start=(kc == 0), stop=(kc == 7)),
                        reads=WQ + [('xb', kc)], writes=[pk])
                P.op('act', lambda e, ps=ps, mc=mc: e.activation(
                    out=qs[:, mc, :], in_=ps[:], func=AF.Copy, scale=1.0 / 16.0), reads=[pk], writes=[('qs', mc)])
            def att_head(h):
                for mm in range(2):
                    ps, pk = psAtt.get()
                    for dc in range(2):
                        P.op('pe', lambda e, ps=ps, dc=dc, mm=mm: e.matmul(
                            ps[:], kT_s[:, 2 * h + dc, mm * 128:(mm + 1) * 128], qs[:, 2 * h + dc, :],
                            start=(dc == 0), stop=(dc == 1)), reads=['kT', ('qs', 2 * h + dc)], writes=[pk])
                    pslot = (h % 2) * 2 + mm
                    P.op('act', lambda e, ps=ps, pslot=pslot: e.activation(
                        out=pt[:, pslot, :], in_=ps[:], func=AF.Exp), reads=[pk], writes=[('pt', pslot)])
                yield
                pl, plk = psAtt.get()
                for mm in range(2):
                    P.op('pe', lambda e, mm=mm: e.matmul(
                        pl[:], ones_1[:], pt[:, (h % 2) * 2 + mm, :], start=(mm == 0), stop=(mm == 1)),
                        reads=['ones_1', ('pt', (h % 2) * 2 + mm)], writes=[plk])
                P.op('dve', lambda e: e.reciprocal(out=rl[:, h % 2, :], in_=pl[:]),
                     reads=[plk], writes=[('rl', h % 2)])
                for dc in range(2):
                    ps, pk = psAtt.get()
                    for mm in range(2):
                        P.op('pe', lambda e, ps=ps, dc=dc, mm=mm: e.matmul(
                            ps[:], V_s[:, mm, h * 256 + dc * 128:h * 256 + (dc + 1) * 128], pt[:, (h % 2) * 2 + mm, :],
                            start=(mm == 0), stop=(mm == 1)), reads=['V', ('pt', (h % 2) * 2 + mm)], writes=[pk])
                    P.op('dve', lambda e, ps=ps, dc=dc: e.tensor_tensor(
                        out=os_[:, 2 * h + dc, :], in0=ps[:], in1=rl[:, h % 2, :], op=ALU.mult),
                        reads=[pk, ('rl', h % 2)], writes=[('os', 2 * h + dc)])
                yield

            prev = None
            for h in range(4):
                g = att_head(h)
                next(g)
                if prev is not None:
                    next(prev, None)
                prev = g
            next(prev, None)
            for mc in range(8):
                ps, pk = psA.get()
                for kc in range(8):
                    P.op('pe', lambda e, ps=ps, kc=kc, mc=mc: e.matmul(
                        ps[:], woo_s[:, kc, mc * 128:(mc + 1) * 128], os_[:, kc, :], start=(kc == 0), stop=(kc == 7)),
                        reads=WOO + [('os', kc)], writes=[pk])
                resid(ps, pk, mc)
            layer_norm(18, 26)
            for jp in range(11):
                slot = jp % 3
                wv = wst[:, slot, :].rearrange("p (g k n) -> p g k n", g=2, k=8)
                P.dma('pool', wv, w_gu[jp], writes=[('wst', slot)])
                for jj in range(2):
                    j = 2 * jp + jj
                    pg, pgk = psA.get()
                    pu, puk = psA.get()
                    for (pp, ppk, g) in ((pg, pgk, 0), (pu, puk, 1)):
                        for kc in range(8):
                            P.op('pe', lambda e, pp=pp, g=g, kc=kc, jj=jj, wv=wv: e.matmul(
                                pp[:], wv[:, g, kc, jj * 128:(jj + 1) * 128], xb[:, kc, :],
                                start=(kc == 0), stop=(kc == 7)), reads=[('wst', slot), ('xb', kc)], writes=[ppk])
                    P.op('act', lambda e, pg=pg, jj=jj: e.activation(out=sg[:, jj, :], in_=pg[:], func=AF.Silu),
                         reads=[pgk], writes=[('sg', jj)])
                    P.op('dve', lambda e, pu=pu, jj=jj, j=j: e.tensor_tensor(
                        out=hid[:, j, :], in0=pu[:], in1=sg[:, jj, :], op=ALU.mult),
                        reads=[puk, ('sg', jj)], writes=[('hid', j)])
            for sw in range(2):
                for jp in range(11):
                    slot = (sw * 11 + jp) % 3
                    wv = wdst[:, slot, :].rearrange("p (j n) -> p j n", j=2)
                    P.dma('pool', wv, w_d[sw, jp], writes=[('wdst', slot)])
                    for jj in range(2):
                        j = 2 * jp + jj
                        for q in range(4):
                            ps, pk = psD[q]
                            P.op('pe', lambda e, ps=ps, wv=wv, jj=jj, q=q, j=j: e.matmul(
                                ps[:], wv[:, jj, q * 128:(q + 1) * 128], hid[:, j, :],
                                start=(j == 0), stop=(j == 21)), reads=[('wdst', slot), ('hid', j)], writes=[pk])
                for q in range(4):
                    ps, pk = psD[q]
                    resid(ps, pk, sw * 4 + q)
            layer_norm(34, 42, final=True)
            if t + 1 < ntile:
                load_x(t + 1)
            for h in range(2):
                P.dma('sp', xov[:, 4 * h:4 * h + 4, ts], xalias[:, 4 * h:4 * h + 4, :],
                      reads=[k for c in range(4 * h, 4 * h + 4) for k in akeys(c)], writes=['xdst'])
            if not last:
                q = (t * NT) // 1024
                c1 = (t * NT) % 1024
                P.dma('sp', dr['xbf'][q].rearrange("(k p) n -> p k n", p=128)[:, :, c1:c1 + NT], xb[:],
                      reads=[('xb', c) for c in range(8)], writes=[('xbf', q)])
                if c1 + NT == 1024:
                    P.op('pool', lambda e, q=q: e.collective_compute(
                        "AllGather", ALU.bypass, replica_groups=PAIRS, ins=[dr['xbf'][q]], outs=[dr['xall'][q]]),
                        reads=[('xbf', q)], writes=[('xall', q)], coll=True)


def prep_D_weights(inp, l):
    f = np.float32
    kp = lambda w: np.ascontiguousarray(w.reshape(8, 128, -1).transpose(1, 0, 2))
    r = {}
    r['w_out'] = kp(inp['w_mix_out'][l])
    r['w_q'] = kp(inp['w_mem_q'][l])
    r['w_o'] = kp(inp['w_mem_o'][l])
    wkv = kp(inp['w_mem_kv'][l])
    r['w_kv'] = np.ascontiguousarray(wkv.reshape(128, 8, 4, 512).transpose(2, 0, 1, 3))
    r['w_glu'] = np.ascontiguousarray(inp['s5_w_glu'][l].reshape(2, 128, 256).transpose(1, 0, 2))
    g = kp(inp['w_ff_gate'][l]).reshape(128, 8, 11, 256)
    u = kp(inp['w_ff_up'][l]).reshape(128, 8, 11, 256)
    gu = np.stack([g, u], axis=1)
    r['w_gu'] = np.ascontiguousarray(gu.transpose(3, 0, 1, 2, 4))
    wd = inp['w_ff_down'][l].reshape(11, 2, 128, 2, 512)
    r['w_d'] = np.ascontiguousarray(wd.transpose(3, 0, 2, 1, 4))
    cols = [inp['s5_b_glu'][l].reshape(2, 128)]
    for nm in ('ln_mix_g', 'ln_mix_b', 'ln_mem_g', 'ln_mem_b', 'ln_ff_g', 'ln_ff_b'):
        cols.append(inp[nm][l].reshape(8, 128))
    r['vecs'] = np.ascontiguousarray(np.concatenate(cols, axis=0).T.astype(f))
    return r


PAIRS = [[0, 1], [2, 3], [4, 5], [6, 7]]

DBG = ''

SB = 2048
NW = 11 * 128 + 16


PAIRS = [[0, 1], [2, 3], [4, 5], [6, 7]]


def emit_A(nc, P, es, l, dr, T, stages='sABCD'):
    nsb = T // SB
    half = T // 2
    w_in, s5p, s5b, s5c, dvec = dr['w_inA'][l], dr['s5p'][l], dr['s5b'][l], dr['s5c'][l], dr['dvec'][l]
    wgate, bgate, gains = dr['wgate'][l], dr['bgate'][l], dr['gains'][l]
    abias, rtab, gmask, ident, blk, kp1, rmask = (dr[k] for k in ('abias', 'rtab', 'gmask', 'ident', 'blk', 'kp1', 'rmask'))
    xTv = dr['xT_full'].rearrange("(k p) n -> p k n", p=128)
    sfx = "_A%d" % l
    if True:
        sb = lambda name, shape, dt=BF16: es.enter_context(nc.sbuf_tensor(name + sfx, shape, dt))
        w_s = sb("w_s", [128, 8, NW])
        xsb = sb("xsb", [128, 8, SB])
        G = [sb("G%d" % i, [128, SB]) for i in range(6)]
        Y = [sb("Y%d" % i, [128, SB]) for i in range(4)]
        F01 = sb("F01", [128, 2, SB], F32)
        F = [F01[:, 0, :], F01[:, 1, :], sb("F2", [128, SB], F32), sb("F3", [128, SB], F32)]
        kA = sb("kA", [128, 2 * SB])
        vA = sb("vA", [128, 2 * SB])
        abias_s = sb("abias_s", [128, 12, 128])
        rtab_s = sb("rtab_s", [128, 5, 128], F32)
        gmask_s = sb("gmask_s", [128, 2, 128], F32)
        ident_s = sb("ident_s", [128, 128])
        identf_s = sb("identf_s", [128, 128], F32)
        blk_s = sb("blk_s", [128, 128])
        ones_s = sb("ones_s", [128, 64])
        kp1_s = sb("kp1_s", [128, 512], F32)
        rmask_s = sb("rmask_s", [128, SB])
        gains_s = sb("gains_s", [128, 2], F32)
        wgate_s = sb("wgate_s", [16, 64])
        bgate_s = sb("bgate_s", [64, 2], F32)
        dvec_s = sb("dvec_s", [128, 1], F32)
        ptA = sb("ptA", [128, 2, 4, 128])
        vblk = sb("vblk", [128, 2, 2, 128])
        ptB = sb("ptB", [128, 2, 2, 128])
        vtok = sb("vtok", [128, 2, 128])
        ktok = sb("ktok", [128, 2, 128])
        qdec = sb("qdec", [128, 2, 128])
        Rr = sb("Rr", [128, 64], F32)
        Rrb = sb("Rrb", [128, 2, 64])
        Rg = sb("Rg", [64, 64], F32)
        Rgb = sb("Rgb", [64, 2, 128])
        ebl = sb("ebl", [64, SB // 64], F32)
        ob = sb("ob", [128, 512])
        s5p_s = sb("s5p_s", [128, 4, 3], F32)
        s5b_s = sb("s5b_s", [128, 4, 2, 16], F32)
        s5c_s = sb("s5c_s", [128, 4, 2, 16], F32)
        sm = sb("sm", [128, 24, 4], F32)
        bb = sb("bb", [128, 4, 2, 16], F32)
        Zf = sb("Zf", [128, 128], F32)
        ZT = sb("ZT", [128, 4, 2, 128])
        CT = sb("CT", [128, 4, 2, 128])
        cosT = sb("cosT", [128, 4, 512], F32)
        sinT = sb("sinT", [128, 4, 512], F32)
        tb = [sb("tb%d" % i, [128, 512], F32) for i in range(2)]
        tt = [sb("tt%d" % i, [128, 512], F32) for i in range(4)]
        tg = sb("tg", [128, 512], F32)
        xri = sb("xri", [128, 2, 2, 512])
        xend = sb("xend", [128, 4, 2], F32)
        etmp = sb("etmp", [128, 4], F32)
        psb = [es.enter_context(nc.psum_tensor("ps%d" % i + sfx, [128, 512], F32)) for i in range(4)]
        pacc = es.enter_context(nc.psum_tensor("pacc" + sfx, [128, 512], F32))
        pst = [es.enter_context(nc.psum_tensor("pst%d" % i + sfx, [128, 8, 128], BF16)) for i in range(2)]
        psu = es.enter_context(nc.psum_tensor("psu" + sfx, [128, 512], F32))

        taps = []

        def tap(name, ap, keys, dt=BF16):
            if 'tap' not in DBG:
                return
            shp = list(ap.shape)
            t = nc.dram_tensor("tap_" + name, shp, dt, kind="ExternalOutput").ap()
            taps.append(P.dma('sp', t, ap, reads=keys))
        psA = Rot([(psb[i], ("ps", i)) for i in range(4)])
        psT = Rot([(pst[i], ('pst', i)) for i in range(2)])
        psAtt = Rot([(psb[i], ('ps', i)) for i in range(3)] + [(pacc, 'pacc')])
        ablk = [0]
        PI = math.pi

        P.dma('pool', w_s[:, 0:4, :], w_in[:, 0:4, :], writes=[('w', 0)])
        P.dma('pool', w_s[:, 4:8, :], w_in[:, 4:8, :], writes=[('w', 1)])
        WK = [('w', 0), ('w', 1)]
        P.dma('pool', abias_s[:], abias, writes=['abias'])
        P.dma('sp', rtab_s[:], rtab, writes=['rtab'])
        P.dma('sp', gmask_s[:], gmask, writes=['gmask'])
        P.dma('pool', ident_s[:], ident, writes=['ident'])
        P.dma('sp', identf_s[:], ident, writes=['identf'])
        P.dma('pool', blk_s[:], blk, writes=['blk'])
        P.dma('sp', kp1_s[:], kp1, writes=['kp1'])
        P.dma('pool', rmask_s[:], rmask, writes=['rmask'])
        P.dma('sp', gains_s[:], gains, writes=['gains'])
        P.dma('pool', wgate_s[:], wgate, writes=['wgate'])
        P.dma('sp', bgate_s[:, 0:1], bgate, writes=['bgate'])
        P.dma('sp', dvec_s[:], dvec, writes=['dvec'])
        P.dma('sp', s5p_s[:], s5p, writes=['s5p'])
        P.dma('sp', s5b_s[:], s5b, writes=['s5b'])
        P.dma('sp', s5c_s[:], s5c, writes=['s5c'])
        P.op('dve', lambda e: e.memset(ones_s[:], 1.0), writes=['ones'])
        P.op('dve', lambda e: e.tensor_scalar(out=bgate_s[:, 1:2], in0=bgate_s[:, 0:1], scalar1=-1.0, scalar2=None,
                                              op0=ALU.mult), reads=['bgate'], writes=['nbgate'])
        P.op('dve', lambda e: e.memset(Rr[:], 0.0), writes=['Rr'])
        P.op('dve', lambda e: e.memset(Rrb[:], 0.0), writes=[('Rrb', 0), ('Rrb', 1)])
        P.op('dve', lambda e: e.memset(Rg[:], 0.0), writes=['Rg'])
        P.op('dve', lambda e: e.memset(Rgb[:], 0.0), writes=[('Rgb', 0), ('Rgb', 1)])
        P.op('dve', lambda e: e.memset(xend[:], 0.0), writes=['xend'])

        P.enabled = 's' in stages
        SMK = 'sm'
        smc = lambda i: sm[:, i, :]

        def dv(fn, reads=(), writes=()):
            P.op('dve', fn, reads=list(reads) + [SMK], writes=list(writes) + [SMK])

        def av(fn, reads=(), writes=()):
            P.op('act', fn, reads=list(reads) + [SMK], writes=list(writes) + [SMK])

        def tsc(out, in0, s1, op0, s2=None, op1=None):
            if op1 is None:
                return lambda e: e.tensor_scalar(out=out, in0=in0, scalar1=s1, scalar2=None, op0=op0)
            return lambda e: e.tensor_scalar(out=out, in0=in0, scalar1=s1, scalar2=s2, op0=op0, op1=op1)

        def tten(out, a, b, op):
            return lambda e: e.tensor_tensor(out=out, in0=a, in1=b, op=op)

        def reduce_turns(dst, src, tmp, wr=dv):
            wr(lambda e: e.tensor_copy(out=tmp.bitcast(I32), in_=src))
            wr(lambda e: e.tensor_copy(out=dst, in_=tmp.bitcast(I32)))
            wr(tten(dst, src, dst, ALU.subtract))
            wr(tsc(tmp, dst, 0.5, ALU.is_gt))
            wr(tten(dst, dst, tmp, ALU.subtract))
            wr(tsc(tmp, dst, -0.5, ALU.is_lt))
            wr(tten(dst, dst, tmp, ALU.add))

        are, aim, ldt = s5p_s[:, :, 0], s5p_s[:, :, 1], s5p_s[:, :, 2]
        DT, RHO, THT, TF, TMP, TF2, SIN, COS, NR, AI, DEN, FRE, FIM, T1, T2 = [smc(i) for i in range(15)]
        av(lambda e: e.activation(out=DT, in_=ldt, func=AF.Exp), reads=['s5p'])
        dv(tten(T1, are, DT, ALU.mult), reads=['s5p'])
        av(lambda e: e.activation(out=RHO, in_=T1, func=AF.Exp))
        dv(tten(THT, aim, DT, ALU.mult), reads=['s5p'])
        dv(tsc(T2, THT, 1.0 / (2 * PI), ALU.mult))
        reduce_turns(TF, T2, TMP)
        av(lambda e: e.activation(out=SIN, in_=TF, func=AF.Sin, scale=2 * PI))
        dv(tsc(T2, T2, 0.25, ALU.add))
        reduce_turns(TF2, T2, TMP)
        av(lambda e: e.activation(out=COS, in_=TF2, func=AF.Sin, scale=2 * PI))
        dv(tten(NR, RHO, COS, ALU.mult))
        dv(tsc(NR, NR, -1.0, ALU.add))
        dv(tten(AI, RHO, SIN, ALU.mult))
        dv(tten(DEN, are, are, ALU.mult), reads=['s5p'])
        dv(tten(T1, aim, aim, ALU.mult), reads=['s5p'])
        dv(tten(DEN, DEN, T1, ALU.add))
        dv(lambda e: e.reciprocal(out=DEN, in_=DEN))
        dv(tten(FRE, NR, are, ALU.mult), reads=['s5p'])
        dv(tten(T1, AI, aim, ALU.mult), reads=['s5p'])
        dv(tten(FRE, FRE, T1, ALU.add))
        dv(tten(FRE, FRE, DEN, ALU.mult))
        dv(tten(FIM, AI, are, ALU.mult), reads=['s5p'])
        dv(tten(T1, NR, aim, ALU.mult), reads=['s5p'])
        dv(tten(FIM, FIM, T1, ALU.subtract))
        dv(tten(FIM, FIM, DEN, ALU.mult))
        for k in range(4):
            bre, bim = s5b_s[:, k, 0, :], s5b_s[:, k, 1, :]
            fre, fim = sm[:, 11, k:k + 1], sm[:, 12, k:k + 1]
            P.op('dve', tsc(bb[:, k, 0, :], bim, fim, ALU.mult), reads=['s5b', SMK], writes=[('bb', k)])
            P.op('dve', lambda e, k=k, bre=bre, fre=fre: e.scalar_tensor_tensor(
                out=bb[:, k, 0, :], in0=bre, scalar=fre, in1=bb[:, k, 0, :], op0=ALU.mult, op1=ALU.subtract),
                reads=['s5b', SMK, ('bb', k)], writes=[('bb', k)])
            P.op('dve', tsc(bb[:, k, 1, :], bre, fim, ALU.mult), reads=['s5b', SMK, ('bb', k)], writes=[('bb', k)])
            P.op('dve', lambda e, k=k, bim=bim, fre=fre: e.scalar_tensor_tensor(
                out=bb[:, k, 1, :], in0=bim, scalar=fre, in1=bb[:, k, 1, :], op0=ALU.mult, op1=ALU.add),
                reads=['s5b', SMK, ('bb', k)], writes=[('bb', k)])
            for ri in range(2):
                P.op('dve', lambda e: e.memset(Zf[:], 0.0), writes=['Zf'])
                for g2 in range(2):
                    c0 = 16 * (2 * k + g2)
                    P.op('dve', lambda e, k=k, ri=ri, g2=g2, c0=c0: e.tensor_copy(
                        out=Zf[64 * g2:64 * g2 + 64, c0:c0 + 16], in_=bb[64 * g2:64 * g2 + 64, k, ri, :]),
                        reads=[('bb', k), 'Zf'], writes=['Zf'])
                P.op('pe', lambda e: e.transpose(out=psu[:, 0:128], in_=Zf[:], identity=identf_s[:]),
                     reads=['Zf', 'identf'], writes=['psu'])
                P.op('act', lambda e, k=k, ri=ri: e.activation(out=ZT[:, k, ri, :], in_=psu[:, 0:128], func=AF.Copy),
                     reads=['psu'], writes=['ZT'])
                P.op('dve', lambda e, k=k, ri=ri: e.memset(CT[:, k, ri, :], 0.0), reads=['CT'], writes=['CT'])
                for g2 in range(2):
                    c0 = 16 * (2 * k + g2)
                    P.op('dve', tsc(CT[64 * g2:64 * g2 + 64, k, ri, c0:c0 + 16], s5c_s[64 * g2:64 * g2 + 64, k, ri, :],
                                    1.0 if ri == 0 else -1.0, ALU.mult), reads=['s5c', 'CT'], writes=['CT'])
            tf = sm[:, 3, k:k + 1]
            P.op('dve', tsc(tt[0][:], kp1_s[:], tf, ALU.mult), reads=['kp1', SMK], writes=['tt0'])
            wr = lambda fn: P.op('dve', fn, reads=['tt0', 'tt1', 'tt2'], writes=['tt0', 'tt1', 'tt2'])
            reduce_turns(tt[1][:], tt[0][:], tt[2][:], wr=wr)
            P.op('act', lambda e, k=k: e.activation(out=sinT[:, k, :], in_=tt[1][:], func=AF.Sin, scale=2 * PI),
                 reads=['tt1'], writes=['sinT'])
            wr(tsc(tt[0][:], tt[0][:], 0.25, ALU.add))
            reduce_turns(tt[1][:], tt[0][:], tt[2][:], wr=wr)
            P.op('act', lambda e, k=k: e.activation(out=cosT[:, k, :], in_=tt[1][:], func=AF.Sin, scale=2 * PI),
                 reads=['tt1'], writes=['cosT'])

        P.enabled = True
        def project(col0, width, evac):
            for nt in range(SB // 512):
                ps, pk = psA.get()
                for kc in range(8):
                    P.op('pe', lambda e, ps=ps, kc=kc, nt=nt: e.matmul(
                        ps[0:width, :], w_s[:, kc, col0:col0 + width], xsb[:, kc, nt * 512:(nt + 1) * 512],
                        start=(kc == 0), stop=(kc == 7)), reads=WK + ['xsb'], writes=[pk])
                evac(ps, pk, nt)

        def proj_tile(col0, width, evac, nt, rot):
            ps, pk = rot.get()
            for kc in range(8):
                P.op('pe', lambda e, kc=kc: e.matmul(
                    ps[0:width, :], w_s[:, kc, col0:col0 + width], xsb[:, kc, nt * 512:(nt + 1) * 512],
                    start=(kc == 0), stop=(kc == 7)), reads=WK + ['xsb'], writes=[pk])
            evac(ps, pk, nt)

        def evac_copy(dst, key, rows=128, scale=1.0, func=AF.Copy, col0=0):
            def f(ps, pk, nt):
                P.op('act', lambda e: e.activation(
                    out=dst[0:rows, col0 + nt * 512:col0 + (nt + 1) * 512], in_=ps[0:rows, :], func=func, scale=scale),
                    reads=[pk], writes=[key])
            return f

        hn_done = []

        pend = []

        def step_pending():
            for g_ in list(pend):
                if next(g_, 'done') == 'done':
                    pend.remove(g_)

        def flush_pending():
            while pend:
                step_pending()

        def head_norm(po, pok, gcol, gate, gatek, ydst, ykey, c0):
            P.op('act', lambda e: e.activation(out=tt[2][:], in_=po[:], func=AF.Copy), reads=[pok], writes=['tt2'])
            if c0 == 0 and not hn_done:
                hn_done.append(1)
                tap('of', tt[2][:], ['tt2'], F32)
            P.op('dve', lambda e: e.tensor_copy(out=ob[:], in_=po[:]), reads=[pok], writes=['ob'])
            pm, pmk = psA.get()
            P.op('pe', lambda e: e.matmul(pm[:], blk_s[:], ob[:], start=True, stop=True),
                 reads=['blk', 'ob'], writes=[pmk])
            yield
            P.op('dve', tten(tt[2][:], tt[2][:], pm[:], ALU.subtract), reads=[pmk, 'tt2'], writes=['tt2'])
            P.op('act', lambda e: e.activation(out=ob[:], in_=tt[2][:], func=AF.Square), reads=['tt2'], writes=['ob'])
            pv, pvk = psA.get()
            P.op('pe', lambda e: e.matmul(pv[:], blk_s[:], ob[:], start=True, stop=True),
                 reads=['blk', 'ob'], writes=[pvk])
            P.op('act', lambda e: e.activation(out=tt[0][:], in_=pv[:], func=AF.Sqrt, bias=1e-5),
                 reads=[pvk], writes=['tt0'])
            yield
            P.op('dve', lambda e: e.reciprocal(out=tt[1][:], in_=tt[0][:]), reads=['tt0'], writes=['tt1'])
            P.op('dve', tten(tt[2][:], tt[2][:], tt[1][:], ALU.mult), reads=['tt1', 'tt2'], writes=['tt2'])
            P.op('dve', lambda e: e.scalar_tensor_tensor(
                out=ydst[:, c0:c0 + 512], in0=tt[2][:], scalar=gains_s[:, gcol:gcol + 1], in1=gate[:, c0:c0 + 512],
                op0=ALU.mult, op1=ALU.mult), reads=['tt2', 'gains', gatek], writes=[ykey])

        outs = []
        for s in range(nsb):
            t0 = s * SB
            ring = (s % 2) * SB
            pring = ((s - 1) % 2) * SB
            if l == 0:
                P.dma('pool', xsb[:, 0:4, :], xTv[:, 0:4, t0:t0 + SB], writes=['xsb'])
                P.dma('pool', xsb[:, 4:8, :], xTv[:, 4:8, t0:t0 + SB], reads=['xsb'], writes=['xsb'])
            else:
                r_ = t0 // half
                q0 = (t0 - r_ * half) // 1024
                for qq in range(2):
                    src = dr['xall'][q0 + qq][r_ * 1024:(r_ + 1) * 1024, :].rearrange("(k p) n -> p k n", p=128)
                    P.dma('sp', xsb[:, :, qq * 1024:(qq + 1) * 1024], src, reads=[('xall', q0 + qq), 'xsb'], writes=['xsb'])

            for (dst, src) in dr['wconv'][l][s::nsb]:
                P.dma('pool', dst, src, writes=['wconv'])
            P.enabled = 'A' in stages
            qA = G[0]
            project(0, 128, evac_copy(qA, 'G0'))
            project(128, 128, evac_copy(kA, 'kA', scale=0.125, col0=ring))
            project(256, 128, evac_copy(vA, 'vA', col0=ring))
            accO, accL = F[0], F[1]
            P.op('dve', lambda e: e.memset(accO[:], 0.0), writes=['F0'])
            P.op('dve', lambda e: e.memset(accL[:], 0.0), writes=['F1'])
            def blk(bi, r, n, c):
                nloc = 16 // r
                base = n * 128 * r + c
                has_prev = (n > 0) or (s > 0)
                kbs = [0, 1] if has_prev else [1]
                if n > 0:
                    pbase = ring + base - 128 * r
                else:
                    pbase = pring + (nloc - 1) * 128 * r + c
                cbase = ring + base
                kcol = {0: pbase, 1: cbase}
                sl = lambda b0: slice(b0, b0 + 127 * r + 1, r)
                pT, pTk = psT.get()
                vs = (psT.i - 1) % 2
                for kb in kbs:
                    P.op('pe', lambda e, pT=pT, kb=kb, cc=kcol[kb], r=r: e.transpose(
                        out=pT[:, kb, :], in_=vA[:, slice(cc, cc + 127 * r + 1, r)], identity=ident_s[:]),
                        reads=['vA', 'ident'], writes=[pTk])
                P.op('act', lambda e, pT=pT, vs=vs, k0=kbs[0]: e.activation(
                    out=vblk[:, vs, k0:2, :], in_=pT[:, k0:2, :], func=AF.Copy),
                    reads=[pTk], writes=[('vblk', vs)])
                ps, pk = psAtt.get()
                psv = ps[:].rearrange("p (h k q) -> p h k q", h=2, k=2)
                for h in range(2):
                    for kb in kbs:
                        P.op('pe', lambda e, psv=psv, h=h, kb=kb, cc=kcol[kb], r=r, base=base: e.matmul(
                            psv[:, h, kb, :], kA[64 * h:64 * h + 64, slice(cc, cc + 127 * r + 1, r)],
                            qA[64 * h:64 * h + 64, slice(base, base + 127 * r + 1, r)], start=True, stop=False),
                            reads=['kA', 'G0'], writes=[pk])
                        P.op('pe', lambda e, psv=psv, h=h, kb=kb, bi=bi: e.matmul(
                            psv[:, h, kb, :], ident_s[:], abias_s[:, bi * 4 + h * 2 + kb, :], start=False, stop=True),
                            reads=['ident', 'abias'], writes=[pk])
                pa = ablk[0] % 2
                ablk[0] += 1
                ptv = ptA[:, pa, :, :].rearrange("p (h k) q -> p h k q", h=2)
                P.op('act', lambda e, psv=psv, ptv=ptv, k0=kbs[0]: e.activation(
                    out=ptv[:, :, k0:2, :], in_=psv[:, :, k0:2, :], func=AF.Exp),
                    reads=[pk], writes=[('ptA', pa)])
                yield
                po, pok = psAtt.get()
                for h in range(2):
                    for idx, kb in enumerate(kbs):
                        P.op('pe', lambda e, po=po, h=h, kb=kb, vs=vs, ptv=ptv, st=(idx == 0), sp=(kb == 1): e.matmul(
                            po[64 * h:64 * h + 64, 0:128], vblk[:, vs, kb, 64 * h:64 * h + 64], ptv[:, h, kb, :],
                            start=st, stop=sp), reads=[('vblk', vs), ('ptA', pa)], writes=[pok])
                    for idx, kb in enumerate(kbs):
                        P.op('pe', lambda e, po=po, h=h, kb=kb, ptv=ptv, st=(idx == 0), sp=(kb == 1): e.matmul(
                            po[64 * h:64 * h + 64, 128:256], ones_s[:, :], ptv[:, h, kb, :],
                            start=st, stop=sp), reads=['ones', ('ptA', pa)], writes=[pok])
                osl = slice(base, base + 127 * r + 1, r)
                P.op('dve', lambda e, po=po, osl=osl: e.tensor_tensor(
                    out=F01[:, :, osl], in0=F01[:, :, osl], in1=po[:, 0:256].rearrange("p (a q) -> p a q", a=2), op=ALU.add),
                    reads=[pok, 'F0', 'F1'], writes=['F0', 'F1'])

            def attn_steps():
                prev = None
                for bi, r in enumerate((1, 4, 16)):
                    for n in range(16 // r):
                        for c in range(r):
                            g = blk(bi, r, n, c)
                            next(g)
                            if prev is not None:
                                next(prev, None)
                            prev = g
                            yield
                next(prev, None)
                yield

            uD = G[5]
            project(1280, 128, evac_copy(uD, 'G5'))
            PS5, PS5K = psb[3], ('ps', 3)

            def s5_front(u):
                seg, k = divmod(u, 4)
                ss = slice(seg * 512, (seg + 1) * 512)
                for ri in range(2):
                    P.op('pe', lambda e, ri=ri: e.matmul(PS5[:], ZT[:, k, ri, :], uD[:, ss], start=True, stop=True),
                         reads=['ZT', 'G5'], writes=[PS5K])
                    P.op('act', lambda e, ri=ri: e.activation(out=tb[ri][:], in_=PS5[:], func=AF.Copy),
                         reads=[PS5K], writes=['tb%d' % ri])

            def s5_dve(u):
                seg, k = divmod(u, 4)
                cs_, sn_ = cosT[:, k, :], sinT[:, k, :]
                T0, T1_, T2_, T3_ = tt[0][:], tt[1][:], tt[2][:], tt[3][:]
                br, bi_ = tb[0][:], tb[1][:]
                rho_k = sm[:, 1, k:k + 1].to_broadcast([128, 512])
                xs_ = k % 2
                P.op('dve', tten(T0, br, cs_, ALU.mult), reads=['tb0', 'cosT'], writes=['tt0'])
                P.op('dve', tten(T1_, bi_, sn_, ALU.mult), reads=['tb1', 'sinT'], writes=['tt1'])
                P.op('dve', tten(T0, T0, T1_, ALU.add), reads=['tt0', 'tt1'], writes=['tt0'])
                P.op('dve', tten(T1_, bi_, cs_, ALU.mult), reads=['tb1', 'cosT', 'tt0'], writes=['tt1'])
                P.op('dve', tten(T2_, br, sn_, ALU.mult), reads=['tb0', 'sinT'], writes=['tt2'])
                P.op('dve', tten(T1_, T1_, T2_, ALU.subtract), reads=['tt1', 'tt2'], writes=['tt1'])
                P.op('dve', lambda e: e.tensor_tensor_scan(
                    out=tt[2][:], data0=rho_k, data1=tt[0][:], initial=xend[:, k, 0:1],
                    op0=ALU.mult, op1=ALU.add), reads=['tt0', SMK, 'xend'], writes=['tt2'])
                P.op('dve', lambda e: e.tensor_tensor_scan(
                    out=tt[3][:], data0=rho_k, data1=tt[1][:], initial=xend[:, k, 1:2],
                    op0=ALU.mult, op1=ALU.add), reads=['tt1', SMK, 'xend'], writes=['tt3'])
                P.op('dve', tten(T0, T2_, cs_, ALU.mult), reads=['tt2', 'cosT'], writes=['tt0'])
                P.op('dve', tten(T1_, T3_, sn_, ALU.mult), reads=['tt3', 'sinT'], writes=['tt1'])
                P.op('dve', tten(xri[:, xs_, 0, :], T0, T1_, ALU.subtract), reads=['tt0', 'tt1'], writes=[('xri', xs_, 0)])
                P.op('dve', tten(xend[:, k, 0:1], T0[:, 511:512], T1_[:, 511:512], ALU.subtract),
                     reads=['tt0', 'tt1', 'xend'], writes=['xend'])
                P.op('dve', tten(T0, T2_, sn_, ALU.mult), reads=['tt2', 'sinT', ('xri', xs_, 0), 'xend'], writes=['tt0'])
                P.op('dve', tten(T1_, T3_, cs_, ALU.mult), reads=['tt3', 'cosT', ('xri', xs_, 0), 'xend'], writes=['tt1'])
                P.op('dve', tten(xri[:, xs_, 1, :], T0, T1_, ALU.add), reads=['tt0', 'tt1'], writes=[('xri', xs_, 1)])
                P.op('dve', tten(xend[:, k, 1:2], T0[:, 511:512], T1_[:, 511:512], ALU.add),
                     reads=['tt0', 'tt1', 'xend'], writes=['xend'])

            def s5_back(u):
                seg, k = divmod(u, 4)
                ss = slice(seg * 512, (seg + 1) * 512)
                xs_ = k % 2
                for ri in range(2):
                    P.op('pe', lambda e, ri=ri: e.matmul(
                        psu[:], CT[:, k, ri, :], xri[:, xs_, ri, :], start=(k == 0 and ri == 0), stop=(k == 3 and ri == 1)),
                        reads=['CT', ('xri', xs_, ri)], writes=['psu'])
                if k == 3:
                    P.op('dve', lambda e: e.scalar_tensor_tensor(
                        out=tg[:], in0=uD[:, ss], scalar=dvec_s[:, 0:1], in1=psu[:], op0=ALU.mult, op1=ALU.add),
                        reads=['psu', 'G5', 'dvec'], writes=['tg'])
                    P.op('act', lambda e: e.activation(out=Y[3][:, ss], in_=tg[:], func=AF.Gelu),
                         reads=['tg'], writes=['Y3'])

            ga = attn_steps()
            nunit = (SB // 512) * 4
            early = [(512, evac_copy(G[1], 'G1')), (640, evac_copy(G[2], 'G2')), (768, evac_copy(G[3], 'G3', func=AF.Silu))]
            for u in range(nunit):
                s5_front(u)
                next(ga, None)
                if u < 12:
                    proj_tile(early[u // 4][0], 128, early[u // 4][1], u % 4, psAtt)
                s5_dve(u)
                if u > 0:
                    s5_back(u - 1)
                next(ga, None)
                next(ga, None)
            s5_back(nunit - 1)
            for _ in ga:
                pass
            P.op('dve', lambda e: e.reciprocal(out=accL[:], in_=accL[:]), reads=['F1'], writes=['F1'])
            P.op('dve', tten(Y[0][:], accO[:], accL[:], ALU.mult), reads=['F0', 'F1'], writes=['Y0'])

            P.enabled = 'B' in stages
            qB, kB, vB, sgB = G[0], G[1], G[2], G[3]
            project(384, 128, evac_copy(qB, 'G0'))
            def ret_chunk(c):
                cs = slice(c * 128, (c + 1) * 128)
                sl2 = c % 2
                rs = c % 2
                oc = slice((c % 4) * 128, (c % 4 + 1) * 128)
                po, pok = pacc, 'pacc'
                for h in range(2):
                    ps, pk = psA.get()
                    P.op('pe', lambda e, ps=ps, h=h: e.matmul(
                        ps[:, 0:128], kB[64 * h:64 * h + 64, cs], qB[64 * h:64 * h + 64, cs], start=True, stop=True),
                        reads=['G0', 'G1'], writes=[pk])
                    P.op('dve', lambda e, ps=ps, h=h: e.tensor_tensor(
                        out=ptB[:, sl2, h, :], in0=ps[:, 0:128], in1=rtab_s[:, h, :], op=ALU.mult),
                        reads=[pk, 'rtab'], writes=[('ptB', sl2, h)])
                pT, pTk = psT.get()
                P.op('pe', lambda e: e.transpose(out=pT[:, 0, :], in_=vB[:, cs], identity=ident_s[:]),
                     reads=['G2', 'ident'], writes=[pTk])
                P.op('pe', lambda e: e.transpose(out=pT[:, 1, :], in_=kB[:, cs], identity=ident_s[:]),
                     reads=['G1', 'ident'], writes=[pTk])
                P.op('act', lambda e: e.activation(out=vtok[:, sl2, :], in_=pT[:, 0, :], func=AF.Copy),
                     reads=[pTk], writes=[('vtok', sl2)])
                P.op('act', lambda e: e.activation(out=ktok[:, sl2, :], in_=pT[:, 1, :], func=AF.Copy),
                     reads=[pTk], writes=[('ktok', sl2)])
                P.op('dve', lambda e: e.tensor_tensor(
                    out=ktok[:, sl2, :], in0=ktok[:, sl2, :], in1=rtab_s[:, 3, :], op=ALU.mult),
                    reads=[('ktok', sl2), 'rtab'], writes=[('ktok', sl2)])
                P.op('dve', lambda e: e.tensor_tensor(
                    out=qdec[:, sl2, :], in0=qB[:, cs], in1=rtab_s[:, 2, :], op=ALU.mult),
                    reads=['G0', 'rtab'], writes=[('qdec', sl2)])
                yield
                for h in range(2):
                    hs = slice(64 * h, 64 * h + 64)
                    P.op('pe', lambda e, hs=hs: e.matmul(
                        psu[hs, 0:64], ktok[:, sl2, hs], vtok[:, sl2, hs], start=True, stop=True),
                        reads=[('ktok', sl2), ('vtok', sl2)], writes=['psu'])
                P.op('dve', lambda e: e.scalar_tensor_tensor(
                    out=Rr[:], in0=Rr[:], scalar=rtab_s[:, 4, 0:1], in1=psu[:, 0:64], op0=ALU.mult, op1=ALU.add),
                    reads=['psu', 'Rr', 'rtab'], writes=['Rr'])
                P.op('act', lambda e: e.activation(out=Rrb[:, 1 - rs, :], in_=Rr[:], func=AF.Copy),
                     reads=['Rr'], writes=[('Rrb', 1 - rs)])
                for h in range(2):
                    hs = slice(64 * h, 64 * h + 64)
                    P.op('pe', lambda e, hs=hs, h=h: e.matmul(
                        po[hs, oc], vtok[:, sl2, hs], ptB[:, sl2, h, :], start=True, stop=False),
                        reads=[('vtok', sl2), ('ptB', sl2, h)], writes=[pok])
                    P.op('pe', lambda e, hs=hs: e.matmul(
                        po[hs, oc], Rrb[hs, rs, :], qdec[hs, sl2, :], start=False, stop=True),
                        reads=[('Rrb', rs), ('qdec', sl2)], writes=[pok])
                if c % 4 == 3:
                    hn = head_norm(po, pok, 0, sgB, 'G3', Y[1], 'Y1', (c // 4) * 512)
                    next(hn)
                    pend.append(hn)
                yield

            prev = None
            for c in range(SB // 128):
                g = ret_chunk(c)
                next(g)
                step_pending()
                if prev is not None:
                    next(prev, None)
                prev = g
            next(prev, None)
            flush_pending()

            P.enabled = 'C' in stages
            qC, kC, laF, bcF = F[0], F[1], F[2], F[3]
            vC, sgC, glow, qin, kout, kdc = G[0], G[1], G[2], G[3], G[4], G[5]
            project(896, 64, evac_copy(qC, 'F0', rows=64))
            project(960, 64, evac_copy(kC, 'F1', rows=64))
            project(1024, 128, evac_copy(vC, 'G0'))
            project(1152, 128, evac_copy(sgC, 'G1', func=AF.Silu))
            project(1408, 16, evac_copy(glow, 'G2', rows=16))
            for nt in range(SB // 512):
                ns = slice(nt * 512, (nt + 1) * 512)
                ps, pk = psA.get()
                P.op('pe', lambda e, ps=ps, ns=ns: e.matmul(ps[0:64, :], wgate_s[:, :], glow[0:16, ns], start=True, stop=True),
                     reads=['wgate', 'G2'], writes=[pk])
                P.op('act', lambda e, ps=ps, ns=ns: e.activation(
                    out=laF[0:64, ns], in_=ps[0:64, :], func=AF.Exp, scale=-1.0, bias=bgate_s[:, 1:2]),
                    reads=[pk, 'nbgate'], writes=['F2'])
            P.op('act', lambda e: e.activation(out=laF[0:64, :], in_=laF[0:64, :], func=AF.Ln, bias=1.0),
                 reads=['F2'], writes=['F2'])
            P.op('dve', tsc(laF[0:64, :], laF[0:64, :], -1.0 / 16.0, ALU.mult), reads=['F2'], writes=['F2'])
            P.op('dve', lambda e: e.tensor_tensor_scan(
                out=bcF[0:64, :], data0=rmask_s[0:64, :], data1=laF[0:64, :], initial=0.0, op0=ALU.mult, op1=ALU.add),
                reads=['F2', 'rmask'], writes=['F3'])
            P.op('act', lambda e: e.activation(out=laF[0:64, :], in_=bcF[0:64, :], func=AF.Exp), reads=['F3'], writes=['F2'])
            P.op('dve', lambda e: e.scalar_tensor_tensor(
                out=qin[0:64, :], in0=qC[0:64, :], scalar=32.0 ** -0.5, in1=laF[0:64, :], op0=ALU.mult, op1=ALU.mult),
                reads=['F0', 'F2'], writes=['G3'])
            P.op('act', lambda e: e.activation(out=laF[0:64, :], in_=bcF[0:64, :], func=AF.Exp, scale=-1.0),
                 reads=['F3', 'G3'], writes=['F2'])
            P.op('dve', tten(kout[0:64, :], kC[0:64, :], laF[0:64, :], ALU.mult), reads=['F1', 'F2'], writes=['G4'])
            blast = bcF[0:64, :].rearrange("p (c j) -> p c j", j=64)[:, :, 63]
            P.op('act', lambda e: e.activation(out=ebl[:, :], in_=blast, func=AF.Exp), reads=['F3'], writes=['ebl'])
            P.op('dve', lambda e: e.tensor_tensor(
                out=kdc[0:64, :].rearrange("p (c j) -> p c j", j=64),
                in0=kout[0:64, :].rearrange("p (c j) -> p c j", j=64),
                in1=ebl[:, :].unsqueeze(2).to_broadcast([64, SB // 64, 64]), op=ALU.mult),
                reads=['G4', 'ebl'], writes=['G5'])
            def gla_chunk(c):
                cs = slice(c * 128, (c + 1) * 128)
                sl2 = c % 2
                oc0 = (c % 4) * 128
                po, pok = pacc, 'pacc'
                for h in range(2):
                    ps, pk = psA.get()
                    P.op('pe', lambda e, ps=ps, h=h: e.matmul(
                        ps[:, 0:128], kout[32 * h:32 * h + 32, cs], qin[32 * h:32 * h + 32, cs], start=True, stop=True),
                        reads=['G3', 'G4'], writes=[pk])
                    P.op('dve', lambda e, ps=ps, h=h: e.tensor_tensor(
                        out=ptB[:, sl2, h, :], in0=ps[:, 0:128], in1=gmask_s[:, h, :], op=ALU.mult),
                        reads=[pk, 'gmask'], writes=[('ptB', sl2, h)])
                pT, pTk = psT.get()
                P.op('pe', lambda e: e.transpose(out=pT[:, 0, :], in_=vC[:, cs], identity=ident_s[:]),
                     reads=['G0', 'ident'], writes=[pTk])
                P.op('pe', lambda e: e.transpose(out=pT[:, 1, 0:64], in_=kdc[0:64, cs], identity=ident_s[0:64, 0:64]),
                     reads=['G5', 'ident'], writes=[pTk])
                P.op('act', lambda e: e.activation(out=vtok[:, sl2, :], in_=pT[:, 0, :], func=AF.Copy),
                     reads=[pTk], writes=[('vtok', sl2)])
                P.op('dve', lambda e: e.tensor_copy(out=ktok[:, sl2, 0:64], in_=pT[:, 1, 0:64]),
                     reads=[pTk], writes=[('ktok', sl2)])
                yield

                def update(cc):
                    ci = 2 * c + cc
                    ts_ = slice(64 * cc, 64 * cc + 64)
                    for h in range(2):
                        ks = slice(32 * h, 32 * h + 32)
                        hs = slice(64 * h, 64 * h + 64)
                        P.op('pe', lambda e, ks=ks, hs=hs: e.matmul(
                            psu[ks, 64:128], ktok[ts_, sl2, ks], vtok[ts_, sl2, hs], start=True, stop=True),
                            reads=[('ktok', sl2), ('vtok', sl2)], writes=['psu'])
                    P.op('dve', lambda e: e.scalar_tensor_tensor(
                        out=Rg[:], in0=Rg[:], scalar=ebl[:, ci:ci + 1], in1=psu[0:64, 64:128], op0=ALU.mult, op1=ALU.add),
                        reads=['psu', 'Rg', 'ebl'], writes=['Rg'])
                    for h in range(2):
                        P.op('act', lambda e, h=h: e.activation(
                            out=Rgb[32 * h:32 * h + 32, 1 - cc, 64 * h:64 * h + 64], in_=Rg[32 * h:32 * h + 32, :], func=AF.Copy),
                            reads=['Rg'], writes=[('Rgb', 1 - cc)])

                def cross(cc):
                    c64 = slice(c * 128 + cc * 64, c * 128 + cc * 64 + 64)
                    P.op('pe', lambda e: e.matmul(
                        po[:, oc0 + cc * 64:oc0 + cc * 64 + 64], Rgb[:, cc, :], qin[0:64, c64], start=False, stop=(cc == 1)),
                        reads=[('Rgb', cc), 'G3'], writes=[pok])

                update(0)
                for h in range(2):
                    hs = slice(64 * h, 64 * h + 64)
                    P.op('pe', lambda e, hs=hs, h=h: e.matmul(
                        po[hs, oc0:oc0 + 128], vtok[:, sl2, hs], ptB[:, sl2, h, :], start=True, stop=False),
                        reads=[('vtok', sl2), ('ptB', sl2, h)], writes=[pok])
                cross(0)
                update(1)
                cross(1)
                if c % 4 == 3:
                    hn = head_norm(po, pok, 1, sgC, 'G1', Y[2], 'Y2', (c // 4) * 512)
                    next(hn)
                    pend.append(hn)
                yield

            prev = None
            for c in range(SB // 128):
                g = gla_chunk(c)
                next(g)
                step_pending()
                if prev is not None:
                    next(prev, None)
                prev = g
            next(prev, None)
            flush_pending()

            P.enabled = True
            for m in range(4):
                P.dma('sp', dr['ybuf'][l][s][m * 128:(m + 1) * 128, :], Y[m][:], reads=['Y%d' % m], writes=[('ybuf', s)])
            P.op('pool', lambda e, s=s: e.collective_compute(
                "AllGather", ALU.bypass, replica_groups=PAIRS, ins=[dr['ybuf'][l][s]], outs=[dr['yall'][l][s]]),
                reads=[('ybuf', s)], writes=[('yall', s)], coll=True)


def prep_A_consts(p):
    f = np.float32
    q = np.arange(128)[None, :]
    k = np.arange(128)[:, None]
    ab = np.zeros((128, 12, 128), f)
    for bi, r in enumerate((1, 4, 16)):
        for h in range(2):
            slope = 2.0 ** (-2.0 * (2 * p + h + 1))
            prev = np.where(k >= q, -slope * r * (q - k + 128), -1e30)
            cur = np.where(k <= q, -slope * r * (q - k), -1e30)
            ab[:, bi * 4 + h * 2 + 0, :] = prev
            ab[:, bi * 4 + h * 2 + 1, :] = cur
    rt = np.zeros((128, 5, 128), np.float64)
    n = np.arange(128)[None, :]
    m = np.arange(128)[:, None]
    for h in range(2):
        lg = np.log(1.0 - 2.0 ** (-5.0 - (2 * p + h)))
        rt[:, h, :] = np.where(n >= m, np.exp(np.maximum(n - m, 0) * lg), 0.0) * 0.125
        rt[64 * h:64 * h + 64, 2, :] = np.exp((np.arange(128) + 1.0) * lg)[None, :]
        rt[:, 3, 64 * h:64 * h + 64] = (np.exp((127 - np.arange(128)) * lg) * 0.125)[:, None]
        rt[64 * h:64 * h + 64, 4, :] = np.exp(128 * lg)
    gm = np.where((n >= m) & ((n // 64) == (m // 64)), 1.0, 0.0)
    blk = np.zeros((128, 128), f)
    blk[:64, :64] = 1.0 / 64
    blk[64:, 64:] = 1.0 / 64
    rm = np.ones((128, SB), f)
    rm[:, ::64] = 0.0
    return dict(abias=ab, rtab=rt.astype(f), gmask=np.stack([gm, gm], 1).astype(f), ident=np.eye(128, dtype=f),
                blk=blk, kp1=np.broadcast_to(np.arange(1, 513, dtype=f), (128, 512)).copy(), rmask=rm)


def prep_A_weights(inp, l, p):
    f = np.float32
    w = inp['w_in'][l]
    offs = np.cumsum([0, 256, 256, 256, 256, 256, 256, 256, 128, 128, 256, 16, 256, 256])
    seg = lambda i, a, b: w[:, offs[i] + a:offs[i] + b]
    hp = lambda i, wd: seg(i, p * wd, (p + 1) * wd)
    cols = [hp(0, 128), hp(1, 128), hp(2, 128), hp(3, 128), hp(4, 128), hp(5, 128), hp(6, 128),
            hp(7, 64), hp(8, 64), hp(9, 128), hp(11, 128), hp(12, 128), seg(10, 0, 16)]
    wc = np.concatenate(cols, axis=1)
    assert wc.shape[1] == NW
    r = {}
    r['w_in'] = np.ascontiguousarray(wc.reshape(8, 128, NW).transpose(1, 0, 2))
    gs = slice(8 * p, 8 * p + 8)
    tile = lambda a: np.ascontiguousarray(a[gs].reshape(4, 128, *a.shape[2:]))
    are, aim = tile(inp['s5_a_re'][l]), tile(inp['s5_a_im'][l])
    ldt = tile(np.broadcast_to(inp['s5_log_dt'][l][:, None], (16, 64)))
    r['s5p'] = np.ascontiguousarray(np.stack([are, aim, ldt], -1).transpose(1, 0, 2)).astype(f)
    bre, bim = tile(inp['s5_b_re'][l]), tile(inp['s5_b_im'][l])
    r['s5b'] = np.ascontiguousarray(np.stack([bre, bim], 2).transpose(1, 0, 2, 3)).astype(f)
    cre = tile(inp['s5_c_re'][l].transpose(0, 2, 1))
    cim = tile(inp['s5_c_im'][l].transpose(0, 2, 1))
    r['s5c'] = np.ascontiguousarray(np.stack([cre, cim], 2).transpose(1, 0, 2, 3)).astype(f)
    r['dvec'] = np.ascontiguousarray(inp['s5_d'][l][gs].reshape(128, 1)).astype(f)
    r['wgate'] = np.ascontiguousarray(inp['gla_w_gate'][l][:, 64 * p:64 * p + 64])
    r['bgate'] = np.ascontiguousarray(inp['gla_b_gate'][l][64 * p:64 * p + 64].reshape(64, 1))
    r['gains'] = np.ascontiguousarray(np.stack([inp['ret_gn_g'][l][128 * p:128 * p + 128],
                                                inp['gla_gn_g'][l][128 * p:128 * p + 128]], 1)).astype(f)
    return r


def build_fused(S=8192, ncores=8, depth=2):
    nc = bass.Bass("TRN2", target_bir_lowering=False)
    pairs = [[2 * i, 2 * i + 1] for i in range(ncores // 2)]
    global PAIRS
    PAIRS = pairs
    half = S // 2
    nsb = S // SB
    nq = half // 1024
    L = depth
    di = lambda name, shape, dt=F32: nc.dram_tensor(name, shape, dt, kind="ExternalInput").ap()
    dn = lambda name, shape, dt=BF16: nc.dram_tensor(name, shape, dt, kind="Internal").ap()
    dr = {}
    dr['xT_full'] = di("xT_full", [1024, S])
    dr['xT_own'] = di("xT_own", [1024, half])
    dr['memT'] = di("memT", [128, 8, 256])
    dr['sel'] = di("sel", [128, 2])
    for k, shp in (('abias', [128, 12, 128]), ('rtab', [128, 5, 128]), ('gmask', [128, 2, 128]), ('ident', [128, 128]),
                   ('blk', [128, 128]), ('kp1', [128, 512]), ('rmask', [128, SB])):
        dr[k] = di(k, shp)
    for k, shp in (('w_inA', [128, 8, NW]), ('s5p', [128, 4, 3]), ('s5b', [128, 4, 2, 16]), ('s5c', [128, 4, 2, 16]),
                   ('dvec', [128, 1]), ('wgate', [16, 64]), ('bgate', [64, 1]), ('gains', [128, 2]),
                   ('w_out', [128, 8, 1024]), ('w_q', [128, 8, 1024]), ('w_o', [128, 8, 1024]), ('w_kv', [4, 128, 8, 512]),
                   ('w_glu', [128, 2, 256]), ('w_gu', [11, 128, 2, 8, 256]), ('w_d', [2, 11, 128, 2, 512]), ('vecs', [128, 50])):
        dr[k] = di(k, [L] + shp)
    wshapes = dict(w_out=[128, 8, 1024], w_q=[128, 8, 1024], w_o=[128, 8, 1024], w_kv=[4, 128, 8, 512],
                   w_glu=[128, 2, 256], w_gu=[11, 128, 2, 8, 256], w_d=[2, 11, 128, 2, 512])
    dr['wconv'] = [[] for _ in range(L)]
    for k, shp in wshapes.items():
        dr['wb_' + k] = dn("wb_" + k, [L] + shp)
        for l in range(L):
            src, dst = dr[k][l], dr['wb_' + k][l]
            if k in ('w_out', 'w_q', 'w_o'):
                pieces = [(dst[:, 4 * h:4 * h + 4, :], src[:, 4 * h:4 * h + 4, :]) for h in range(2)]
            elif k == 'w_kv':
                pieces = [(dst[i], src[i]) for i in range(4)]
            elif k == 'w_gu':
                pieces = [(dst[i], src[i]) for i in range(11)]
            elif k == 'w_d':
                pieces = [(dst[i], src[i]) for i in range(2)]
            else:
                pieces = [(dst, src)]
            dr['wconv'][l] += pieces
    dr['xo'] = nc.dram_tensor("xo", [1024, half], F32, kind="ExternalOutput").ap()
    dr['ybuf'] = [[dn("ybuf_%d_%d" % (l, s), [512, SB]) for s in range(nsb)] for l in range(L)]
    dr['yall'] = [[dn("yall_%d_%d" % (l, s), [1024, SB]) for s in range(nsb)] for l in range(L)]
    dr['xown'] = dn("xown", [1024, half], F32)
    dr['xbf'] = [dn("xbf_%d" % q, [1024, 1024]) for q in range(nq)]
    dr['xall'] = [dn("xall_%d" % q, [2048, 1024]) for q in range(nq)]
    with ExitStack() as es_outer:
        P = Prog(nc, es_outer)
        for l in range(L):
            with ExitStack() as es:
                emit_A(nc, P, es, l, dr, S)
                P.emit_block()
            with ExitStack() as es:
                emit_D(nc, P, es, l, dr, half, last=(l == L - 1))
                P.emit_block()
    return nc


def make_in_maps(inp, ncores=8):
    x = inp['x']
    S = x.shape[1]
    half = S // 2
    depth = inp['w_in'].shape[0]
    constsA = [prep_A_consts(p) for p in range(2)]
    wA = [[prep_A_weights(inp, l, p) for l in range(depth)] for p in range(2)]
    wD = [prep_D_weights(inp, l) for l in range(depth)]
    wDs = {k: np.stack([wD[l][k] for l in range(depth)]) for k in wD[0]}
    wAs = [{('w_inA' if k == 'w_in' else k): np.stack([wA[p][l][k] for l in range(depth)]) for k in wA[p][0]} for p in range(2)]
    maps = []
    for c in range(ncores):
        b, p = c // 2, c % 2
        xT = np.ascontiguousarray(x[b].T)
        m = {}
        m.update(constsA[p])
        m.update(wAs[p])
        m.update(wDs)
        m['xT_full'] = xT
        m['xT_own'] = np.ascontiguousarray(xT[:, p * half:(p + 1) * half])
        m['memT'] = np.ascontiguousarray(inp['mem'][b].T.reshape(8, 128, -1).transpose(1, 0, 2))
        sel = np.zeros((128, 2), np.float32)
        sel[:, p] = 1.0
        m['sel'] = sel
        maps.append(m)
    return maps


def kernel(**inputs):
    inp = {k: np.asarray(v) for k, v in inputs.items()}
    x = inp['x']
    nb, S, D = x.shape
    ncores = 2 * nb
    half = S // 2
    nc = build_fused(S, ncores, inp['w_in'].shape[0])
    maps = make_in_maps(inp, ncores)
    res = run_bass_kernel_spmd(nc, maps, core_ids=list(range(ncores)))
    out = np.zeros((nb, S, D), np.float32)
    for c in range(ncores):
        out[c // 2, (c % 2) * half:(c % 2 + 1) * half] = np.asarray(res.results[c]['xo']).T
    return out
```
